# Optimizing a Trainium2 kernel written in Bass

```python
import math
import jax, jax.numpy as jnp
from jax import lax
import numpy as np

D_MODEL = 1024
BATCH = 8
SEQ = 2048
DEPTH = 2
DEC_BATCH = 128
DEC_SEQ = 8
PAST_LEN = 16384
PAGE_SIZE = 128

N_MIXERS = 2
N_LAYERS_A = (DEPTH + 1) // 2
N_LAYERS_B = DEPTH // 2
EPS = 1e-6
D_A = 2 * D_MODEL
G_A = 8
DG_A = D_A // G_A
CHUNK_A = 128
H_B = 8
DK_B = D_MODEL // 8
DV_B = 2 * D_MODEL // H_B
D_B = H_B * DV_B
QK_B = 2 * H_B * DK_B
PROJ_B = QK_B + 2 * D_B + 2 * H_B
CONV_B = 4
CHUNK_B = 64
F_BIAS_INIT = 3.0
D_FF = 2816
CONV_F = 3

kernel_name = "hybrid_gmlp_mlstm_convffn_step"


def rmsnorm(x, g):
    x32 = x.astype(jnp.float32)
    y = x32 * lax.rsqrt(jnp.mean(x32 * x32, axis=-1, keepdims=True) + EPS)
    return (y * g.astype(jnp.float32)).astype(x.dtype)


def causal_dwconv(x, buf, w, b):
    k_w = w.shape[0]
    s = x.shape[1]
    xf = jnp.concatenate([buf.astype(x.dtype), x], axis=1)
    y = b + w[k_w - 1] * xf[:, k_w - 1:k_w - 1 + s]
    for j in range(k_w - 1):
        y = y + w[j] * xf[:, j:j + s]
    return y, xf[:, xf.shape[1] - (k_w - 1):]


def chunk_spatial_gating(x, w_in, ln_g, ln_b, w_s, b_s, w_out):
    bsz, s, _ = x.shape
    h = jax.nn.gelu(x @ w_in)
    u, v = jnp.split(h, 2, axis=-1)
    v32 = v.astype(jnp.float32)
    mu = jnp.mean(v32, axis=-1, keepdims=True)
    var = jnp.mean(jnp.square(v32 - mu), axis=-1, keepdims=True)
    v = ((v32 - mu) * lax.rsqrt(var + EPS) * ln_g + ln_b).astype(x.dtype)
    L = min(s, CHUNK_A)
    nc = s // L
    mask = jnp.tril(jnp.ones((L, L), dtype=bool))
    ws = jnp.where(mask, w_s[:, :L, :L], 0.0).astype(x.dtype)
    vc = v.reshape(bsz, nc, L, G_A, DG_A)
    mixed = jnp.einsum('gts,bnsgd->bntgd', ws, vc) + b_s[:, :L].T[:, :, None]
    y = (u * mixed.reshape(bsz, s, D_A)) @ w_out
    return y, v


def mlstm_chunkwise(q, k, v, ig, lf, C0, n0, m0):
    bsz, s, nh, dk = q.shape
    dv = v.shape[-1]
    L = math.gcd(s, CHUNK_B)
    nc = s // L

    def to_chunks(a):
        a = a.reshape((bsz, nc, L) + a.shape[2:])
        return jnp.moveaxis(jnp.moveaxis(a, 1, 0), 2, 3)

    mask = jnp.tril(jnp.ones((L, L), dtype=bool))

    def step(carry, inp):
        C, n, m = carry
        qc, kc, vc, ic, fc = inp
        b = jnp.cumsum(fc, axis=-1)
        D = b[..., :, None] - b[..., None, :] + ic[..., None, :]
        D = jnp.where(mask, D, -jnp.inf)
        inter = b + m[..., None]
        m_t = jnp.maximum(inter, jnp.max(D, axis=-1))
        w_intra = jnp.exp(D - m_t[..., None])
        w_inter = jnp.exp(inter - m_t)
        sc = jnp.einsum('bhtd,bhsd->bhts', qc, kc) * w_intra
        num = w_inter[..., None] * jnp.einsum('bhtd,bhde->bhte', qc, C) \
            + jnp.einsum('bhts,bhse->bhte', sc, vc)
        den = w_inter * jnp.einsum('bhtd,bhd->bht', qc, n) + jnp.sum(sc, axis=-1)
        h = num / jnp.maximum(jnp.abs(den), jnp.exp(-m_t))[..., None]
        bL = b[..., -1]
        m_new = m_t[..., -1]
        g = jnp.exp(bL[..., None] - b + ic - m_new[..., None])
        decay = jnp.exp(bL + m - m_new)
        kg = kc * g[..., None]
        C_new = decay[..., None, None] * C + jnp.einsum('bhsd,bhse->bhde', kg, vc)
        n_new = decay[..., None] * n + jnp.sum(kg, axis=2)
        return (C_new, n_new, m_new), h

    (C, n, m), h = lax.scan(step, (C0, n0, m0),
                            (to_chunks(q), to_chunks(k), to_chunks(v), to_chunks(ig), to_chunks(lf)))
    h = jnp.transpose(h, (1, 0, 3, 2, 4)).reshape(bsz, s, nh, dv)
    return h, C, n, m


def mlstm_mixer(x, conv_buf, C0, n0, m0, w_in, conv_w, conv_b, b_i, b_f, gn_g, w_out):
    bsz, s, _ = x.shape
    p = x @ w_in
    qk_pre = p[..., :QK_B]
    v = p[..., QK_B:QK_B + D_B]
    o_pre = p[..., QK_B + D_B:QK_B + 2 * D_B]
    i_pre = p[..., QK_B + 2 * D_B:QK_B + 2 * D_B + H_B]
    f_pre = p[..., QK_B + 2 * D_B + H_B:]
    qk, new_buf = causal_dwconv(qk_pre, conv_buf, conv_w, conv_b)
    qk = jax.nn.silu(qk).astype(jnp.float32)
    q = qk[..., :QK_B // 2].reshape(bsz, s, H_B, DK_B)
    k = qk[..., QK_B // 2:].reshape(bsz, s, H_B, DK_B) * (DK_B ** -0.5)
    v32 = v.astype(jnp.float32).reshape(bsz, s, H_B, DV_B)
    ig = (i_pre + b_i).astype(jnp.float32)
    lf = jax.nn.log_sigmoid((f_pre + b_f).astype(jnp.float32))
    h, C, n, m = mlstm_chunkwise(q, k, v32, ig, lf, C0.astype(jnp.float32),
                                 n0.astype(jnp.float32), m0.astype(jnp.float32))
    mu = jnp.mean(h, axis=-1, keepdims=True)
    var = jnp.mean(jnp.square(h - mu), axis=-1, keepdims=True)
    hn = ((h - mu) * lax.rsqrt(var + EPS)).reshape(bsz, s, D_B) * gn_g.astype(jnp.float32)
    o = jax.nn.sigmoid(o_pre.astype(jnp.float32))
    y = (o * hn).astype(x.dtype) @ w_out
    return y, new_buf, C.astype(C0.dtype), n.astype(n0.dtype), m.astype(m0.dtype)


def conv_ffn(x, buf, w_up, conv_w, conv_b, w_down):
    up = x @ w_up
    a, g = jnp.split(up, 2, axis=-1)
    a_c, new_buf = causal_dwconv(a, buf, conv_w, conv_b)
    y = (jax.nn.gelu(a_c) * g) @ w_down
    return y, new_buf


def trunk(x, ffn_buf, mC, mn, mm, mconv, norm_mix_g, norm_ffn_g, final_norm_g,
          a_w_in, a_ln_g, a_ln_b, a_w_s, a_b_s, a_w_out,
          b_w_in, b_conv_w, b_conv_b, b_bias_i, b_bias_f, b_gn_g, b_w_out,
          f_w_up, f_conv_w, f_conv_b, f_w_down):
    vs, Cs, ns, ms, mcs, fbs = [], [], [], [], [], []
    for i in range(DEPTH):
        h = rmsnorm(x, norm_mix_g[i])
        j = i // N_MIXERS
        if i % N_MIXERS == 0:
            y, v = chunk_spatial_gating(h, a_w_in[j], a_ln_g[j], a_ln_b[j], a_w_s[j], a_b_s[j], a_w_out[j])
            vs.append(v)
        else:
            y, cb, C, n, m = mlstm_mixer(h, mconv[j], mC[j], mn[j], mm[j], b_w_in[j], b_conv_w[j],
                                         b_conv_b[j], b_bias_i[j], b_bias_f[j], b_gn_g[j], b_w_out[j])
            Cs.append(C); ns.append(n); ms.append(m); mcs.append(cb)
        x = x + y
        h = rmsnorm(x, norm_ffn_g[i])
        y, fb = conv_ffn(h, ffn_buf[i], f_w_up[i], f_conv_w[i], f_conv_b[i], f_w_down[i])
        fbs.append(fb)
        x = x + y
    return (rmsnorm(x, final_norm_g), jnp.stack(vs), jnp.stack(Cs), jnp.stack(ns), jnp.stack(ms),
            jnp.stack(mcs), jnp.stack(fbs))


def setup_inputs(seed: int = 0) -> dict:
    key = jax.random.key(seed)
    ks = jax.random.split(key, 32)

    def nrm(k, shape, scale):
        return jax.random.normal(k, shape, jnp.float32) * scale

    return {
        "x_prompt": nrm(ks[0], (BATCH, SEQ, D_MODEL), 1.0),
        "x_sample": nrm(ks[1], (DEC_BATCH, DEC_SEQ, D_MODEL), 1.0),
        "state_mlstm_C": nrm(ks[2], (N_LAYERS_B, DEC_BATCH, H_B, DK_B, DV_B), 0.1),
        "state_mlstm_n": nrm(ks[3], (N_LAYERS_B, DEC_BATCH, H_B, DK_B), 0.5),
        "state_mlstm_m": nrm(ks[4], (N_LAYERS_B, DEC_BATCH, H_B), 1.0),
        "state_mlstm_conv": nrm(ks[5], (N_LAYERS_B, DEC_BATCH, CONV_B - 1, QK_B), 1.0),
        "state_ffn_conv": nrm(ks[6], (DEPTH, DEC_BATCH, CONV_F - 1, D_FF), 1.0),
        "norm_mix_g": 1.0 + nrm(ks[7], (DEPTH, D_MODEL), 0.02),
        "norm_ffn_g": 1.0 + nrm(ks[8], (DEPTH, D_MODEL), 0.02),
        "final_norm_g": 1.0 + nrm(ks[9], (D_MODEL,), 0.02),
        "a_w_in": nrm(ks[10], (N_LAYERS_A, D_MODEL, 2 * D_A), D_MODEL ** -0.5),
        "a_ln_g": 1.0 + nrm(ks[11], (N_LAYERS_A, D_A), 0.02),
        "a_ln_b": nrm(ks[12], (N_LAYERS_A, D_A), 0.02),
        "a_w_s": nrm(ks[13], (N_LAYERS_A, G_A, CHUNK_A, CHUNK_A), CHUNK_A ** -0.5),
        "a_b_s": 1.0 + nrm(ks[14], (N_LAYERS_A, G_A, CHUNK_A), 0.1),
        "a_w_out": nrm(ks[15], (N_LAYERS_A, D_A, D_MODEL), D_A ** -0.5),
        "b_w_in": nrm(ks[16], (N_LAYERS_B, D_MODEL, PROJ_B), D_MODEL ** -0.5),
        "b_conv_w": nrm(ks[17], (N_LAYERS_B, CONV_B, QK_B), CONV_B ** -0.5),
        "b_conv_b": nrm(ks[18], (N_LAYERS_B, QK_B), 0.02),
        "b_bias_i": nrm(ks[19], (N_LAYERS_B, H_B), 0.1),
        "b_bias_f": F_BIAS_INIT + nrm(ks[20], (N_LAYERS_B, H_B), 0.5),
        "b_gn_g": 1.0 + nrm(ks[21], (N_LAYERS_B, D_B), 0.02),
        "b_w_out": nrm(ks[22], (N_LAYERS_B, D_B, D_MODEL), D_B ** -0.5),
        "f_w_up": nrm(ks[23], (DEPTH, D_MODEL, 2 * D_FF), D_MODEL ** -0.5),
        "f_conv_w": nrm(ks[24], (DEPTH, CONV_F, D_FF), CONV_F ** -0.5),
        "f_conv_b": nrm(ks[25], (DEPTH, D_FF), 0.02),
        "f_w_down": nrm(ks[26], (DEPTH, D_FF, D_MODEL), D_FF ** -0.5),
    }


def reference(x_prompt, x_sample, state_mlstm_C, state_mlstm_n, state_mlstm_m, state_mlstm_conv,
              state_ffn_conv, norm_mix_g, norm_ffn_g, final_norm_g,
              a_w_in, a_ln_g, a_ln_b, a_w_s, a_b_s, a_w_out,
              b_w_in, b_conv_w, b_conv_b, b_bias_i, b_bias_f, b_gn_g, b_w_out,
              f_w_up, f_conv_w, f_conv_b, f_w_down):
    dt = x_prompt.dtype
    bp = x_prompt.shape[0]
    p_C = jnp.zeros((N_LAYERS_B, bp, H_B, DK_B, DV_B), dt)
    p_n = jnp.zeros((N_LAYERS_B, bp, H_B, DK_B), dt)
    p_m = jnp.zeros((N_LAYERS_B, bp, H_B), dt)
    p_mconv = jnp.zeros((N_LAYERS_B, bp, CONV_B - 1, QK_B), dt)
    p_fconv = jnp.zeros((DEPTH, bp, CONV_F - 1, D_FF), dt)
    params = (norm_mix_g, norm_ffn_g, final_norm_g,
              a_w_in, a_ln_g, a_ln_b, a_w_s, a_b_s, a_w_out,
              b_w_in, b_conv_w, b_conv_b, b_bias_i, b_bias_f, b_gn_g, b_w_out,
              f_w_up, f_conv_w, f_conv_b, f_w_down)
    y_prompt, _, C_p, n_p, m_p, mc_p, fc_p = trunk(x_prompt, p_fconv, p_C, p_n, p_m, p_mconv, *params)
    y_sample, v_s, C_s, n_s, m_s, mc_s, fc_s = trunk(x_sample, state_ffn_conv, state_mlstm_C,
                                                     state_mlstm_n, state_mlstm_m, state_mlstm_conv,
                                                     *params)
    return (y_prompt, y_sample, C_p, n_p, m_p, mc_p, fc_p, v_s, C_s, n_s, m_s, mc_s, fc_s)
```

```python
import os
import types
from contextlib import ExitStack
import numpy as np
import ml_dtypes
import concourse.bass as bass
import concourse.mybir as mybir
from concourse.bass_utils import run_bass_kernel_spmd

F32 = mybir.dt.float32
BF16 = mybir.dt.bfloat16
ALU = mybir.AluOpType
AF = mybir.ActivationFunctionType

ENGS = ("tensor", "vector", "scalar", "gpsimd", "sync")
EPS = 1e-6
LNSCALE = -0.5 * float(np.log(128.0))
NFILL = 10
BIG = 30000.0
D_FF = 2816
NJ = 22
TW = 1152


def _freeze(fn):
    if fn.__closure__ is None:
        return fn
    cells = tuple(types.CellType(c.cell_contents) for c in fn.__closure__)
    return types.FunctionType(fn.__code__, fn.__globals__, fn.__name__, fn.__defaults__, cells)


class Buf:
    __slots__ = ("name", "t", "last_write", "reads", "dsem", "dcount", "excl")

    def __init__(self, name, t=None, excl=False):
        self.name = name
        self.t = t
        self.excl = excl
        self.last_write = None
        self.reads = []
        self.dsem = None
        self.dcount = 0

    def __getitem__(self, idx):
        return self.t[idx]


class Prog:
    def __init__(self, nc):
        self.nc = nc
        self.es = ExitStack()
        self.rec = {e: [] for e in ENGS}
        self.count = {e: 0 for e in ENGS}
        self.waited = {e: {} for e in ENGS}
        self.esem = {}
        self.sems = {}
        self.nsem = 0
        for e in ENGS:
            self.esem[e] = self.new_sem("done_" + e)
        self.owner = {v: k for k, v in self.esem.items()}
        self.out_events = []
        self.marks = []

    def new_sem(self, name):
        s = self.es.enter_context(self.nc.semaphore(name))
        self.nsem += 1
        self.sems[name] = s
        return name

    def sbuf(self, name, shape, dtype):
        t = self.es.enter_context(self.nc.sbuf_tensor("sb_" + name, list(shape), dtype))
        return Buf(name, t)

    def psum(self, name, shape, dtype=F32):
        t = self.es.enter_context(self.nc.psum_tensor("ps_" + name, list(shape), dtype))
        return Buf(name, t, excl=True)

    def _waits(self, eng, reads, writes, dsem=None):
        w = {}

        def need(ev, same_ok):
            if ev is None:
                return
            sk, val = ev
            if same_ok and sk == self.esem[eng]:
                return
            if val > w.get(sk, 0):
                w[sk] = val
        for b in reads:
            need(b.last_write, False)
            if b.excl:
                for ev in b.reads:
                    need(ev, True)
        for b in writes:
            lw = b.last_write
            if not (lw is not None and dsem is not None and lw[0] == dsem):
                need(lw, True)
            for ev in b.reads:
                need(ev, True)
        out = []
        for sk, val in w.items():
            if self.waited[eng].get(sk, 0) >= val:
                continue
            if sk in self.owner and val > self.count[self.owner[sk]]:
                raise RuntimeError(f"{eng} waits on unsignaled op of {self.owner[sk]}")
            self.waited[eng][sk] = val
            out.append((sk, val))
        return out

    def mark(self, name):
        self.marks.append((name, sum(1 for r in self.rec["tensor"] if r[1] is not None)))

    def op(self, eng, fn, reads=(), writes=(), signal=True):
        fn = _freeze(fn)
        waits = self._waits(eng, reads, writes)
        if signal:
            self.count[eng] += 1
            ev = (self.esem[eng], self.count[eng])
        else:
            ev = (self.esem[eng], self.count[eng] + 1)
        self.rec[eng].append((waits, fn, (self.esem[eng], 1) if signal else None))
        for b in reads:
            b.reads.append(ev)
        for b in writes:
            b.last_write = ev
            b.reads = []
        return ev

    def dma(self, eng, out_ap, in_ap, reads=(), writes=(), sembuf=None, is_output=False):
        sb = sembuf if sembuf is not None else (writes[0] if writes else reads[0])
        if sb.dsem is None:
            sb.dsem = self.new_sem("dma_" + sb.name)
        waits = self._waits(eng, reads, writes, dsem=sb.dsem)
        sb.dcount += 16
        ev = (sb.dsem, sb.dcount)

        def fn(e, out_ap=out_ap, in_ap=in_ap):
            return e.dma_start(out=out_ap, in_=in_ap)
        self.rec[eng].append((waits, fn, (sb.dsem, 16)))
        for b in reads:
            b.reads.append(ev)
        for b in writes:
            b.last_write = ev
            b.reads = []
        if is_output:
            self.out_events.append(ev)
        return ev

    def finish(self, eng="sync"):
        w = {}
        for sk, val in self.out_events:
            w[sk] = max(w.get(sk, 0), val)
        self.rec[eng].append(([(sk, v) for sk, v in w.items()], None, None))

    def emit(self):
        with self.nc.Block() as block:
            def make(engname):
                def body(e):
                    with self.nc.allow_non_contiguous_dma(reason="tiny state rows"):
                        for waits, fn, inc in self.rec[engname]:
                            for sk, val in waits:
                                e.wait_ge(self.sems[sk], val)
                            if fn is not None:
                                ins = fn(e)
                                if inc is not None:
                                    ins.then_inc(self.sems[inc[0]], inc[1])
                return body
            block.tensor(make("tensor"))
            block.vector(make("vector"))
            block.scalar(make("scalar"))
            block.gpsimd(make("gpsimd"))
            block.sync(make("sync"))

    def close(self):
        self.es.close()


class Pool:
    def __init__(self, bufs):
        self.bufs = bufs
        self.i = 0

    def get(self):
        b = self.bufs[self.i % len(self.bufs)]
        self.i += 1
        return b


def build_program(stop_after=99):
    nc = bass.Bass("TRN2", target_bir_lowering=False)
    P = Prog(nc)

    def din(name, shape, dt=F32):
        return nc.dram_tensor(name, list(shape), dt, kind="ExternalInput").ap()

    def dout(name, shape, dt=F32):
        return nc.dram_tensor(name, list(shape), dt, kind="ExternalOutput").ap()

    d_xTp = din("xTp", [128, 8, 2048]); d_xTs = din("xTs", [128, 8, 128])
    d_wA_in = din("wA_in", [8, 128, 8, 512]); d_wA_out = din("wA_out", [4, 128, 16, 256])
    d_wF_up = din("wF_up", [2, 11, 128, 8, 512]); d_wF_dn = din("wF_dn", [2, 8, 128, 22, 128])
    d_wB_in = din("wB_in", [12, 128, 8, 512]); d_wB_g = din("wB_g", [128, 8, 16])
    d_wB_out = din("wB_out", [4, 128, 16, 256])
    d_gcol = din("gcol", [128, 5, 8]); d_fcw = din("fcw", [128, 2, 3, 22]); d_fcb = din("fcb", [128, 2, 22])
    d_bcw = din("bcw", [128, 4, 16]); d_bcb = din("bcb", [128, 16]); d_bif = din("bif", [8, 2])
    d_gncol = din("gncol", [128, 16])
    d_lng = din("lng", [1, 2048]); d_lnb = din("lnb", [1, 2048])
    d_ws = din("ws", [128, 8, 128]); d_wsbd = din("wsbd", [128, 8, 128])
    d_bs8 = din("bs8", [8, 128]); d_bs8s = din("bs8s", [8, 128])
    d_Cst = din("Cst", [16, 8, 128, 256]); d_nst = din("nst", [128, 128]); d_m0T = din("m0T", [8, 16])
    d_cvs = din("cvs", [128, 16, 16, 3]); d_ffs = din("ffs", [128, 2, 22, 16, 2])
    d_identb = din("identb", [128, 128], BF16); d_identf = din("identf", [128, 128])
    d_onesb = din("onesb", [128, 128], BF16)
    d_trip = din("trip", [128, 128], BF16); d_bdp = din("bdp", [128, 128], BF16)
    d_tri01 = din("tri01", [128, 128]); d_bmask = din("bmask", [128, 16], BF16)
    d_sel = din("sel", [8, 8, 128]); d_rst = din("rst", [8, 2, 128])
    o_yTp = dout("yTp", [128, 8, 2048]); o_yTs = dout("yTs", [128, 8, 128])
    o_pC = dout("pC", [8, 128, 256]); o_pn = dout("pn", [128, 8]); o_pm = dout("pm", [8, 1])
    o_pconv = dout("pconv", [128, 16, 3]); o_pffn = dout("pffn", [128, 2, 22, 2])
    o_sv = dout("sv", [128, 2048])
    o_sC = dout("sC", [16, 8, 128, 256]); o_sn = dout("sn", [128, 128]); o_sm = dout("sm", [8, 16])
    o_sconv = dout("sconv", [128, 16, 16, 3]); o_sffn = dout("sffn", [128, 2, 22, 16, 2])

    xT = P.sbuf("xT", [128, 8, TW], F32)
    hT = P.sbuf("hT", [128, 8, TW], BF16)
    bigt = P.sbuf("big", [128, NJ, TW], BF16)
    xg = [Buf(f"xg{i}") for i in range(3)]
    hg = [Buf(f"hg{i}") for i in range(3)]
    slotL = [[Buf(f"slot{i}_{t}") for t in range(9)] for i in range(NJ)]

    def SLC(c, lo, n):
        return slotL[c][lo // 128:(lo + n + 127) // 128]

    def SLW(c):
        return list(slotL[c])
    NRING = 4
    ring = [P.sbuf(f"ring{i}", [128, 4096], BF16) for i in range(NRING)]
    ring_i = [0]
    U16 = [P.sbuf(f"U16_{i}", [128, 2056], F32) for i in range(2)]
    wg = P.sbuf("wg", [128, 8, 16], BF16)
    gcol = P.sbuf("gcol", [128, 5, 8], F32); fcw = P.sbuf("fcw", [128, 2, 3, 22], F32)
    fcb = P.sbuf("fcb", [128, 2, 22], F32); bcw = P.sbuf("bcw", [128, 4, 16], F32)
    bcb = P.sbuf("bcb", [128, 16], F32); bif = P.sbuf("bif", [8, 2], F32); gncol = P.sbuf("gncol", [128, 16], F32)
    wsT = P.sbuf("wsT", [128, 8, 128], BF16); wsTs = P.sbuf("wsTs", [128, 8, 128], BF16)
    bsh = P.sbuf("bsh", [40, 2, 128], BF16)
    selb = P.sbuf("selb", [40, 8, 128], BF16)
    identb = P.sbuf("identb", [128, 128], BF16); identf = P.sbuf("identf", [128, 128], F32)
    onesb = P.sbuf("onesb", [128, 128], BF16); trip = P.sbuf("trip", [128, 128], BF16)
    bdp = P.sbuf("bdp", [128, 128], BF16)
    bmask = P.sbuf("bmask", [128, 16], BF16); sel = P.sbuf("sel", [8, 8, 128], F32)
    rst = P.sbuf("rst", [8, 2, 128], F32)
    Mrows = P.sbuf("Mrows", [8, 1 + TW], F32)
    bcar = P.sbuf("bcar", [8, 2], F32)
    acols = P.sbuf("acols", [128, 9, 16], F32)
    gcols = P.sbuf("gcols", [128, 8], F32)
    C32 = P.sbuf("C32", [128, 8, 257], F32)
    C32h = [Buf(f"C32h{h}") for h in range(8)]
    Mpcol = P.sbuf("Mpcol", [128, 8], F32)
    atail = P.sbuf("atail", [128, 2, NJ, 2], F32)
    qtail = P.sbuf("qtail", [128, 16, 3], F32)
    nsT = P.sbuf("nsT", [128, 128], F32)
    decb = P.sbuf("decb", [128, 8, 16], F32)
    zerob = P.sbuf("zerob", [128, 128], BF16)
    m0row = P.sbuf("m0row", [8, 16, 8], F32)
    onesr = P.sbuf("onesr", [8, 1], F32); zerosr = P.sbuf("zerosr", [8, 1], F32)
    S32 = Pool([P.sbuf(f"S32_{i}", [128, 520], F32) for i in range(3)])
    OG = Pool([P.sbuf(f"OG_{i}", [128, 512], F32) for i in range(2)])
    WT = Pool([P.sbuf(f"WT_{i}", [128, 256], F32) for i in range(3)])
    SA = Pool([P.sbuf(f"SA_{i}", [128, 128], BF16) for i in range(7)])
    CB = Pool([P.sbuf(f"CB_{i}", [128, 258], BF16) for i in range(3)])
    GT = Pool([P.sbuf(f"GT_{i}", [128, 256], BF16) for i in range(3)])
    RS = Pool([P.sbuf(f"RS_{i}", [128, 512], F32) for i in range(1)])
    VA = Pool([P.sbuf(f"VA_{i}", [128, 514], BF16) for i in range(2)])
    S16 = Pool([P.sbuf(f"S16_{i}", [128, 512], BF16) for i in range(2)])
    SC = Pool([P.sbuf(f"SC_{i}", [128, 16], F32) for i in range(8)])
    HS = Pool([P.sbuf(f"HS_{i}", [128, 48], F32) for i in range(3)])
    PA = Pool([P.psum(f"pa{i}", [128, 512], F32) for i in range(6)])
    PB = Pool([P.psum(f"pb{i}", [128, 1024], BF16) for i in range(2)])
    print("sbuf remaining", nc.sbuf_bytes_remaining, "sems", P.nsem)

    def V(e):
        return "vector"

    onetime = []
    for sb, d in ((gcol, d_gcol), (fcw, d_fcw), (fcb, d_fcb), (bcw, d_bcw), (bcb, d_bcb), (bif, d_bif),
                  (gncol, d_gncol), (identb, d_identb), (identf, d_identf),
                  (onesb, d_onesb), (trip, d_trip), (bdp, d_bdp), (bmask, d_bmask),
                  (sel, d_sel), (rst, d_rst)):
        P.dma("sync", sb[:], d, writes=[sb], sembuf=gcol)
        onetime.append(sb)
    for sb in onetime:
        sb.last_write = (gcol.dsem, gcol.dcount)
    P.dma("gpsimd", wg[:], d_wB_g, writes=[wg])
    P.op("vector", lambda e: e.memset(onesr[:], 1.0), writes=[onesr])
    P.op("vector", lambda e: e.memset(zerosr[:], 0.0), writes=[zerosr])
    P.op("vector", lambda e: e.memset(C32[:], 0.0), writes=C32h)
    P.op("vector", lambda e: e.memset(Mrows[:, 0:1], 0.0), writes=[Mrows])
    P.op("vector", lambda e: e.memset(bcar[:], 0.0), writes=[bcar])
    P.op("vector", lambda e: e.memset(atail[:], 0.0), writes=[atail])
    P.op("vector", lambda e: e.memset(qtail[:], 0.0), writes=[qtail])
    P.op("vector", lambda e: e.memset(Mpcol[:], 0.0), writes=[Mpcol])
    P.op("vector", lambda e: e.memset(zerob[:], 0.0), writes=[zerob])

    P.op("vector", lambda e: e.memset(selb[:], 0.0), writes=[selb])
    P.op("vector", lambda e: e.memset(bsh[:], 0.0), writes=[bsh])
    P.dma("gpsimd", selb[0:8, :, :], d_sel, writes=[selb])
    P.dma("gpsimd", selb[32:40, :, :], d_sel, writes=[selb])
    tri01 = S32.get()
    P.dma("sync", tri01[:, 0:128], d_tri01, writes=[tri01])
    for i_, dsrc_ in enumerate((d_bs8, d_bs8s)):
        src_ = S32.get()
        sap = src_[0:8, 0:128]
        P.dma("sync", sap, dsrc_, writes=[src_])
        P.op("vector", lambda e: e.tensor_copy(bsh[0:8, i_, :], sap), reads=[src_], writes=[bsh])
        P.op("vector", lambda e: e.tensor_tensor(sap, sap, bsh[0:8, i_, :], ALU.subtract), reads=[src_, bsh], writes=[src_])
        P.dma("gpsimd", bsh[32:40, i_, :], sap, reads=[src_], writes=[bsh])
    for (dsrc, dst) in ((d_ws, wsT), (d_wsbd, wsTs)):
        for half in range(2):
            t4 = U16[half]
            P.dma("sync", t4[:, 0:512].rearrange("p (g s) -> p g s", g=4), dsrc[:, half * 4:(half + 1) * 4, :], writes=[t4])
            P.op("vector", lambda e, t4=t4: e.tensor_tensor(t4[:, 0:512].rearrange("p (g s) -> p g s", g=4),
                                                             t4[:, 0:512].rearrange("p (g s) -> p g s", g=4),
                                                             tri01[:, 0:128].unsqueeze(1).to_broadcast([128, 4, 128]), ALU.mult),
                 reads=[t4, tri01], writes=[t4])
            pb = PA.get()
            for g in range(4):
                P.op("tensor", lambda e, g=g, t4=t4, pb=pb: e.transpose(pb[:, g * 128:(g + 1) * 128], t4[:, g * 128:(g + 1) * 128], identf[:]),
                     reads=[t4, identf], writes=[pb], signal=(g == 3))
            P.op("vector", lambda e, pb=pb, dst=dst, half=half: e.tensor_copy(dst[:, half * 4:(half + 1) * 4, :],
                                                                               pb[:, 0:512].rearrange("p (g s) -> p g s", g=4)),
                 reads=[pb], writes=[dst])

    def ring_load(parts):
        s = ring[ring_i[0] % NRING]
        ring_i[0] += 1
        for (lo, n, src) in parts:
            P.dma("gpsimd", s[:, lo:lo + n], src, writes=[s])
        return s

    def groups_of(ntiles, has_sample):
        gs = []
        npt = ntiles - (1 if has_sample else 0)
        t = 0
        gi = 0
        while t < npt:
            n = min(4, npt - t)
            gs.append(dict(lo=t * 128, n=n * 128, gi=gi, sample=False, tiles=list(range(t, t + n))))
            t += n
            gi += 1
        if has_sample:
            gs.append(dict(lo=npt * 128, n=128, gi=gi, sample=True, tiles=[npt]))
        return gs

    def rsqrt_act(out_ap, in_ap, scale, bias, rbufs, wbufs):
        P.op("scalar", lambda e: e.activation(out_ap, in_ap, AF.Ln, bias=bias, scale=scale), reads=rbufs, writes=wbufs)
        P.op("scalar", lambda e: e.activation(out_ap, out_ap, AF.Exp, scale=-0.5), reads=wbufs, writes=wbufs)

    def rmsnorm_to(gi_idx, groups, dst_fn):
        for g in groups:
            lo, n = g["lo"], g["n"]
            pss = PA.get()
            for c in range(8):
                sq = S16.get()
                P.op("scalar", lambda e, sq=sq, c=c: e.activation(sq[:, 0:n], xT[:, c, lo:lo + n], AF.Square),
                     reads=[xg[g["gi"]]], writes=[sq])
                P.op("tensor", lambda e, sq=sq, c=c: e.matmul(pss[:, 0:n], lhsT=onesb[:], rhs=sq[:, 0:n], start=(c == 0), stop=(c == 7)),
                     reads=[sq, onesb], writes=[pss], signal=True)
            rstd = RS.get()
            rsqrt_act(rstd[:, 0:n], pss[:, 0:n], 1.0 / 1024.0, EPS, [pss], [rstd])
            for c in range(8):
                dst_fn(g, c, xT[:, c, lo:lo + n], gcol[:, gi_idx, c:c + 1], rstd[:, 0:n], rstd)

    def norm_to_hT(gi_idx, groups):
        def dst(g, c, xin, gc, rs, rstd):
            lo, n = g["lo"], g["n"]
            P.op("vector", lambda e: e.scalar_tensor_tensor(hT[:, c, lo:lo + n], xin, gc, rs, ALU.mult, ALU.mult),
                 reads=[xg[g["gi"]], rstd, gcol], writes=[hg[g["gi"]]])
        rmsnorm_to(gi_idx, groups, dst)

    def resid_add(g, m, ps):
        lo, n = g["lo"], g["n"]
        P.op("vector", lambda e: e.tensor_tensor(xT[:, m, lo:lo + n], xT[:, m, lo:lo + n], ps[:, 0:n], ALU.add),
             reads=[ps, xg[g["gi"]]], writes=[xg[g["gi"]]])

    def out_proj(d_w, groups, kslots):
        for mm in range(4):
            s = ring_load([(0, 4096, d_w[mm].rearrange("p a b -> p (a b)"))])
            sv = s[:, 0:4096].rearrange("p (a b) -> p a b", a=16)
            for g in groups:
                lo, n = g["lo"], g["n"]
                for m2 in range(2):
                    ps = PA.get()
                    for ei in range(16):
                        sl = kslots[ei]
                        P.op("tensor", lambda e, ei=ei, sl=sl, ps=ps, m2=m2: e.matmul(
                            ps[:, 0:n], lhsT=sv[:, ei, m2 * 128:(m2 + 1) * 128], rhs=bigt[:, sl, lo:lo + n],
                            start=(ei == 0), stop=(ei == 15)),
                            reads=[s, *SLC(sl, lo, n)], writes=[ps], signal=(ei == 15))
                    resid_add(g, mm * 2 + m2, ps)

    def mixer_A(groups, tiles, has_sample):
        P.mark('A.norm')
        norm_to_hT(0, groups)
        P.dma("sync", U16[0][:, 0:2048], d_lng.partition_broadcast(128), writes=[U16[0]])
        P.dma("sync", U16[1][:, 0:2048], d_lnb.partition_broadcast(128), writes=[U16[1]])
        P.mark('A.u')
        for q in range(4):
            s = ring_load([(0, 4096, d_wA_in[q].rearrange("p a b -> p (a b)"))])
            sv = s[:, 0:4096].rearrange("p (a b) -> p a b", a=8)
            for g in groups:
                lo, n = g["lo"], g["n"]
                for cc in range(4):
                    c = q * 4 + cc
                    ps = PA.get()
                    for k in range(8):
                        P.op("tensor", lambda e, k=k, cc=cc, ps=ps: e.matmul(ps[:, 0:n], lhsT=sv[:, k, cc * 128:(cc + 1) * 128],
                                                                               rhs=hT[:, k, lo:lo + n], start=(k == 0), stop=(k == 7)),
                             reads=[s, hg[g["gi"]]], writes=[ps], signal=(k == 7))
                    P.op("scalar", lambda e, c=c, ps=ps: e.activation(bigt[:, c, lo:lo + n], ps[:, 0:n], AF.Gelu_apprx_tanh),
                         reads=[ps], writes=SLC(c, lo, n))
        P.mark('A.v')
        vp = []
        for q in range(4):
            s = ring_load([(0, 4096, d_wA_in[4 + q].rearrange("p a b -> p (a b)"))])
            vp.append(s)
        v32 = bigt[:, 16:20, :].rearrange("p a b -> p (a b)").bitcast(F32)[:, 0:2048]
        v32b = SLW(16) + SLW(17) + SLW(18) + SLW(19)
        vln = bigt[:, 20:22, :].rearrange("p a b -> p (a b)")[:, 0:2048]
        vlnb = SLW(20) + SLW(21)
        PV = Pool(PA.bufs[0:4])
        PM = Pool(PA.bufs[4:6])

        def a_part1(ti):
            lo = ti * 128
            gi = lo // 512
            pss_ = []
            for q in range(4):
                s = vp[q]
                sv = s[:, 0:4096].rearrange("p (a b) -> p a b", a=8)
                ps = PV.get()
                for k in range(8):
                    P.op("tensor", lambda e: e.matmul(ps[:, 0:512], lhsT=hT[:, k, lo:lo + 128], rhs=sv[:, k, :],
                                                      start=(k == 0), stop=(k == 7)),
                         reads=[s, hg[gi]], writes=[ps], signal=(k == 7))
                pss_.append(ps)
            return pss_

        def a_part2(ti, pss_):
            is_s = has_sample and ti == len(tiles) - 1
            for q in range(4):
                ps = pss_[q]
                P.op("scalar", lambda e: e.activation(v32[:, q * 512:(q + 1) * 512], ps[:, 0:512], AF.Gelu_apprx_tanh),
                     reads=[ps], writes=v32b)
            st = S32.get()
            for q in range(4):
                P.op("vector", lambda e: e.bn_stats(st[:, q * 6:(q + 1) * 6], v32[:, q * 512:(q + 1) * 512]),
                     reads=v32b, writes=[st])
            mv = SC.get()
            P.op("vector", lambda e: e.bn_aggr(mv[:, 0:2], st[:, 0:24].rearrange("p (a b) -> p a b", a=4)),
                 reads=[st], writes=[mv])
            rsqrt_act(mv[:, 2:3], mv[:, 1:2], 1.0, EPS, [mv], [mv])
            P.op("vector", lambda e: e.tensor_scalar(v32, v32, mv[:, 0:1], mv[:, 2:3], ALU.subtract, ALU.mult),
                 reads=v32b + [mv], writes=v32b)
            P.op("vector", lambda e: e.tensor_tensor(v32, v32, U16[0][:, 0:2048], ALU.mult), reads=v32b + [U16[0]], writes=v32b)
            if is_s:
                P.op("vector", lambda e: e.tensor_tensor(v32, v32, U16[1][:, 0:2048], ALU.add), reads=v32b + [U16[1]], writes=v32b)
                P.dma("sync", o_sv, v32, reads=v32b, sembuf=slotL[16][0], is_output=True)
                P.op("scalar", lambda e: e.activation(vln, v32, AF.Copy), reads=v32b, writes=vlnb)
            else:
                P.op("vector", lambda e: e.tensor_tensor(vln, v32, U16[1][:, 0:2048], ALU.add), reads=v32b + [U16[1]], writes=vlnb)

        def a_part3(ti):
            lo = ti * 128
            is_s = has_sample and ti == len(tiles) - 1
            wT_ = wsTs if is_s else wsT
            bsi = 1 if is_s else 0
            for cb in range(4):
                ps = PM.get()
                for cc in range(4):
                    c = cb * 4 + cc
                    gidx = c // 2
                    P.op("tensor", lambda e: e.matmul(ps[:, cc * 128:(cc + 1) * 128], lhsT=vln[:, c * 128:(c + 1) * 128],
                                                      rhs=wT_[:, gidx, :], start=True, stop=False),
                         reads=vlnb + [wT_], writes=[ps], signal=False)
                    P.op("tensor", lambda e: e.matmul(ps[:, cc * 128:(cc + 1) * 128], lhsT=selb[:, gidx, :],
                                                      rhs=bsh[:, bsi, :], start=False, stop=True),
                         reads=[selb, bsh], writes=[ps], signal=(cc == 3))
                P.op("vector", lambda e: e.tensor_tensor(bigt[:, cb * 4:cb * 4 + 4, lo:lo + 128],
                                                         ps[:, 0:512].rearrange("p (a b) -> p a b", a=4),
                                                         bigt[:, cb * 4:cb * 4 + 4, lo:lo + 128], ALU.mult),
                     reads=[ps] + [b_ for i in range(4) for b_ in SLC(cb * 4 + i, lo, 128)], writes=[b_ for i in range(4) for b_ in SLC(cb * 4 + i, lo, 128)])

        nt_ = len(tiles)
        pend = a_part1(0)
        a_part2(0, pend)
        for ti in range(nt_):
            nxt = a_part1(ti + 1) if ti + 1 < nt_ else None
            a_part3(ti)
            if nxt is not None:
                a_part2(ti + 1, nxt)
        P.mark('A.out')
        out_proj(d_wA_out, groups, list(range(16)))

    def ffn(l, groups, last_pass):
        P.mark('F.norm')
        norm_to_hT(2 + l, groups)
        P.mark('F.up')
        pend_f = []
        pend_a = []
        for jj in range(11):
            s = ring_load([(0, 4096, d_wF_up[l, jj].rearrange("p a b -> p (a b)"))])
            sv = s[:, 0:4096].rearrange("p (a b) -> p a b", a=8)
            for j2 in range(2):
                j = jj * 2 + j2
                w0 = fcw[:, l, 0, j:j + 1]; w1 = fcw[:, l, 1, j:j + 1]; w2 = fcw[:, l, 2, j:j + 1]
                cb_ = fcb[:, l, j:j + 1]
                for g in groups:
                    lo, n = g["lo"], g["n"]
                    psa = PA.get()
                    for k in range(8):
                        P.op("tensor", lambda e, k=k, psa=psa: e.matmul(psa[:, 0:n], lhsT=sv[:, k, j2 * 128:(j2 + 1) * 128],
                                                                        rhs=hT[:, k, lo:lo + n], start=(k == 0), stop=(k == 7)),
                             reads=[s, hg[g["gi"]]], writes=[psa], signal=(k == 7))
                    psg = PA.get()
                    for k in range(8):
                        P.op("tensor", lambda e, k=k, psg=psg: e.matmul(psg[:, 0:n], lhsT=sv[:, k, 256 + j2 * 128:256 + (j2 + 1) * 128],
                                                                        rhs=hT[:, k, lo:lo + n], start=(k == 0), stop=(k == 7)),
                             reads=[s, hg[g["gi"]]], writes=[psg], signal=(k == 7))
                    t0 = S32.get()
                    P.op("scalar", lambda e: e.activation(t0[:, 0:n], psa[:, 0:n], AF.Identity, bias=cb_, scale=w2),
                         reads=[psa, fcw, fcb], writes=[t0])
                    if pend_a:
                        pend_a.pop()()
                    if not g["sample"]:
                        P.op("vector", lambda e: e.scalar_tensor_tensor(t0[:, 1:n], psa[:, 0:n - 1], w1, t0[:, 1:n], ALU.mult, ALU.add),
                             reads=[psa, t0, fcw], writes=[t0])
                        P.op("vector", lambda e: e.scalar_tensor_tensor(t0[:, 2:n], psa[:, 0:n - 2], w0, t0[:, 2:n], ALU.mult, ALU.add),
                             reads=[psa, t0, fcw], writes=[t0])
                        P.op("vector", lambda e: e.scalar_tensor_tensor(t0[:, 0:2], atail[:, l, j, 0:2], w0, t0[:, 0:2], ALU.mult, ALU.add),
                             reads=[atail, t0, fcw], writes=[t0])
                        P.op("vector", lambda e: e.scalar_tensor_tensor(t0[:, 0:1], atail[:, l, j, 1:2], w1, t0[:, 0:1], ALU.mult, ALU.add),
                             reads=[atail, t0, fcw], writes=[t0])

                        def act2(t0=t0, psa=psa, j=j, n=n):
                            P.op("scalar", lambda e: e.activation(atail[:, l, j, :], psa[:, n - 2:n], AF.Copy), reads=[psa], writes=[atail])
                            P.op("scalar", lambda e: e.activation(t0[:, 0:n], t0[:, 0:n], AF.Gelu_apprx_tanh), reads=[t0], writes=[t0])
                    else:
                        hs = HS.get(); os_ = HS.get()
                        hs3 = hs[:, 0:32].rearrange("p (b r) -> p b r", b=16)
                        os3 = os_[:, 0:32].rearrange("p (b r) -> p b r", b=16)
                        psa3 = psa[:, 0:128].rearrange("p (b r) -> p b r", b=16)
                        t03 = t0[:, 0:128].rearrange("p (b r) -> p b r", b=16)
                        P.dma("sync", hs3, d_ffs[:, l, j, :, :], writes=[hs])
                        P.op("vector", lambda e: e.scalar_tensor_tensor(t03[:, :, 1:8], psa3[:, :, 0:7], w1, t03[:, :, 1:8], ALU.mult, ALU.add),
                             reads=[psa, t0, fcw], writes=[t0])
                        P.op("vector", lambda e: e.scalar_tensor_tensor(t03[:, :, 2:8], psa3[:, :, 0:6], w0, t03[:, :, 2:8], ALU.mult, ALU.add),
                             reads=[psa, t0, fcw], writes=[t0])
                        P.op("vector", lambda e: e.scalar_tensor_tensor(t03[:, :, 0:2], hs3[:, :, 0:2], w0, t03[:, :, 0:2], ALU.mult, ALU.add),
                             reads=[hs, t0, fcw], writes=[t0])
                        P.op("vector", lambda e: e.scalar_tensor_tensor(t03[:, :, 0:1], hs3[:, :, 1:2], w1, t03[:, :, 0:1], ALU.mult, ALU.add),
                             reads=[hs, t0, fcw], writes=[t0])

                        def act2(t0=t0, psa3=psa3, psa=psa, os3=os3, os_=os_, j=j, n=n):
                            P.op("scalar", lambda e: e.activation(os3, psa3[:, :, 6:8], AF.Copy), reads=[psa], writes=[os_])
                            P.dma("sync", o_sffn[:, l, j, :, :], os3, reads=[os_], sembuf=os_, is_output=True)
                            P.op("scalar", lambda e: e.activation(t0[:, 0:n], t0[:, 0:n], AF.Gelu_apprx_tanh), reads=[t0], writes=[t0])
                    pend_a.append(act2)
                    if pend_f:
                        pend_f.pop()()

                    def fin(t0=t0, psg=psg, j=j, lo=lo, n=n):
                        P.op("vector", lambda e: e.tensor_tensor(bigt[:, j, lo:lo + n], t0[:, 0:n], psg[:, 0:n], ALU.mult),
                             reads=[t0, psg], writes=SLC(j, lo, n))
                    pend_f.append(fin)
        if pend_a:
            pend_a.pop()()
        if pend_f:
            pend_f.pop()()
        if last_pass:
            P.dma("sync", o_pffn[:, l, :, :], atail[:, l, :, :], reads=[atail], sembuf=atail, is_output=True)
        P.mark('F.down')
        for m in range(8):
            s = ring_load([(0, NJ * 128, d_wF_dn[l, m].rearrange("p a b -> p (a b)"))])
            sv = s[:, 0:NJ * 128].rearrange("p (a b) -> p a b", a=NJ)
            for g in groups:
                lo, n = g["lo"], g["n"]
                ps = PA.get()
                for j in range(NJ):
                    P.op("tensor", lambda e, j=j, ps=ps: e.matmul(ps[:, 0:n], lhsT=sv[:, j, :], rhs=bigt[:, j, lo:lo + n],
                                                                 start=(j == 0), stop=(j == NJ - 1)),
                         reads=[s, *SLC(j, lo, n)], writes=[ps], signal=(j == NJ - 1))
                resid_add(g, m, ps)

    def mixer_B(groups, tiles, has_sample, first_pass, last_pass):
        P.mark('B.norm')
        norm_to_hT(1, groups)
        P.mark('B.qk')
        npt = len(tiles) - (1 if has_sample else 0)
        def gates_gen():
            RW = 520
            def rowbuf(i):
                return bigt[0:8, 16 + i, :].bitcast(F32)[:, 0:RW]
            for g in groups:
                lo, n, gi = g["lo"], g["n"], g["gi"]
                igr, zr, lfr, bgr, t1r, t2r = [rowbuf(i) for i in range(6)]
                rb = [SLW(16 + i) for i in range(6)]
                psi = PA.get()
                for k in range(8):
                    P.op("tensor", lambda e, k=k, psi=psi: e.matmul(psi[0:8, 0:n], lhsT=wg[:, k, 0:8], rhs=hT[:, k, lo:lo + n], start=(k == 0), stop=(k == 7)),
                         reads=[wg, hg[gi]], writes=[psi], signal=(k == 7))
                P.op("scalar", lambda e, psi=psi: e.activation(igr[:, 0:n], psi[0:8, 0:n], AF.Identity, bias=bif[:, 0:1], scale=1.0),
                     reads=[psi, bif], writes=[*rb[0]])
                yield
                psf = PA.get()
                for k in range(8):
                    P.op("tensor", lambda e, k=k, psf=psf: e.matmul(psf[0:8, 0:n], lhsT=wg[:, k, 8:16], rhs=hT[:, k, lo:lo + n], start=(k == 0), stop=(k == 7)),
                         reads=[wg, hg[gi]], writes=[psf], signal=(k == 7))
                P.op("scalar", lambda e, psf=psf: e.activation(zr[:, 0:n], psf[0:8, 0:n], AF.Identity, bias=bif[:, 1:2], scale=1.0),
                     reads=[psf, bif], writes=[*rb[1]])
                yield
                P.op("vector", lambda e: e.scalar_tensor_tensor(t1r[:, 0:n], zr[:, 0:n], -1.0, zr[:, 0:n], ALU.mult, ALU.max), reads=[*rb[1]], writes=[*rb[4]])
                yield
                P.op("scalar", lambda e: e.activation(t1r[:, 0:n], t1r[:, 0:n], AF.Exp, scale=-1.0), reads=[*rb[4]], writes=[*rb[4]])
                yield
                P.op("scalar", lambda e: e.activation(t1r[:, 0:n], t1r[:, 0:n], AF.Ln, bias=1.0, scale=1.0), reads=[*rb[4]], writes=[*rb[4]])
                yield
                P.op("vector", lambda e: e.tensor_scalar(lfr[:, 0:n], zr[:, 0:n], 0.0, None, ALU.min), reads=[*rb[1]], writes=[*rb[2]])
                yield
                P.op("vector", lambda e: e.tensor_tensor(lfr[:, 0:n], lfr[:, 0:n], t1r[:, 0:n], ALU.subtract), reads=[*rb[2], *rb[4]], writes=[*rb[2]])
                yield
                mcol0 = 1 + lo
                if not g["sample"]:
                    P.op("vector", lambda e: e.tensor_tensor_scan(bgr[:, 0:n], onesr[:, 0:1].to_broadcast([8, n]), lfr[:, 0:n], bcar[:, 0:1], ALU.mult, ALU.add),
                         reads=[onesr, *rb[2], bcar], writes=[*rb[3]])
                    yield
                    P.op("vector", lambda e: e.tensor_copy(bcar[:, 0:1], bgr[:, n - 1:n]), reads=[*rb[3]], writes=[bcar])
                    yield
                    P.op("vector", lambda e: e.tensor_tensor(igr[:, 0:n], igr[:, 0:n], bgr[:, 0:n], ALU.subtract), reads=[*rb[0], *rb[3]], writes=[*rb[0]])
                    yield
                    P.op("vector", lambda e: e.tensor_tensor_scan(Mrows[:, mcol0:mcol0 + n], zerosr[:, 0:1].to_broadcast([8, n]), igr[:, 0:n], Mrows[:, mcol0 - 1:mcol0], ALU.add, ALU.max),
                         reads=[zerosr, *rb[0], Mrows], writes=[Mrows])
                    yield
                else:
                    P.dma("sync", m0row[:, :, 0:1], d_m0T.unsqueeze(2), writes=[m0row])
                    P.op("vector", lambda e: e.tensor_copy(m0row[:, :, 1:8], m0row[:, :, 0:1].to_broadcast([8, 16, 7])), reads=[m0row], writes=[m0row])
                    yield
                    P.op("vector", lambda e: e.tensor_tensor_scan(bgr[:, 0:n], rst[:, 0, :], lfr[:, 0:n], 0.0, ALU.mult, ALU.add),
                         reads=[rst, *rb[2]], writes=[*rb[3]])
                    yield
                    P.op("vector", lambda e: e.tensor_tensor(igr[:, 0:n], igr[:, 0:n], bgr[:, 0:n], ALU.subtract), reads=[*rb[0], *rb[3]], writes=[*rb[0]])
                    yield
                    P.op("vector", lambda e: e.scalar_tensor_tensor(t2r[:, 0:n], rst[:, 0, :], -1e30, m0row[:].rearrange("p b r -> p (b r)"), ALU.mult, ALU.add),
                         reads=[rst, m0row], writes=[*rb[5]])
                    yield
                    P.op("vector", lambda e: e.tensor_tensor(t2r[:, 0:n], t2r[:, 0:n], igr[:, 0:n], ALU.max), reads=[*rb[5], *rb[0]], writes=[*rb[5]])
                    yield
                    P.op("vector", lambda e: e.tensor_tensor_scan(Mrows[:, mcol0:mcol0 + n], rst[:, 1, :], t2r[:, 0:n], -1e30, ALU.add, ALU.max),
                         reads=[rst, *rb[5]], writes=[Mrows])
                    yield
                P.op("vector", lambda e: e.tensor_tensor(t1r[:, 0:n], bgr[:, 0:n], Mrows[:, mcol0:mcol0 + n], ALU.add), reads=[*rb[3], Mrows], writes=[*rb[4]])
                yield
                if not g["sample"]:
                    P.op("vector", lambda e: e.tensor_copy(bcar[:, 1:2], t1r[:, n - 1:n]), reads=[*rb[4]], writes=[bcar])
                    yield
                else:
                    P.dma("sync", o_sm.unsqueeze(2), t1r[:, 0:128].rearrange("p (b r) -> p b r", b=16)[:, :, 7:8], reads=[*rb[4]], sembuf=m0row, is_output=True)
                P.op("vector", lambda e: e.tensor_scalar(t1r[:, 0:n], t1r[:, 0:n], -2.0, 80.0, ALU.mult, ALU.min), reads=[*rb[4]], writes=[*rb[4]])
                yield
                P.op("scalar", lambda e: e.activation(t1r[:, 0:n], t1r[:, 0:n], AF.Exp), reads=[*rb[4]], writes=[*rb[4]])
                yield
                if g["sample"]:
                    Mend = Mrows[:, mcol0:mcol0 + 128].rearrange("p (b r) -> p b r", b=16)[:, :, 7:8]
                    P.op("vector", lambda e: e.tensor_tensor(t2r[:, 0:128].rearrange("p (b r) -> p b r", b=16), igr[:, 0:128].rearrange("p (b r) -> p b r", b=16),
                                                              Mend.to_broadcast([8, 16, 8]), ALU.subtract), reads=[*rb[0], Mrows], writes=[*rb[5]])
                    yield
                    P.op("scalar", lambda e: e.activation(t2r[:, 0:128], t2r[:, 0:128], AF.Exp, bias=LNSCALE_AP[0:8, :], scale=1.0), reads=[*rb[5], lnsc], writes=[*rb[5]])
                    yield
                    P.op("vector", lambda e: e.tensor_tensor(zr[:, 0:16].unsqueeze(2), m0row[:, :, 0:1], Mend, ALU.subtract), reads=[m0row, Mrows], writes=[*rb[1]])
                    yield
                    P.op("scalar", lambda e: e.activation(zr[:, 0:16], zr[:, 0:16], AF.Exp), reads=[*rb[1]], writes=[*rb[1]])
                    yield
                    P.op("vector", lambda e: e.tensor_tensor(lfr[:, 0:128], m0row[:].rearrange("p b r -> p (b r)"), Mrows[:, mcol0:mcol0 + 128], ALU.subtract),
                         reads=[m0row, Mrows], writes=[*rb[2]])
                    yield
                for til in g["tiles"]:
                    o = (til * 128) - lo
                    pc = PA.get()
                    P.op("tensor", lambda e, o=o, pc=pc: e.transpose(pc[:, 0:8], igr[:, o:o + 128], identf[0:8, 0:8]), reads=[*rb[0], identf], writes=[pc], signal=False)
                    P.op("tensor", lambda e, o=o, pc=pc: e.transpose(pc[:, 8:16], t1r[:, o:o + 128], identf[0:8, 0:8]), reads=[*rb[4], identf], writes=[pc], signal=not g["sample"])
                    if g["sample"]:
                        P.op("tensor", lambda e, pc=pc: e.transpose(pc[:, 16:24], t2r[:, 0:128], identf[0:8, 0:8]), reads=[*rb[5], identf], writes=[pc], signal=True)
                        P.op("vector", lambda e, pc=pc: e.tensor_copy(gcols[:], pc[:, 16:24]), reads=[pc], writes=[gcols])
                        pd = PA.get()
                        for h in range(8):
                            P.op("tensor", lambda e, h=h, pd=pd: e.matmul(pd[:, h * 16:(h + 1) * 16], lhsT=sel[:, h, :], rhs=zr[:, 0:16], start=True, stop=True),
                                 reads=[sel, *rb[1]], writes=[pd], signal=(h == 7))
                        P.op("vector", lambda e, pd=pd: e.tensor_copy(decb[:].rearrange("p h b -> p (h b)"), pd[:, 0:128]), reads=[pd], writes=[decb])
                    P.op("vector", lambda e, til=til, pc=pc: e.tensor_scalar(acols[:, til, 0:8], pc[:, 0:8], LNSCALE, None, ALU.add), reads=[pc], writes=[acols])
                    P.op("vector", lambda e, til=til, pc=pc: e.tensor_copy(acols[:, til, 8:16], pc[:, 8:16]), reads=[pc], writes=[acols])
                    yield
                if g["sample"]:
                    P.op("vector", lambda e: e.tensor_copy(dMs[:, 0:128], lfr[:, 0:128]), reads=[*rb[2]], writes=[dMsb])

            yield
        gates = gates_gen()
        pend_q = []
        for q in range(4):
            s = ring_load([(0, 4096, d_wB_in[q].rearrange("p a b -> p (a b)"))])
            sv = s[:, 0:4096].rearrange("p (a b) -> p a b", a=8)
            for cc in range(4):
                c = q * 4 + cc
                wc = [bcw[:, i, c:c + 1] for i in range(4)]
                cbb = bcb[:, c:c + 1]
                while pend_q:
                    pend_q.pop(0)()
                dg = []
                for i in range(4):
                    d_ = SA.get()
                    P.op("vector", lambda e: e.tensor_scalar(d_[:, 0:128], identb[:], wc[i], None, ALU.mult), reads=[identb, bcw], writes=[d_])
                    dg.append(d_)
                for g in groups:
                    lo, n = g["lo"], g["n"]
                    ps = PA.get()
                    for k in range(8):
                        P.op("tensor", lambda e, k=k, ps=ps: e.matmul(ps[:, 0:n], lhsT=sv[:, k, cc * 128:(cc + 1) * 128],
                                                                      rhs=hT[:, k, lo:lo + n], start=(k == 0), stop=(k == 7)),
                             reads=[s, hg[g["gi"]]], writes=[ps], signal=(k == 7))
                    if not g["sample"]:
                        xb = S16.get()
                        P.op("scalar", lambda e: e.activation(xb[:, 0:n], ps[:, 0:n], AF.Copy), reads=[ps], writes=[xb])
                        t3 = SC.get()
                        P.op("scalar", lambda e: e.activation(t3[:, 0:3], ps[:, 0:3], AF.Identity, bias=cbb, scale=wc[3]),
                             reads=[ps, bcw, bcb], writes=[t3])
                        P.op("vector", lambda e: e.scalar_tensor_tensor(t3[:, 1:3], ps[:, 0:2], wc[2], t3[:, 1:3], ALU.mult, ALU.add),
                             reads=[ps, t3, bcw], writes=[t3])
                        P.op("vector", lambda e: e.scalar_tensor_tensor(t3[:, 2:3], ps[:, 0:1], wc[1], t3[:, 2:3], ALU.mult, ALU.add),
                             reads=[ps, t3, bcw], writes=[t3])
                        for i in range(3):
                            P.op("vector", lambda e: e.scalar_tensor_tensor(t3[:, 0:3 - i], qtail[:, c, i:3], wc[i], t3[:, 0:3 - i], ALU.mult, ALU.add),
                                 reads=[qtail, t3, bcw], writes=[t3])
                        P.op("scalar", lambda e: e.activation(qtail[:, c, :], ps[:, n - 3:n], AF.Copy), reads=[ps], writes=[qtail])
                        P.op("scalar", lambda e: e.activation(bigt[:, c, lo:lo + 3], t3[:, 0:3], AF.Silu), reads=[t3], writes=SLC(c, lo, 3))

                        def finq(xb=xb, c=c, lo=lo, n=n, dg=dg, cbb=cbb):
                            pc = PA.get()
                            for i in range(4):
                                P.op("tensor", lambda e: e.matmul(pc[:, 3:n], lhsT=dg[i][:, 0:128], rhs=xb[:, i:i + n - 3], start=(i == 0), stop=(i == 3)),
                                     reads=[dg[i], xb], writes=[pc], signal=(i == 3))
                            P.op("scalar", lambda e: e.activation(bigt[:, c, lo + 3:lo + n], pc[:, 3:n], AF.Silu, bias=cbb, scale=1.0),
                                 reads=[pc, bcb], writes=SLC(c, lo, n))
                    else:
                        t0 = S32.get()
                        P.op("scalar", lambda e: e.activation(t0[:, 0:n], ps[:, 0:n], AF.Identity, bias=cbb, scale=wc[3]),
                             reads=[ps, bcw, bcb], writes=[t0])
                        hs = HS.get(); os_ = HS.get()
                        hs3 = hs[:, 0:48].rearrange("p (b r) -> p b r", b=16)
                        os3 = os_[:, 0:48].rearrange("p (b r) -> p b r", b=16)
                        ps3 = ps[:, 0:128].rearrange("p (b r) -> p b r", b=16)
                        t03 = t0[:, 0:128].rearrange("p (b r) -> p b r", b=16)
                        P.dma("sync", hs3, d_cvs[:, c, :, :], writes=[hs])
                        for i in range(3):
                            sh = 3 - i
                            P.op("vector", lambda e: e.scalar_tensor_tensor(t03[:, :, sh:8], ps3[:, :, 0:8 - sh], wc[i], t03[:, :, sh:8], ALU.mult, ALU.add),
                                 reads=[ps, t0, bcw], writes=[t0])
                        for i in range(3):
                            P.op("vector", lambda e: e.scalar_tensor_tensor(t03[:, :, 0:3 - i], hs3[:, :, i:3], wc[i], t03[:, :, 0:3 - i], ALU.mult, ALU.add),
                                 reads=[hs, t0, bcw], writes=[t0])
                        P.op("scalar", lambda e: e.activation(os3, ps3[:, :, 5:8], AF.Copy), reads=[ps], writes=[os_])
                        P.dma("sync", o_sconv[:, c, :, :], os3, reads=[os_], sembuf=os_, is_output=True)

                        def finq(t0=t0, c=c, lo=lo, n=n):
                            P.op("scalar", lambda e: e.activation(bigt[:, c, lo:lo + n], t0[:, 0:n], AF.Silu), reads=[t0], writes=SLC(c, lo, n))
                    pend_q.append(finq)
                    if len(pend_q) > 1:
                        pend_q.pop(0)()
                    for _ in range(3):
                        next(gates, None)
        while pend_q:
            pend_q.pop(0)()
        if last_pass:
            P.dma("sync", o_pconv, qtail[:], reads=[qtail], sembuf=qtail, is_output=True)
        P.mark('B.gates')
        for _ in gates:
            pass
        if last_pass:
            P.dma("sync", o_pm, bcar[:, 1:2], reads=[bcar], sembuf=bcar, is_output=True)
        if has_sample:
            nin = S32.get()
            P.dma("sync", nin[:, 0:128], d_nst, writes=[nin])
            pn_ = PA.get()
            P.op("tensor", lambda e: e.transpose(pn_[:, 0:128], nin[:, 0:128], identf[:]), reads=[nin, identf], writes=[pn_])
            P.op("vector", lambda e: e.tensor_copy(nsT[:], pn_[:, 0:128]), reads=[pn_], writes=[nsT])
        P.mark('B.rec')
        Cs16 = bigt[:, 16:18, :].rearrange("p a b -> p (a b)")[:, 0:2056].rearrange("p (b e) -> p b e", b=8)
        Cs16b = SLW(16) + SLW(17)
        Vbd = bigt[:, 18:20, :].rearrange("p a b -> p (a b)")[:, 0:2056].rearrange("p (b e) -> p b e", b=8)
        Vbdb = SLW(18) + SLW(19)
        Bigq = bigt[:, 20:22, :].rearrange("p a b -> p (a b)")[:, 0:1984].rearrange("p (b e) -> p b e", b=8)
        Bigb = SLW(20) + SLW(21)
        if has_sample:
            P.op("vector", lambda e: e.memset(bigt[:, 20:22, :], 0.0), writes=Bigb)
        NT = len(tiles)
        PN = Pool(PA.bufs[4:6])
        PP = Pool(PA.bufs[0:2])
        PS = Pool(PA.bufs[2:4])

        def proj(pr, ti, sv_, so_):
            lo = ti * 128
            gi = lo // 512
            svv = sv_[:, 0:4096].rearrange("p (a b) -> p a b", a=8)
            sov = so_[:, 0:4096].rearrange("p (a b) -> p a b", a=8)
            psv = PP.get()
            for k in range(8):
                P.op("tensor", lambda e: e.matmul(psv[:, 0:512], lhsT=hT[:, k, lo:lo + 128], rhs=svv[:, k, :], start=(k == 0), stop=(k == 7)),
                     reads=[sv_, hg[gi]], writes=[psv], signal=(k == 7))
            va = VA.get()
            va3 = va[:, 0:514].rearrange("p (h e) -> p h e", h=2)
            P.op("scalar", lambda e: e.activation(va3[:, :, 0:256], psv[:, 0:512].rearrange("p (h e) -> p h e", h=2), AF.Copy), reads=[psv], writes=[va])
            P.op("vector", lambda e: e.memset(va3[:, :, 256:257], 1.0), writes=[va])
            pso = PP.get()
            for k in range(8):
                P.op("tensor", lambda e: e.matmul(pso[:, 0:512], lhsT=hT[:, k, lo:lo + 128], rhs=sov[:, k, :], start=(k == 0), stop=(k == 7)),
                     reads=[so_, hg[gi]], writes=[pso], signal=(k == 7))
            og = OG.get()
            def sig():
                P.op("scalar", lambda e: e.activation(og[:, 0:512], pso[:, 0:512], AF.Exp, scale=-1.0), reads=[pso], writes=[og])
                P.op("scalar", lambda e: e.activation(og[:, 0:512], og[:, 0:512], AF.Ln, bias=1.0, scale=1.0), reads=[og], writes=[og])
                P.op("scalar", lambda e: e.activation(og[:, 0:512], og[:, 0:512], AF.Exp, scale=-1.0), reads=[og], writes=[og])
            return va, va3, og, sig

        def stageA(cx):
            ti, h, hh, is_s = cx["ti"], cx["h"], cx["hh"], cx["is_s"]
            lo = ti * 128
            qs, ks = h, 8 + h
            pst = PS.get()
            pmb = pst
            pbk = PB.bufs[cx["idx"] % 2]
            cx["pbk"] = pbk
            P.op("tensor", lambda e: e.matmul(pst[:, 0:128], lhsT=bigt[:, ks, lo:lo + 128], rhs=bigt[:, qs, lo:lo + 128], start=True, stop=True),
                 reads=[*SLC(ks, lo, 128), *SLC(qs, lo, 128)], writes=[pst])
            mc = 1 + lo
            P.op("tensor", lambda e: e.matmul(pmb[:, 128:256], lhsT=sel[:, h, :], rhs=Mrows[:, mc:mc + 128], start=True, stop=False),
                 reads=[sel, Mrows], writes=[pmb], signal=False)
            P.op("tensor", lambda e: e.matmul(pmb[:, 128:256], lhsT=identb[:], rhs=(bdp if is_s else trip)[:], start=False, stop=True),
                 reads=[identb, bdp, trip], writes=[pmb], signal=False)
            if not is_s:
                P.op("tensor", lambda e: e.matmul(pmb[:, 256:384], lhsT=sel[:, h, :], rhs=Mrows[:, mc:mc + 128], start=True, stop=True),
                     reads=[sel, Mrows], writes=[pmb], signal=True)
            else:
                P.op("tensor", lambda e: e.matmul(pmb[:, 256:384], lhsT=sel[:, h, :], rhs=dMs[:, 0:128], start=True, stop=True),
                     reads=[sel, dMsb], writes=[pmb], signal=True)
            ptk = pbk
            P.op("tensor", lambda e: e.transpose(ptk[:, 0:128], bigt[:, ks, lo:lo + 128], identb[:]), reads=[*SLC(ks, lo, 128), identb], writes=[ptk])
            wTt = WT.get()
            P.op("scalar", lambda e: e.activation(wTt[:, 0:128], pmb[:, 128:256], AF.Exp, bias=acols[:, ti, h:h + 1], scale=-1.0),
                 reads=[pmb, acols], writes=[wTt])
            if not is_s:
                P.op("scalar", lambda e: e.activation(wTt[:, 128:256], pmb[:, 256:384], AF.Exp, bias=Mpcol[:, h:h + 1], scale=-1.0),
                     reads=[pmb, Mpcol], writes=[wTt])
                P.op("vector", lambda e: e.tensor_copy(Mpcol[:, h:h + 1], pmb[:, 383:384]), reads=[pmb], writes=[Mpcol])
            else:
                P.op("scalar", lambda e: e.activation(wTt[:, 128:256], pmb[:, 256:384], AF.Exp), reads=[pmb], writes=[wTt])
            scT = SA.get()
            P.op("vector", lambda e: e.tensor_tensor(scT[:, 0:128], pst[:, 0:128], wTt[:, 0:128], ALU.mult),
                 reads=[pst, wTt], writes=[scT])
            kg = SA.get()
            gsc = gcols[:, h:h + 1] if is_s else wTt[:, 127:128]
            P.op("scalar", lambda e: e.activation(kg[:, 0:128], ptk[:, 0:128], AF.Identity, scale=gsc),
                 reads=[ptk, wTt, gcols], writes=[kg])
            cx.update(wTt=wTt, scT=scT, kg=kg)
            if not is_s:
                cb16 = CB.get()
                P.op("scalar", lambda e: e.activation(cb16[:, 0:257], C32[:, h, :], AF.Copy), reads=[C32h[h]], writes=[cb16])
                cx.update(cb16=cb16)
                qp = SA.get()
                P.op("vector", lambda e: e.tensor_tensor(qp[:, 0:128], bigt[:, qs, lo:lo + 128], wTt[:, 128:256], ALU.mult),
                     reads=[*SLC(qs, lo, 128), wTt], writes=[qp])
                cx.update(qp=qp)

        def stageB(cx):
            ti, h, hh, is_s = cx["ti"], cx["h"], cx["hh"], cx["is_s"]
            lo = ti * 128
            qs, ks = h, 8 + h
            va, va3, og = cx["va"], cx["va3"], cx["og"]
            vah = va3[:, hh, :]
            wTt, scT, kg = cx["wTt"], cx["scT"], cx["kg"]
            pbk = cx["pbk"]
            pdcv = pbk[:, 384:898].bitcast(F32)
            if not is_s:
                P.op("tensor", lambda e: e.matmul(pdcv, lhsT=kg[:, 0:128], rhs=vah, start=True, stop=True),
                     reads=[kg, va], writes=[pbk])
                P.op("vector", lambda e: e.scalar_tensor_tensor(C32[:, h, :], C32[:, h, :], wTt[:, 255:256], pdcv, ALU.mult, ALU.add),
                     reads=[pbk, wTt, C32h[h]], writes=[C32h[h]])
            pnum = PN.get()
            if not is_s:
                qp = cx["qp"]
                cb16 = cx["cb16"]
                for kk in range(NFILL):
                    P.op("tensor", lambda e: e.matmul(pnum[:, 0:257], lhsT=zerob[:], rhs=vah, start=(kk == 0), stop=False),
                         reads=[zerob, va], writes=[pnum], signal=False)
                P.op("tensor", lambda e: e.matmul(pnum[:, 0:257], lhsT=qp[:, 0:128], rhs=cb16[:, 0:257], start=(NFILL == 0), stop=False),
                     reads=[qp, cb16], writes=[pnum], signal=False)
                P.op("tensor", lambda e: e.matmul(pnum[:, 0:257], lhsT=scT[:, 0:128], rhs=vah, start=False, stop=True),
                     reads=[scT, va], writes=[pnum], signal=True)
            else:
                for half in range(2):
                    P.op("vector", lambda e: e.tensor_tensor(
                        Bigq[:, :, 120:128], bigt[:, qs, lo + half * 64:lo + half * 64 + 64].rearrange("p (b r) -> p b r", b=8),
                        wTt[:, 128 + half * 64:128 + half * 64 + 64].rearrange("p (b r) -> p b r", b=8), ALU.mult),
                        reads=[*SLC(qs, lo, 128), wTt], writes=Bigb)
                    cs = U16[half]
                    cs3 = cs[:, 0:2056].rearrange("p (b e) -> p b e", b=8)
                    P.dma("sync", cs3[:, :, 0:256], d_Cst[half * 8:(half + 1) * 8, h, :, :].rearrange("b p e -> p b e"), writes=[cs])
                    P.op("vector", lambda e: e.tensor_copy(
                        cs3[:, :, 256:257], nsT[:].rearrange("p (b h) -> p b h", h=8)[:, half * 8:(half + 1) * 8, h:h + 1]),
                        reads=[nsT], writes=[cs])
                    P.op("scalar", lambda e: e.activation(Cs16.rearrange("p b e -> p (b e)"), cs[:, 0:2056], AF.Copy), reads=[cs], writes=Cs16b)
                    for b8 in range(8):
                        bb = half * 8 + b8
                        P.op("tensor", lambda e: e.matmul(
                            pnum[:, 0:257], lhsT=win_ap(Bigq, b8, bb), rhs=Cs16[:, b8, :],
                            start=(bb == 0), stop=False), reads=Bigb + Cs16b, writes=[pnum], signal=(b8 == 7))
                P.op("tensor", lambda e: e.matmul(pnum[:, 0:257], lhsT=scT[:, 0:128], rhs=vah, start=False, stop=True),
                     reads=[scT, va], writes=[pnum], signal=True)
            sc = SC.get()
            st6 = SC.get()
            P.op("scalar", lambda e: e.activation(sc[:, 0:1], pnum[:, 256:257], AF.Square), reads=[pnum], writes=[sc])
            P.op("vector", lambda e: e.bn_stats(st6[:, 0:6], pnum[:, 0:256]), reads=[pnum], writes=[st6])
            P.op("vector", lambda e: e.bn_aggr(st6[:, 6:8], st6[:, 0:6]), reads=[st6], writes=[st6])
            P.op("vector", lambda e: e.tensor_tensor(sc[:, 1:2], sc[:, 0:1], acols[:, ti, 8 + h:9 + h], ALU.max), reads=[sc, acols], writes=[sc])
            P.op("vector", lambda e: e.scalar_tensor_tensor(sc[:, 1:2], sc[:, 1:2], EPS, st6[:, 7:8], ALU.mult, ALU.add), reads=[sc, st6], writes=[sc])
            if is_s:
                for half in range(2):
                    cs = U16[half]
                    cs3 = cs[:, 0:2056].rearrange("p (b e) -> p b e", b=8)
                    P.op("vector", lambda e: e.tensor_tensor(Vbd, vah.unsqueeze(1).to_broadcast([128, 8, 257]),
                                                             bmask[:, half * 8:(half + 1) * 8].unsqueeze(2).to_broadcast([128, 8, 257]), ALU.mult),
                         reads=[va, bmask], writes=Vbdb)
                    for b8 in range(8):
                        bb = half * 8 + b8
                        pdx = PS.get()
                        P.op("tensor", lambda e: e.matmul(pdx[:, 0:257], lhsT=kg[:, 0:128], rhs=Vbd[:, b8, :], start=True, stop=True),
                             reads=[kg] + Vbdb, writes=[pdx])
                        P.op("vector", lambda e: e.scalar_tensor_tensor(cs3[:, b8, :], cs3[:, b8, :], decb[:, h, bb:bb + 1], pdx[:, 0:257], ALU.mult, ALU.add),
                             reads=[pdx, decb, cs], writes=[cs])
                    P.dma("sync", o_sC[half * 8:(half + 1) * 8, h, :, :].rearrange("b p e -> p b e"), cs3[:, :, 0:256], reads=[cs], sembuf=cs, is_output=True)
                    P.op("vector", lambda e: e.tensor_copy(
                        nsT[:].rearrange("p (b h) -> p b h", h=8)[:, half * 8:(half + 1) * 8, h:h + 1], cs3[:, :, 256:257]),
                        reads=[cs], writes=[nsT])
            cx.update(sc=sc, st6=st6, pnum=pnum)

        def stageB2(cx):
            sc = cx["sc"]
            rsqrt_act(sc[:, 2:3], sc[:, 1:2], 1.0, 0.0, [sc], [sc])

        def stageB3(cx):
            ti, h, hh, is_s = cx["ti"], cx["h"], cx["hh"], cx["is_s"]
            va, va3, og = cx["va"], cx["va3"], cx["og"]
            vah = va3[:, hh, :]
            wTt, scT, kg = cx["wTt"], cx["scT"], cx["kg"]
            sc, st6, pnum = cx["sc"], cx["st6"], cx["pnum"]
            hn = S32.get()
            P.op("vector", lambda e: e.scalar_tensor_tensor(hn[:, 0:256], pnum[:, 0:256], st6[:, 6:7], og[:, hh * 256:(hh + 1) * 256], ALU.subtract, ALU.mult),
                 reads=[pnum, st6, og], writes=[hn])
            gt = GT.get()
            P.op("scalar", lambda e: e.activation(gt[:, 0:256], hn[:, 0:256], AF.Identity, scale=sc[:, 2:3]),
                 reads=[hn, sc], writes=[gt])
            cx.update(gt=gt)

        def stageC(cx):
            ti, h = cx["ti"], cx["h"]
            lo = ti * 128
            qs, ks = h, 8 + h
            gt = cx["gt"]
            ptg = cx["pbk"]
            for e2 in range(2):
                P.op("tensor", lambda e: e.transpose(ptg[:, 128 + e2 * 128:128 + (e2 + 1) * 128], gt[:, e2 * 128:(e2 + 1) * 128], identb[:]),
                     reads=[gt, identb], writes=[ptg], signal=(e2 == 1))
            for e2 in range(2):
                sl = qs if e2 == 0 else ks
                P.op("vector", lambda e: e.tensor_scalar(bigt[:, sl, lo:lo + 128], ptg[:, 128 + e2 * 128:128 + (e2 + 1) * 128],
                                                         gncol[:, 2 * h + e2:2 * h + e2 + 1], None, ALU.mult),
                     reads=[ptg, gncol], writes=SLC(sl, lo, 128))

        for pr in range(4):
            sv_ = ring_load([(0, 4096, d_wB_in[4 + pr].rearrange("p a b -> p (a b)"))])
            so_ = ring_load([(0, 4096, d_wB_in[8 + pr].rearrange("p a b -> p (a b)"))])
            items = []
            for ti in range(NT):
                is_s = has_sample and ti == NT - 1
                for hh in range(2):
                    items.append(dict(ti=ti, h=pr * 2 + hh, hh=hh, is_s=is_s))
            NI = len(items)
            pj = {}
            for i in range(NI + 3):
                if i < NI:
                    cx = items[i]
                    cx["idx"] = i
                    if cx["hh"] == 0:
                        pj[cx["ti"]] = proj(pr, cx["ti"], sv_, so_)
                    cx["va"], cx["va3"], cx["og"], sigf = pj[cx["ti"]]
                    stageA(cx)
                    if cx["hh"] == 1:
                        sigf()
                if 0 <= i - 1 < NI:
                    stageB(items[i - 1])
                if 0 <= i - 2 < NI:
                    stageB2(items[i - 2])
                    stageB3(items[i - 2])
                if 0 <= i - 3 < NI:
                    stageC(items[i - 3])
        if last_pass:
            P.dma("sync", o_pC.rearrange("h p e -> p h e"), C32[:, :, 0:256], reads=C32h, sembuf=C32h[0], is_output=True)
            P.dma("sync", o_pn, C32[:, :, 256], reads=C32h, sembuf=C32h[0], is_output=True)
        if has_sample:
            pn2 = PA.get()
            P.op("tensor", lambda e: e.transpose(pn2[:, 0:128], nsT[:], identf[:]), reads=[nsT, identf], writes=[pn2])
            nout = S32.get()
            P.op("vector", lambda e: e.tensor_copy(nout[:, 0:128], pn2[:, 0:128]), reads=[pn2], writes=[nout])
            P.dma("sync", o_sn, nout[:, 0:128], reads=[nout], sembuf=nout, is_output=True)
        P.mark('B.out')
        kslots = []
        for e_ in range(16):
            kslots.append(e_ // 2 if e_ % 2 == 0 else 8 + e_ // 2)
        out_proj(d_wB_out, groups, kslots)

    lnsc = P.sbuf("lnsc", [128, 1], F32)
    LNSCALE_AP = lnsc
    P.op("vector", lambda e: e.memset(lnsc[:], LNSCALE), writes=[lnsc])
    dMs_t = P.sbuf("dMs", [8, 128], F32)
    dMs = dMs_t
    dMsb = dMs_t
    cur_tiles = [None]

    def til_idx(ti):
        return ti

    def win_ap(Bq, b8, bb):
        off = 120 - 8 * bb
        return Bq[:, b8, off:off + 128]

    passes = [dict(ptiles=list(range(0, 8)), sample=False), dict(ptiles=list(range(8, 16)), sample=True)]
    for pi, ps_ in enumerate(passes):
        ntiles = len(ps_["ptiles"]) + (1 if ps_["sample"] else 0)
        groups = groups_of(ntiles, ps_["sample"])
        tiles = list(range(ntiles))
        first, last = pi == 0, pi == len(passes) - 1
        p0 = ps_["ptiles"][0] * 128
        npc = len(ps_["ptiles"]) * 128
        for g in groups:
            lo, n = g["lo"], g["n"]
            if g["sample"]:
                P.dma("sync", xT[:, :, lo:lo + n], d_xTs, writes=[xg[g["gi"]]])
            else:
                P.dma("sync", xT[:, :, lo:lo + n], d_xTp[:, :, p0 + lo:p0 + lo + n], writes=[xg[g["gi"]]])
        step = 0
        if step < stop_after:
            mixer_A(groups, tiles, ps_["sample"])
        step += 1
        if step < stop_after:
            ffn(0, groups, last)
        step += 1
        if step < stop_after:
            mixer_B(groups, tiles, ps_["sample"], first, last)
        step += 1
        if step < stop_after:
            ffn(1, groups, last)
        P.mark('final')
        def dst(g, c, xin, gc, rs, rstd):
            lo, n = g["lo"], g["n"]
            yt = S32.get()
            P.op("vector", lambda e, yt=yt: e.scalar_tensor_tensor(yt[:, 0:n], xin, gc, rs, ALU.mult, ALU.mult),
                 reads=[xg[g["gi"]], rstd, gcol], writes=[yt])
            if g["sample"]:
                P.dma("sync", o_yTs[:, c, :], yt[:, 0:n], reads=[yt], sembuf=yt, is_output=True)
            else:
                P.dma("sync", o_yTp[:, c, p0 + lo:p0 + lo + n], yt[:, 0:n], reads=[yt], sembuf=yt, is_output=True)
        rmsnorm_to(4, groups, dst)

    P.mark('end')
    P.finish("sync")
    if os.environ.get('MK_MARKS'):
        import json
        json.dump(P.marks, open(os.environ['MK_MARKS'], 'w'))
    P.emit()
    P.close()
    print("instr counts", {e: len(P.rec[e]) for e in ENGS}, "sems", P.nsem)
    return nc


def _pk(w, ncols):
    K, N = w.shape
    kc = K // 128
    return np.ascontiguousarray(w.reshape(kc, 128, N // ncols, ncols).transpose(2, 1, 0, 3))


def _consts():
    bf = ml_dtypes.bfloat16
    s = np.arange(128)[:, None]
    t = np.arange(128)[None, :]
    c = {}
    c["identb"] = np.eye(128, dtype=np.float32).astype(bf)
    c["identf"] = np.eye(128, dtype=np.float32)
    c["onesb"] = np.ones((128, 128), np.float32).astype(bf)
    c["trip"] = np.where(s <= t, 0.0, BIG).astype(np.float32).astype(bf)
    c["bdp"] = np.where((s <= t) & (s // 8 == t // 8), 0.0, BIG).astype(np.float32).astype(bf)
    c["tri01"] = (s >= t).astype(np.float32)
    c["bmask"] = (np.arange(128)[:, None] // 8 == np.arange(16)[None, :]).astype(np.float32).astype(bf)
    sel = np.zeros((8, 8, 128), np.float32)
    for h in range(8):
        sel[h, h, :] = 1.0
    c["sel"] = sel
    rst = np.zeros((8, 2, 128), np.float32)
    start = (np.arange(128) % 8 == 0)
    rst[:, 0, :] = np.where(start, 0.0, 1.0)
    rst[:, 1, :] = np.where(start, -1e30, 0.0)
    c["rst"] = rst
    return c


_NC_CACHE = {}


def kernel(**inputs):
    f32 = np.float32
    inp = {k: np.asarray(v) for k, v in inputs.items()}
    stop_after = int(os.environ.get("MK_STOP", "99"))
    if stop_after not in _NC_CACHE:
        _NC_CACHE[stop_after] = build_program(stop_after)
    nc = _NC_CACHE[stop_after]

    shared = {}
    a_w_in = inp["a_w_in"][0]
    shared["wA_in"] = _pk(a_w_in, 512)
    shared["wA_out"] = _pk(inp["a_w_out"][0], 256)
    wfu = []
    for l in range(2):
        w = inp["f_w_up"][l]
        a = w[:, :D_FF].reshape(8, 128, 11, 256)
        g = w[:, D_FF:].reshape(8, 128, 11, 256)
        wfu.append(np.concatenate([a, g], axis=3).transpose(2, 1, 0, 3))
    shared["wF_up"] = np.ascontiguousarray(np.stack(wfu))
    shared["wF_dn"] = np.ascontiguousarray(np.stack([_pk(inp["f_w_down"][l], 128) for l in range(2)]))
    b_w_in = inp["b_w_in"][0]
    shared["wB_in"] = _pk(b_w_in[:, :6144], 512)
    shared["wB_g"] = np.ascontiguousarray(b_w_in[:, 6144:6160].reshape(8, 128, 16).transpose(1, 0, 2))
    shared["wB_out"] = _pk(inp["b_w_out"][0], 256)
    gall = np.stack([inp["norm_mix_g"][0], inp["norm_mix_g"][1], inp["norm_ffn_g"][0], inp["norm_ffn_g"][1], inp["final_norm_g"]])
    shared["gcol"] = np.ascontiguousarray(gall.reshape(5, 8, 128).transpose(2, 0, 1))
    shared["fcw"] = np.ascontiguousarray(inp["f_conv_w"].reshape(2, 3, NJ, 128).transpose(3, 0, 1, 2))
    shared["fcb"] = np.ascontiguousarray(inp["f_conv_b"].reshape(2, NJ, 128).transpose(2, 0, 1))
    shared["bcw"] = np.ascontiguousarray(inp["b_conv_w"][0].reshape(4, 16, 128).transpose(2, 0, 1))
    shared["bcb"] = np.ascontiguousarray(inp["b_conv_b"][0].reshape(16, 128).transpose(1, 0))
    shared["bif"] = np.ascontiguousarray(np.stack([inp["b_bias_i"][0], inp["b_bias_f"][0]], axis=1))
    shared["gncol"] = np.ascontiguousarray(inp["b_gn_g"][0].reshape(16, 128).transpose(1, 0))
    shared["lng"] = np.ascontiguousarray(inp["a_ln_g"][0:1])
    shared["lnb"] = np.ascontiguousarray(inp["a_ln_b"][0:1])
    ws = inp["a_w_s"][0]
    shared["ws"] = np.ascontiguousarray(ws.transpose(1, 0, 2))
    wsbd = np.zeros((8, 128, 128), f32)
    for b in range(16):
        wsbd[:, 8 * b:8 * b + 8, 8 * b:8 * b + 8] = ws[:, :8, :8]
    shared["wsbd"] = np.ascontiguousarray(wsbd.transpose(1, 0, 2))
    bs = inp["a_b_s"][0]
    shared["bs8"] = np.ascontiguousarray(bs)
    shared["bs8s"] = np.ascontiguousarray(np.tile(bs[:, :8], (1, 16)))
    shared.update(_consts())
    shared = {k: (v if v.dtype != np.float64 else v.astype(f32)) for k, v in shared.items()}

    in_maps = []
    for c in range(8):
        m = dict(shared)
        m["xTp"] = np.ascontiguousarray(inp["x_prompt"][c].T.reshape(8, 128, 2048).transpose(1, 0, 2))
        xs = inp["x_sample"][16 * c:16 * c + 16].reshape(128, 1024)
        m["xTs"] = np.ascontiguousarray(xs.T.reshape(8, 128, 128).transpose(1, 0, 2))
        m["Cst"] = np.ascontiguousarray(inp["state_mlstm_C"][0, 16 * c:16 * c + 16])
        m["nst"] = np.ascontiguousarray(inp["state_mlstm_n"][0, 16 * c:16 * c + 16].reshape(128, 128))
        m["m0T"] = np.ascontiguousarray(inp["state_mlstm_m"][0, 16 * c:16 * c + 16].T)
        cv = inp["state_mlstm_conv"][0, 16 * c:16 * c + 16]
        m["cvs"] = np.ascontiguousarray(cv.reshape(16, 3, 16, 128).transpose(3, 2, 0, 1))
        ff = inp["state_ffn_conv"][:, 16 * c:16 * c + 16]
        m["ffs"] = np.ascontiguousarray(ff.reshape(2, 16, 2, NJ, 128).transpose(4, 0, 3, 1, 2))
        in_maps.append(m)

    res = run_bass_kernel_spmd(nc, in_maps, core_ids=list(range(8)))
    R = res.results

    y_prompt = np.stack([R[c]["yTp"].transpose(1, 0, 2).reshape(1024, 2048).T for c in range(8)]).astype(f32)
    y_sample = np.concatenate([R[c]["yTs"].transpose(1, 0, 2).reshape(1024, 128).T.reshape(16, 8, 1024) for c in range(8)]).astype(f32)
    pC = np.stack([R[c]["pC"] for c in range(8)])[None].astype(f32)
    pn = np.stack([R[c]["pn"].T for c in range(8)])[None].astype(f32)
    pm = np.stack([R[c]["pm"][:, 0] for c in range(8)])[None].astype(f32)
    pconv = np.stack([R[c]["pconv"].transpose(2, 1, 0).reshape(3, 2048) for c in range(8)])[None].astype(f32)
    pffn = np.stack([R[c]["pffn"].transpose(1, 3, 2, 0).reshape(2, 2, D_FF) for c in range(8)], axis=1).astype(f32)
    sv = np.concatenate([R[c]["sv"].reshape(16, 8, 2048) for c in range(8)])[None].astype(f32)
    sC = np.concatenate([R[c]["sC"] for c in range(8)])[None].astype(f32)
    sn = np.concatenate([R[c]["sn"].reshape(16, 8, 128) for c in range(8)])[None].astype(f32)
    sm = np.concatenate([R[c]["sm"].T for c in range(8)])[None].astype(f32)
    sconv = np.concatenate([R[c]["sconv"].transpose(2, 3, 1, 0).reshape(16, 3, 2048) for c in range(8)])[None].astype(f32)
    sffn = np.concatenate([R[c]["sffn"].transpose(1, 3, 4, 2, 0).reshape(2, 16, 2, D_FF) for c in range(8)], axis=1).astype(f32)
    return (y_prompt, y_sample, pC, pn, pm, pconv, pffn, sv, sC, sn, sm, sconv, sffn)
```

```python
import os
import types
from contextlib import ExitStack
import numpy as np
import ml_dtypes
import concourse.bass as bass
import concourse.mybir as mybir
from concourse.bass_utils import run_bass_kernel_spmd

F32 = mybir.dt.float32
BF16 = mybir.dt.bfloat16
ALU = mybir.AluOpType
AF = mybir.ActivationFunctionType

ENGS = ("tensor", "vector", "scalar", "gpsimd", "sync")
EPS = 1e-6
LNSCALE = -0.5 * float(np.log(128.0))
NFILL = 10
BIG = 30000.0
D_FF = 2816
NJ = 22
TW = 1152


def _freeze(fn):
    if fn.__closure__ is None:
        return fn
    cells = tuple(types.CellType(c.cell_contents) for c in fn.__closure__)
    return types.FunctionType(fn.__code__, fn.__globals__, fn.__name__, fn.__defaults__, cells)


class Buf:
    __slots__ = ("name", "t", "last_write", "reads", "dsem", "dcount", "excl")

    def __init__(self, name, t=None, excl=False):
        self.name = name
        self.t = t
        self.excl = excl
        self.last_write = None
        self.reads = []
        self.dsem = None
        self.dcount = 0

    def __getitem__(self, idx):
        return self.t[idx]


class Prog:
    def __init__(self, nc):
        self.nc = nc
        self.es = ExitStack()
        self.rec = {e: [] for e in ENGS}
        self.count = {e: 0 for e in ENGS}
        self.waited = {e: {} for e in ENGS}
        self.esem = {}
        self.sems = {}
        self.nsem = 0
        for e in ENGS:
            self.esem[e] = self.new_sem("done_" + e)
        self.owner = {v: k for k, v in self.esem.items()}
        self.out_events = []
        self.marks = []

    def new_sem(self, name):
        s = self.es.enter_context(self.nc.semaphore(name))
        self.nsem += 1
        self.sems[name] = s
        return name

    def sbuf(self, name, shape, dtype):
        t = self.es.enter_context(self.nc.sbuf_tensor("sb_" + name, list(shape), dtype))
        return Buf(name, t)

    def psum(self, name, shape, dtype=F32):
        t = self.es.enter_context(self.nc.psum_tensor("ps_" + name, list(shape), dtype))
        return Buf(name, t, excl=True)

    def _waits(self, eng, reads, writes, dsem=None):
        w = {}

        def need(ev, same_ok):
            if ev is None:
                return
            sk, val = ev
            if same_ok and sk == self.esem[eng]:
                return
            if val > w.get(sk, 0):
                w[sk] = val
        for b in reads:
            need(b.last_write, False)
            if b.excl:
                for ev in b.reads:
                    need(ev, True)
        for b in writes:
            lw = b.last_write
            if not (lw is not None and dsem is not None and lw[0] == dsem):
                need(lw, True)
            for ev in b.reads:
                need(ev, True)
        out = []
        for sk, val in w.items():
            if self.waited[eng].get(sk, 0) >= val:
                continue
            if sk in self.owner and val > self.count[self.owner[sk]]:
                raise RuntimeError(f"{eng} waits on unsignaled op of {self.owner[sk]}")
            self.waited[eng][sk] = val
            out.append((sk, val))
        return out

    def mark(self, name):
        self.marks.append((name, sum(1 for r in self.rec["tensor"] if r[1] is not None)))

    def op(self, eng, fn, reads=(), writes=(), signal=True):
        fn = _freeze(fn)
        waits = self._waits(eng, reads, writes)
        if signal:
            self.count[eng] += 1
            ev = (self.esem[eng], self.count[eng])
        else:
            ev = (self.esem[eng], self.count[eng] + 1)
        self.rec[eng].append((waits, fn, (self.esem[eng], 1) if signal else None))
        for b in reads:
            b.reads.append(ev)
        for b in writes:
            b.last_write = ev
            b.reads = []
        return ev

    def dma(self, eng, out_ap, in_ap, reads=(), writes=(), sembuf=None, is_output=False):
        sb = sembuf if sembuf is not None else (writes[0] if writes else reads[0])
        if sb.dsem is None:
            sb.dsem = self.new_sem("dma_" + sb.name)
        waits = self._waits(eng, reads, writes, dsem=sb.dsem)
        sb.dcount += 16
        ev = (sb.dsem, sb.dcount)

        def fn(e, out_ap=out_ap, in_ap=in_ap):
            return e.dma_start(out=out_ap, in_=in_ap)
        self.rec[eng].append((waits, fn, (sb.dsem, 16)))
        for b in reads:
            b.reads.append(ev)
        for b in writes:
            b.last_write = ev
            b.reads = []
        if is_output:
            self.out_events.append(ev)
        return ev

    def finish(self, eng="sync"):
        w = {}
        for sk, val in self.out_events:
            w[sk] = max(w.get(sk, 0), val)
        self.rec[eng].append(([(sk, v) for sk, v in w.items()], None, None))

    def emit(self):
        with self.nc.Block() as block:
            def make(engname):
                def body(e):
                    with self.nc.allow_non_contiguous_dma(reason="tiny state rows"):
                        for waits, fn, inc in self.rec[engname]:
                            for sk, val in waits:
                                e.wait_ge(self.sems[sk], val)
                            if fn is not None:
                                ins = fn(e)
                                if inc is not None:
                                    ins.then_inc(self.sems[inc[0]], inc[1])
                return body
            block.tensor(make("tensor"))
            block.vector(make("vector"))
            block.scalar(make("scalar"))
            block.gpsimd(make("gpsimd"))
            block.sync(make("sync"))

    def close(self):
        self.es.close()


class Pool:
    def __init__(self, bufs):
        self.bufs = bufs
        self.i = 0

    def get(self):
        b = self.bufs[self.i % len(self.bufs)]
        self.i += 1
        return b


def build_program(stop_after=99):
    nc = bass.Bass("TRN2", target_bir_lowering=False)
    P = Prog(nc)

    def din(name, shape, dt=F32):
        return nc.dram_tensor(name, list(shape), dt, kind="ExternalInput").ap()

    def dout(name, shape, dt=F32):
        return nc.dram_tensor(name, list(shape), dt, kind="ExternalOutput").ap()

    d_xTp = din("xTp", [128, 8, 2048]); d_xTs = din("xTs", [128, 8, 128])
    d_wA_in = din("wA_in", [8, 128, 8, 512]); d_wA_out = din("wA_out", [4, 128, 16, 256])
    d_wF_up = din("wF_up", [2, 11, 128, 8, 512]); d_wF_dn = din("wF_dn", [2, 8, 128, 22, 128])
    d_wB_in = din("wB_in", [12, 128, 8, 512]); d_wB_g = din("wB_g", [128, 8, 16])
    d_wB_out = din("wB_out", [4, 128, 16, 256])
    d_gcol = din("gcol", [128, 5, 8]); d_fcw = din("fcw", [128, 2, 3, 22]); d_fcb = din("fcb", [128, 2, 22])
    d_bcw = din("bcw", [128, 4, 16]); d_bcb = din("bcb", [128, 16]); d_bif = din("bif", [8, 2])
    d_gncol = din("gncol", [128, 16])
    d_lng = din("lng", [1, 2048]); d_lnb = din("lnb", [1, 2048])
    d_ws = din("ws", [128, 8, 128]); d_wsbd = din("wsbd", [128, 8, 128])
    d_bs8 = din("bs8", [8, 128]); d_bs8s = din("bs8s", [8, 128])
    d_Cst = din("Cst", [16, 8, 128, 256]); d_nst = din("nst", [128, 128]); d_m0T = din("m0T", [8, 16])
    d_cvs = din("cvs", [128, 16, 16, 3]); d_ffs = din("ffs", [128, 2, 22, 16, 2])
    d_identb = din("identb", [128, 128], BF16); d_identf = din("identf", [128, 128])
    d_onesb = din("onesb", [128, 128], BF16)
    d_trip = din("trip", [128, 128], BF16); d_bdp = din("bdp", [128, 128], BF16)
    d_tri01 = din("tri01", [128, 128]); d_bmask = din("bmask", [128, 16], BF16)
    d_sel = din("sel", [8, 8, 128]); d_rst = din("rst", [8, 2, 128])
    o_yTp = dout("yTp", [128, 8, 2048]); o_yTs = dout("yTs", [128, 8, 128])
    o_pC = dout("pC", [8, 128, 256]); o_pn = dout("pn", [128, 8]); o_pm = dout("pm", [8, 1])
    o_pconv = dout("pconv", [128, 16, 3]); o_pffn = dout("pffn", [128, 2, 22, 2])
    o_sv = dout("sv", [128, 2048])
    o_sC = dout("sC", [16, 8, 128, 256]); o_sn = dout("sn", [128, 128]); o_sm = dout("sm", [8, 16])
    o_sconv = dout("sconv", [128, 16, 16, 3]); o_sffn = dout("sffn", [128, 2, 22, 16, 2])

    xT = P.sbuf("xT", [128, 8, TW], F32)
    hT = P.sbuf("hT", [128, 8, TW], BF16)
    bigt = P.sbuf("big", [128, NJ, TW], BF16)
    xg = [Buf(f"xg{i}") for i in range(3)]
    hg = [Buf(f"hg{i}") for i in range(3)]
    slotL = [[Buf(f"slot{i}_{t}") for t in range(9)] for i in range(NJ)]

    def SLC(c, lo, n):
        return slotL[c][lo // 128:(lo + n + 127) // 128]

    def SLW(c):
        return list(slotL[c])
    NRING = 4
    ring = [P.sbuf(f"ring{i}", [128, 4096], BF16) for i in range(NRING)]
    ring_i = [0]
    U16 = [P.sbuf(f"U16_{i}", [128, 2056], F32) for i in range(2)]
    wg = P.sbuf("wg", [128, 8, 16], BF16)
    gcol = P.sbuf("gcol", [128, 5, 8], F32); fcw = P.sbuf("fcw", [128, 2, 3, 22], F32)
    fcb = P.sbuf("fcb", [128, 2, 22], F32); bcw = P.sbuf("bcw", [128, 4, 16], F32)
    bcb = P.sbuf("bcb", [128, 16], F32); bif = P.sbuf("bif", [8, 2], F32); gncol = P.sbuf("gncol", [128, 16], F32)
    wsT = P.sbuf("wsT", [128, 8, 128], BF16); wsTs = P.sbuf("wsTs", [128, 8, 128], BF16)
    bsh = P.sbuf("bsh", [40, 2, 128], BF16)
    selb = P.sbuf("selb", [40, 8, 128], BF16)
    identb = P.sbuf("identb", [128, 128], BF16); identf = P.sbuf("identf", [128, 128], F32)
    onesb = P.sbuf("onesb", [128, 128], BF16); trip = P.sbuf("trip", [128, 128], BF16)
    bdp = P.sbuf("bdp", [128, 128], BF16)
    bmask = P.sbuf("bmask", [128, 16], BF16); sel = P.sbuf("sel", [8, 8, 128], F32)
    rst = P.sbuf("rst", [8, 2, 128], F32)
    Mrows = P.sbuf("Mrows", [8, 1 + TW], F32)
    bcar = P.sbuf("bcar", [8, 2], F32)
    acols = P.sbuf("acols", [128, 9, 16], F32)
    gcols = P.sbuf("gcols", [128, 8], F32)
    C32 = P.sbuf("C32", [128, 8, 257], F32)
    C32h = [Buf(f"C32h{h}") for h in range(8)]
    Mpcol = P.sbuf("Mpcol", [128, 8], F32)
    atail = P.sbuf("atail", [128, 2, NJ, 2], F32)
    qtail = P.sbuf("qtail", [128, 16, 3], F32)
    nsT = P.sbuf("nsT", [128, 128], F32)
    decb = P.sbuf("decb", [128, 8, 16], F32)
    zerob = P.sbuf("zerob", [128, 128], BF16)
    m0row = P.sbuf("m0row", [8, 16, 8], F32)
    onesr = P.sbuf("onesr", [8, 1], F32); zerosr = P.sbuf("zerosr", [8, 1], F32)
    S32 = Pool([P.sbuf(f"S32_{i}", [128, 520], F32) for i in range(3)])
    OG = Pool([P.sbuf(f"OG_{i}", [128, 512], F32) for i in range(2)])
    WT = Pool([P.sbuf(f"WT_{i}", [128, 256], F32) for i in range(3)])
    SA = Pool([P.sbuf(f"SA_{i}", [128, 128], BF16) for i in range(7)])
    CB = Pool([P.sbuf(f"CB_{i}", [128, 258], BF16) for i in range(3)])
    GT = Pool([P.sbuf(f"GT_{i}", [128, 256], BF16) for i in range(3)])
    RS = Pool([P.sbuf(f"RS_{i}", [128, 512], F32) for i in range(1)])
    VA = Pool([P.sbuf(f"VA_{i}", [128, 514], BF16) for i in range(2)])
    S16 = Pool([P.sbuf(f"S16_{i}", [128, 512], BF16) for i in range(2)])
    SC = Pool([P.sbuf(f"SC_{i}", [128, 16], F32) for i in range(8)])
    HS = Pool([P.sbuf(f"HS_{i}", [128, 48], F32) for i in range(3)])
    PA = Pool([P.psum(f"pa{i}", [128, 512], F32) for i in range(6)])
    PB = Pool([P.psum(f"pb{i}", [128, 1024], BF16) for i in range(2)])
    print("sbuf remaining", nc.sbuf_bytes_remaining, "sems", P.nsem)

    def V(e):
        return "vector"

    onetime = []
    for sb, d in ((gcol, d_gcol), (fcw, d_fcw), (fcb, d_fcb), (bcw, d_bcw), (bcb, d_bcb), (bif, d_bif),
                  (gncol, d_gncol), (identb, d_identb), (identf, d_identf),
                  (onesb, d_onesb), (trip, d_trip), (bdp, d_bdp), (bmask, d_bmask),
                  (sel, d_sel), (rst, d_rst)):
        P.dma("sync", sb[:], d, writes=[sb], sembuf=gcol)
        onetime.append(sb)
    for sb in onetime:
        sb.last_write = (gcol.dsem, gcol.dcount)
    P.dma("gpsimd", wg[:], d_wB_g, writes=[wg])
    P.op("vector", lambda e: e.memset(onesr[:], 1.0), writes=[onesr])
    P.op("vector", lambda e: e.memset(zerosr[:], 0.0), writes=[zerosr])
    P.op("vector", lambda e: e.memset(C32[:], 0.0), writes=C32h)
    P.op("vector", lambda e: e.memset(Mrows[:, 0:1], 0.0), writes=[Mrows])
    P.op("vector", lambda e: e.memset(bcar[:], 0.0), writes=[bcar])
    P.op("vector", lambda e: e.memset(atail[:], 0.0), writes=[atail])
    P.op("vector", lambda e: e.memset(qtail[:], 0.0), writes=[qtail])
    P.op("vector", lambda e: e.memset(Mpcol[:], 0.0), writes=[Mpcol])
    P.op("vector", lambda e: e.memset(zerob[:], 0.0), writes=[zerob])

    tri01 = S32.get()
    P.dma("sync", tri01[:, 0:128], d_tri01, writes=[tri01])
    bias_init_done = [False]

    def init_bias():
        if bias_init_done[0]:
            return
        bias_init_done[0] = True
        P.op("vector", lambda e: e.memset(selb[:], 0.0), writes=[selb])
        P.op("vector", lambda e: e.memset(bsh[:], 0.0), writes=[bsh])
        P.dma("gpsimd", selb[0:8, :, :], d_sel, writes=[selb])
        P.dma("gpsimd", selb[32:40, :, :], d_sel, writes=[selb])
        for i_, dsrc_ in enumerate((d_bs8, d_bs8s)):
            src_ = S32.get()
            sap = src_[0:8, 0:128]
            P.dma("sync", sap, dsrc_, writes=[src_])
            P.op("vector", lambda e: e.tensor_copy(bsh[0:8, i_, :], sap), reads=[src_], writes=[bsh])
            P.op("vector", lambda e: e.tensor_tensor(sap, sap, bsh[0:8, i_, :], ALU.subtract), reads=[src_, bsh], writes=[src_])
            P.dma("gpsimd", bsh[32:40, i_, :], sap, reads=[src_], writes=[bsh])
    for (dsrc, dst) in ((d_ws, wsT), (d_wsbd, wsTs)):
        for half in range(2):
            t4 = U16[half]
            P.dma("sync", t4[:, 0:512].rearrange("p (g s) -> p g s", g=4), dsrc[:, half * 4:(half + 1) * 4, :], writes=[t4])
            P.op("vector", lambda e, t4=t4: e.tensor_tensor(t4[:, 0:512].rearrange("p (g s) -> p g s", g=4),
                                                             t4[:, 0:512].rearrange("p (g s) -> p g s", g=4),
                                                             tri01[:, 0:128].unsqueeze(1).to_broadcast([128, 4, 128]), ALU.mult),
                 reads=[t4, tri01], writes=[t4])
            pb = PA.get()
            for g in range(4):
                P.op("tensor", lambda e, g=g, t4=t4, pb=pb: e.transpose(pb[:, g * 128:(g + 1) * 128], t4[:, g * 128:(g + 1) * 128], identf[:]),
                     reads=[t4, identf], writes=[pb], signal=(g == 3))
            P.op("vector", lambda e, pb=pb, dst=dst, half=half: e.tensor_copy(dst[:, half * 4:(half + 1) * 4, :],
                                                                               pb[:, 0:512].rearrange("p (g s) -> p g s", g=4)),
                 reads=[pb], writes=[dst])

    def ring_load(parts):
        s = ring[ring_i[0] % NRING]
        ring_i[0] += 1
        for (lo, n, src) in parts:
            P.dma("gpsimd", s[:, lo:lo + n], src, writes=[s])
        return s

    def groups_of(ntiles, has_sample):
        gs = []
        npt = ntiles - (1 if has_sample else 0)
        t = 0
        gi = 0
        while t < npt:
            n = min(4, npt - t)
            gs.append(dict(lo=t * 128, n=n * 128, gi=gi, sample=False, tiles=list(range(t, t + n))))
            t += n
            gi += 1
        if has_sample:
            gs.append(dict(lo=npt * 128, n=128, gi=gi, sample=True, tiles=[npt]))
        return gs

    def rsqrt_act(out_ap, in_ap, scale, bias, rbufs, wbufs):
        P.op("scalar", lambda e: e.activation(out_ap, in_ap, AF.Ln, bias=bias, scale=scale), reads=rbufs, writes=wbufs)
        P.op("scalar", lambda e: e.activation(out_ap, out_ap, AF.Exp, scale=-0.5), reads=wbufs, writes=wbufs)

    def rmsnorm_to(gi_idx, groups, dst_fn):
        for g in groups:
            lo, n = g["lo"], g["n"]
            pss = PA.get()
            for c in range(8):
                sq = S16.get()
                P.op("scalar", lambda e, sq=sq, c=c: e.activation(sq[:, 0:n], xT[:, c, lo:lo + n], AF.Square),
                     reads=[xg[g["gi"]]], writes=[sq])
                P.op("tensor", lambda e, sq=sq, c=c: e.matmul(pss[:, 0:n], lhsT=onesb[:], rhs=sq[:, 0:n], start=(c == 0), stop=(c == 7)),
                     reads=[sq, onesb], writes=[pss], signal=True)
            rstd = RS.get()
            rsqrt_act(rstd[:, 0:n], pss[:, 0:n], 1.0 / 1024.0, EPS, [pss], [rstd])
            for c in range(8):
                dst_fn(g, c, xT[:, c, lo:lo + n], gcol[:, gi_idx, c:c + 1], rstd[:, 0:n], rstd)

    def norm_to_hT(gi_idx, groups):
        def dst(g, c, xin, gc, rs, rstd):
            lo, n = g["lo"], g["n"]
            P.op("vector", lambda e: e.scalar_tensor_tensor(hT[:, c, lo:lo + n], xin, gc, rs, ALU.mult, ALU.mult),
                 reads=[xg[g["gi"]], rstd, gcol], writes=[hg[g["gi"]]])
        rmsnorm_to(gi_idx, groups, dst)

    def resid_add(g, m, ps):
        lo, n = g["lo"], g["n"]
        P.op("vector", lambda e: e.tensor_tensor(xT[:, m, lo:lo + n], xT[:, m, lo:lo + n], ps[:, 0:n], ALU.add),
             reads=[ps, xg[g["gi"]]], writes=[xg[g["gi"]]])

    def out_proj(d_w, groups, kslots):
        for mm in range(4):
            s = ring_load([(0, 4096, d_w[mm].rearrange("p a b -> p (a b)"))])
            sv = s[:, 0:4096].rearrange("p (a b) -> p a b", a=16)
            for g in groups:
                lo, n = g["lo"], g["n"]
                for m2 in range(2):
                    ps = PA.get()
                    for ei in range(16):
                        sl = kslots[ei]
                        P.op("tensor", lambda e, ei=ei, sl=sl, ps=ps, m2=m2: e.matmul(
                            ps[:, 0:n], lhsT=sv[:, ei, m2 * 128:(m2 + 1) * 128], rhs=bigt[:, sl, lo:lo + n],
                            start=(ei == 0), stop=(ei == 15)),
                            reads=[s, *SLC(sl, lo, n)], writes=[ps], signal=(ei == 15))
                    resid_add(g, mm * 2 + m2, ps)

    def mixer_A(groups, tiles, has_sample):
        P.mark('A.norm')
        norm_to_hT(0, groups)
        P.dma("sync", U16[0][:, 0:2048], d_lng.partition_broadcast(128), writes=[U16[0]])
        P.dma("sync", U16[1][:, 0:2048], d_lnb.partition_broadcast(128), writes=[U16[1]])
        P.mark('A.u')
        for q in range(4):
            s = ring_load([(0, 4096, d_wA_in[q].rearrange("p a b -> p (a b)"))])
            sv = s[:, 0:4096].rearrange("p (a b) -> p a b", a=8)
            for g in groups:
                lo, n = g["lo"], g["n"]
                for cc in range(4):
                    c = q * 4 + cc
                    ps = PA.get()
                    for k in range(8):
                        P.op("tensor", lambda e, k=k, cc=cc, ps=ps: e.matmul(ps[:, 0:n], lhsT=sv[:, k, cc * 128:(cc + 1) * 128],
                                                                               rhs=hT[:, k, lo:lo + n], start=(k == 0), stop=(k == 7)),
                             reads=[s, hg[g["gi"]]], writes=[ps], signal=(k == 7))
                    P.op("scalar", lambda e, c=c, ps=ps: e.activation(bigt[:, c, lo:lo + n], ps[:, 0:n], AF.Gelu_apprx_tanh),
                         reads=[ps], writes=SLC(c, lo, n))
        P.mark('A.v')
        vp = []
        for q in range(4):
            s = ring_load([(0, 4096, d_wA_in[4 + q].rearrange("p a b -> p (a b)"))])
            vp.append(s)
        init_bias()
        v32 = bigt[:, 16:20, :].rearrange("p a b -> p (a b)").bitcast(F32)[:, 0:2048]
        v32b = SLW(16) + SLW(17) + SLW(18) + SLW(19)
        vln = bigt[:, 20:22, :].rearrange("p a b -> p (a b)")[:, 0:2048]
        vlnb = SLW(20) + SLW(21)
        PV = Pool(PA.bufs[0:4])
        PM = Pool(PA.bufs[4:6])

        def a_part1(ti):
            lo = ti * 128
            gi = lo // 512
            pss_ = []
            for q in range(4):
                s = vp[q]
                sv = s[:, 0:4096].rearrange("p (a b) -> p a b", a=8)
                ps = PV.get()
                for k in range(8):
                    P.op("tensor", lambda e: e.matmul(ps[:, 0:512], lhsT=hT[:, k, lo:lo + 128], rhs=sv[:, k, :],
                                                      start=(k == 0), stop=(k == 7)),
                         reads=[s, hg[gi]], writes=[ps], signal=(k == 7))
                pss_.append(ps)
            return pss_

        def a_part2(ti, pss_):
            is_s = has_sample and ti == len(tiles) - 1
            for q in range(4):
                ps = pss_[q]
                P.op("scalar", lambda e: e.activation(v32[:, q * 512:(q + 1) * 512], ps[:, 0:512], AF.Gelu_apprx_tanh),
                     reads=[ps], writes=v32b)
            st = S32.get()
            for q in range(4):
                P.op("vector", lambda e: e.bn_stats(st[:, q * 6:(q + 1) * 6], v32[:, q * 512:(q + 1) * 512]),
                     reads=v32b, writes=[st])
            mv = SC.get()
            P.op("vector", lambda e: e.bn_aggr(mv[:, 0:2], st[:, 0:24].rearrange("p (a b) -> p a b", a=4)),
                 reads=[st], writes=[mv])
            rsqrt_act(mv[:, 2:3], mv[:, 1:2], 1.0, EPS, [mv], [mv])
            P.op("vector", lambda e: e.tensor_scalar(v32, v32, mv[:, 0:1], mv[:, 2:3], ALU.subtract, ALU.mult),
                 reads=v32b + [mv], writes=v32b)
            P.op("vector", lambda e: e.tensor_tensor(v32, v32, U16[0][:, 0:2048], ALU.mult), reads=v32b + [U16[0]], writes=v32b)
            if is_s:
                P.op("vector", lambda e: e.tensor_tensor(v32, v32, U16[1][:, 0:2048], ALU.add), reads=v32b + [U16[1]], writes=v32b)
                P.dma("sync", o_sv, v32, reads=v32b, sembuf=slotL[16][0], is_output=True)
                P.op("scalar", lambda e: e.activation(vln, v32, AF.Copy), reads=v32b, writes=vlnb)
            else:
                P.op("vector", lambda e: e.tensor_tensor(vln, v32, U16[1][:, 0:2048], ALU.add), reads=v32b + [U16[1]], writes=vlnb)

        def a_part3(ti):
            lo = ti * 128
            is_s = has_sample and ti == len(tiles) - 1
            wT_ = wsTs if is_s else wsT
            bsi = 1 if is_s else 0
            for cb in range(4):
                ps = PM.get()
                for cc in range(4):
                    c = cb * 4 + cc
                    gidx = c // 2
                    P.op("tensor", lambda e: e.matmul(ps[:, cc * 128:(cc + 1) * 128], lhsT=vln[:, c * 128:(c + 1) * 128],
                                                      rhs=wT_[:, gidx, :], start=True, stop=False),
                         reads=vlnb + [wT_], writes=[ps], signal=False)
                    P.op("tensor", lambda e: e.matmul(ps[:, cc * 128:(cc + 1) * 128], lhsT=selb[:, gidx, :],
                                                      rhs=bsh[:, bsi, :], start=False, stop=True),
                         reads=[selb, bsh], writes=[ps], signal=(cc == 3))
                P.op("vector", lambda e: e.tensor_tensor(bigt[:, cb * 4:cb * 4 + 4, lo:lo + 128],
                                                         ps[:, 0:512].rearrange("p (a b) -> p a b", a=4),
                                                         bigt[:, cb * 4:cb * 4 + 4, lo:lo + 128], ALU.mult),
                     reads=[ps] + [b_ for i in range(4) for b_ in SLC(cb * 4 + i, lo, 128)], writes=[b_ for i in range(4) for b_ in SLC(cb * 4 + i, lo, 128)])

        nt_ = len(tiles)
        pend = a_part1(0)
        a_part2(0, pend)
        for ti in range(nt_):
            nxt = a_part1(ti + 1) if ti + 1 < nt_ else None
            a_part3(ti)
            if nxt is not None:
                a_part2(ti + 1, nxt)
        P.mark('A.out')
        out_proj(d_wA_out, groups, list(range(16)))

    def ffn(l, groups, last_pass):
        P.mark('F.norm')
        norm_to_hT(2 + l, groups)
        P.mark('F.up')
        pend_f = []
        pend_a = []
        for jj in range(11):
            s = ring_load([(0, 4096, d_wF_up[l, jj].rearrange("p a b -> p (a b)"))])
            sv = s[:, 0:4096].rearrange("p (a b) -> p a b", a=8)
            for j2 in range(2):
                j = jj * 2 + j2
                w0 = fcw[:, l, 0, j:j + 1]; w1 = fcw[:, l, 1, j:j + 1]; w2 = fcw[:, l, 2, j:j + 1]
                cb_ = fcb[:, l, j:j + 1]
                for g in groups:
                    lo, n = g["lo"], g["n"]
                    psa = PA.get()
                    for k in range(8):
                        P.op("tensor", lambda e, k=k, psa=psa: e.matmul(psa[:, 0:n], lhsT=sv[:, k, j2 * 128:(j2 + 1) * 128],
                                                                        rhs=hT[:, k, lo:lo + n], start=(k == 0), stop=(k == 7)),
                             reads=[s, hg[g["gi"]]], writes=[psa], signal=(k == 7))
                    psg = PA.get()
                    for k in range(8):
                        P.op("tensor", lambda e, k=k, psg=psg: e.matmul(psg[:, 0:n], lhsT=sv[:, k, 256 + j2 * 128:256 + (j2 + 1) * 128],
                                                                        rhs=hT[:, k, lo:lo + n], start=(k == 0), stop=(k == 7)),
                             reads=[s, hg[g["gi"]]], writes=[psg], signal=(k == 7))
                    t0 = S32.get()
                    P.op("scalar", lambda e: e.activation(t0[:, 0:n], psa[:, 0:n], AF.Identity, bias=cb_, scale=w2),
                         reads=[psa, fcw, fcb], writes=[t0])
                    if pend_a:
                        pend_a.pop()()
                    if not g["sample"]:
                        P.op("vector", lambda e: e.scalar_tensor_tensor(t0[:, 1:n], psa[:, 0:n - 1], w1, t0[:, 1:n], ALU.mult, ALU.add),
                             reads=[psa, t0, fcw], writes=[t0])
                        P.op("vector", lambda e: e.scalar_tensor_tensor(t0[:, 2:n], psa[:, 0:n - 2], w0, t0[:, 2:n], ALU.mult, ALU.add),
                             reads=[psa, t0, fcw], writes=[t0])
                        P.op("vector", lambda e: e.scalar_tensor_tensor(t0[:, 0:2], atail[:, l, j, 0:2], w0, t0[:, 0:2], ALU.mult, ALU.add),
                             reads=[atail, t0, fcw], writes=[t0])
                        P.op("vector", lambda e: e.scalar_tensor_tensor(t0[:, 0:1], atail[:, l, j, 1:2], w1, t0[:, 0:1], ALU.mult, ALU.add),
                             reads=[atail, t0, fcw], writes=[t0])

                        def act2(t0=t0, psa=psa, j=j, n=n):
                            P.op("scalar", lambda e: e.activation(atail[:, l, j, :], psa[:, n - 2:n], AF.Copy), reads=[psa], writes=[atail])
                            P.op("scalar", lambda e: e.activation(t0[:, 0:n], t0[:, 0:n], AF.Gelu_apprx_tanh), reads=[t0], writes=[t0])
                    else:
                        hs = HS.get(); os_ = HS.get()
                        hs3 = hs[:, 0:32].rearrange("p (b r) -> p b r", b=16)
                        os3 = os_[:, 0:32].rearrange("p (b r) -> p b r", b=16)
                        psa3 = psa[:, 0:128].rearrange("p (b r) -> p b r", b=16)
                        t03 = t0[:, 0:128].rearrange("p (b r) -> p b r", b=16)
                        P.dma("sync", hs3, d_ffs[:, l, j, :, :], writes=[hs])
                        P.op("vector", lambda e: e.scalar_tensor_tensor(t03[:, :, 1:8], psa3[:, :, 0:7], w1, t03[:, :, 1:8], ALU.mult, ALU.add),
                             reads=[psa, t0, fcw], writes=[t0])
                        P.op("vector", lambda e: e.scalar_tensor_tensor(t03[:, :, 2:8], psa3[:, :, 0:6], w0, t03[:, :, 2:8], ALU.mult, ALU.add),
                             reads=[psa, t0, fcw], writes=[t0])
                        P.op("vector", lambda e: e.scalar_tensor_tensor(t03[:, :, 0:2], hs3[:, :, 0:2], w0, t03[:, :, 0:2], ALU.mult, ALU.add),
                             reads=[hs, t0, fcw], writes=[t0])
                        P.op("vector", lambda e: e.scalar_tensor_tensor(t03[:, :, 0:1], hs3[:, :, 1:2], w1, t03[:, :, 0:1], ALU.mult, ALU.add),
                             reads=[hs, t0, fcw], writes=[t0])

                        def act2(t0=t0, psa3=psa3, psa=psa, os3=os3, os_=os_, j=j, n=n):
                            P.op("scalar", lambda e: e.activation(os3, psa3[:, :, 6:8], AF.Copy), reads=[psa], writes=[os_])
                            P.dma("sync", o_sffn[:, l, j, :, :], os3, reads=[os_], sembuf=os_, is_output=True)
                            P.op("scalar", lambda e: e.activation(t0[:, 0:n], t0[:, 0:n], AF.Gelu_apprx_tanh), reads=[t0], writes=[t0])
                    pend_a.append(act2)
                    if pend_f:
                        pend_f.pop()()

                    def fin(t0=t0, psg=psg, j=j, lo=lo, n=n):
                        P.op("vector", lambda e: e.tensor_tensor(bigt[:, j, lo:lo + n], t0[:, 0:n], psg[:, 0:n], ALU.mult),
                             reads=[t0, psg], writes=SLC(j, lo, n))
                    pend_f.append(fin)
        if pend_a:
            pend_a.pop()()
        if pend_f:
            pend_f.pop()()
        if last_pass:
            P.dma("sync", o_pffn[:, l, :, :], atail[:, l, :, :], reads=[atail], sembuf=atail, is_output=True)
        P.mark('F.down')
        for m in range(8):
            s = ring_load([(0, NJ * 128, d_wF_dn[l, m].rearrange("p a b -> p (a b)"))])
            sv = s[:, 0:NJ * 128].rearrange("p (a b) -> p a b", a=NJ)
            for g in groups:
                lo, n = g["lo"], g["n"]
                ps = PA.get()
                for j in range(NJ):
                    P.op("tensor", lambda e, j=j, ps=ps: e.matmul(ps[:, 0:n], lhsT=sv[:, j, :], rhs=bigt[:, j, lo:lo + n],
                                                                 start=(j == 0), stop=(j == NJ - 1)),
                         reads=[s, *SLC(j, lo, n)], writes=[ps], signal=(j == NJ - 1))
                resid_add(g, m, ps)

    def mixer_B(groups, tiles, has_sample, first_pass, last_pass):
        P.mark('B.norm')
        norm_to_hT(1, groups)
        P.mark('B.qk')
        npt = len(tiles) - (1 if has_sample else 0)
        def gates_gen():
            RW = 520
            def rowbuf(i):
                return bigt[0:8, 16 + i, :].bitcast(F32)[:, 0:RW]
            for g in groups:
                lo, n, gi = g["lo"], g["n"], g["gi"]
                igr, zr, lfr, bgr, t1r, t2r = [rowbuf(i) for i in range(6)]
                rb = [SLW(16 + i) for i in range(6)]
                psi = PA.get()
                for k in range(8):
                    P.op("tensor", lambda e, k=k, psi=psi: e.matmul(psi[0:8, 0:n], lhsT=wg[:, k, 0:8], rhs=hT[:, k, lo:lo + n], start=(k == 0), stop=(k == 7)),
                         reads=[wg, hg[gi]], writes=[psi], signal=(k == 7))
                P.op("scalar", lambda e, psi=psi: e.activation(igr[:, 0:n], psi[0:8, 0:n], AF.Identity, bias=bif[:, 0:1], scale=1.0),
                     reads=[psi, bif], writes=[*rb[0]])
                yield
                psf = PA.get()
                for k in range(8):
                    P.op("tensor", lambda e, k=k, psf=psf: e.matmul(psf[0:8, 0:n], lhsT=wg[:, k, 8:16], rhs=hT[:, k, lo:lo + n], start=(k == 0), stop=(k == 7)),
                         reads=[wg, hg[gi]], writes=[psf], signal=(k == 7))
                P.op("scalar", lambda e, psf=psf: e.activation(zr[:, 0:n], psf[0:8, 0:n], AF.Identity, bias=bif[:, 1:2], scale=1.0),
                     reads=[psf, bif], writes=[*rb[1]])
                yield
                P.op("vector", lambda e: e.scalar_tensor_tensor(t1r[:, 0:n], zr[:, 0:n], -1.0, zr[:, 0:n], ALU.mult, ALU.max), reads=[*rb[1]], writes=[*rb[4]])
                yield
                P.op("scalar", lambda e: e.activation(t1r[:, 0:n], t1r[:, 0:n], AF.Exp, scale=-1.0), reads=[*rb[4]], writes=[*rb[4]])
                yield
                P.op("scalar", lambda e: e.activation(t1r[:, 0:n], t1r[:, 0:n], AF.Ln, bias=1.0, scale=1.0), reads=[*rb[4]], writes=[*rb[4]])
                yield
                P.op("vector", lambda e: e.tensor_scalar(lfr[:, 0:n], zr[:, 0:n], 0.0, None, ALU.min), reads=[*rb[1]], writes=[*rb[2]])
                yield
                P.op("vector", lambda e: e.tensor_tensor(lfr[:, 0:n], lfr[:, 0:n], t1r[:, 0:n], ALU.subtract), reads=[*rb[2], *rb[4]], writes=[*rb[2]])
                yield
                mcol0 = 1 + lo
                if not g["sample"]:
                    P.op("vector", lambda e: e.tensor_tensor_scan(bgr[:, 0:n], onesr[:, 0:1].to_broadcast([8, n]), lfr[:, 0:n], bcar[:, 0:1], ALU.mult, ALU.add),
                         reads=[onesr, *rb[2], bcar], writes=[*rb[3]])
                    yield
                    P.op("vector", lambda e: e.tensor_copy(bcar[:, 0:1], bgr[:, n - 1:n]), reads=[*rb[3]], writes=[bcar])
                    yield
                    P.op("vector", lambda e: e.tensor_tensor(igr[:, 0:n], igr[:, 0:n], bgr[:, 0:n], ALU.subtract), reads=[*rb[0], *rb[3]], writes=[*rb[0]])
                    yield
                    P.op("vector", lambda e: e.tensor_tensor_scan(Mrows[:, mcol0:mcol0 + n], zerosr[:, 0:1].to_broadcast([8, n]), igr[:, 0:n], Mrows[:, mcol0 - 1:mcol0], ALU.add, ALU.max),
                         reads=[zerosr, *rb[0], Mrows], writes=[Mrows])
                    yield
                else:
                    P.dma("sync", m0row[:, :, 0:1], d_m0T.unsqueeze(2), writes=[m0row])
                    P.op("vector", lambda e: e.tensor_copy(m0row[:, :, 1:8], m0row[:, :, 0:1].to_broadcast([8, 16, 7])), reads=[m0row], writes=[m0row])
                    yield
                    P.op("vector", lambda e: e.tensor_tensor_scan(bgr[:, 0:n], rst[:, 0, :], lfr[:, 0:n], 0.0, ALU.mult, ALU.add),
                         reads=[rst, *rb[2]], writes=[*rb[3]])
                    yield
                    P.op("vector", lambda e: e.tensor_tensor(igr[:, 0:n], igr[:, 0:n], bgr[:, 0:n], ALU.subtract), reads=[*rb[0], *rb[3]], writes=[*rb[0]])
                    yield
                    P.op("vector", lambda e: e.scalar_tensor_tensor(t2r[:, 0:n], rst[:, 0, :], -1e30, m0row[:].rearrange("p b r -> p (b r)"), ALU.mult, ALU.add),
                         reads=[rst, m0row], writes=[*rb[5]])
                    yield
                    P.op("vector", lambda e: e.tensor_tensor(t2r[:, 0:n], t2r[:, 0:n], igr[:, 0:n], ALU.max), reads=[*rb[5], *rb[0]], writes=[*rb[5]])
                    yield
                    P.op("vector", lambda e: e.tensor_tensor_scan(Mrows[:, mcol0:mcol0 + n], rst[:, 1, :], t2r[:, 0:n], -1e30, ALU.add, ALU.max),
                         reads=[rst, *rb[5]], writes=[Mrows])
                    yield
                P.op("vector", lambda e: e.tensor_tensor(t1r[:, 0:n], bgr[:, 0:n], Mrows[:, mcol0:mcol0 + n], ALU.add), reads=[*rb[3], Mrows], writes=[*rb[4]])
                yield
                if not g["sample"]:
                    P.op("vector", lambda e: e.tensor_copy(bcar[:, 1:2], t1r[:, n - 1:n]), reads=[*rb[4]], writes=[bcar])
                    yield
                else:
                    P.dma("sync", o_sm.unsqueeze(2), t1r[:, 0:128].rearrange("p (b r) -> p b r", b=16)[:, :, 7:8], reads=[*rb[4]], sembuf=m0row, is_output=True)
                P.op("vector", lambda e: e.tensor_scalar(t1r[:, 0:n], t1r[:, 0:n], -2.0, 80.0, ALU.mult, ALU.min), reads=[*rb[4]], writes=[*rb[4]])
                yield
                P.op("scalar", lambda e: e.activation(t1r[:, 0:n], t1r[:, 0:n], AF.Exp), reads=[*rb[4]], writes=[*rb[4]])
                yield
                if g["sample"]:
                    Mend = Mrows[:, mcol0:mcol0 + 128].rearrange("p (b r) -> p b r", b=16)[:, :, 7:8]
                    P.op("vector", lambda e: e.tensor_tensor(t2r[:, 0:128].rearrange("p (b r) -> p b r", b=16), igr[:, 0:128].rearrange("p (b r) -> p b r", b=16),
                                                              Mend.to_broadcast([8, 16, 8]), ALU.subtract), reads=[*rb[0], Mrows], writes=[*rb[5]])
                    yield
                    P.op("scalar", lambda e: e.activation(t2r[:, 0:128], t2r[:, 0:128], AF.Exp, bias=LNSCALE_AP[0:8, :], scale=1.0), reads=[*rb[5], lnsc], writes=[*rb[5]])
                    yield
                    P.op("vector", lambda e: e.tensor_tensor(zr[:, 0:16].unsqueeze(2), m0row[:, :, 0:1], Mend, ALU.subtract), reads=[m0row, Mrows], writes=[*rb[1]])
                    yield
                    P.op("scalar", lambda e: e.activation(zr[:, 0:16], zr[:, 0:16], AF.Exp), reads=[*rb[1]], writes=[*rb[1]])
                    yield
                    P.op("vector", lambda e: e.tensor_tensor(lfr[:, 0:128], m0row[:].rearrange("p b r -> p (b r)"), Mrows[:, mcol0:mcol0 + 128], ALU.subtract),
                         reads=[m0row, Mrows], writes=[*rb[2]])
                    yield
                for til in g["tiles"]:
                    o = (til * 128) - lo
                    pc = PA.get()
                    P.op("tensor", lambda e, o=o, pc=pc: e.transpose(pc[:, 0:8], igr[:, o:o + 128], identf[0:8, 0:8]), reads=[*rb[0], identf], writes=[pc], signal=False)
                    P.op("tensor", lambda e, o=o, pc=pc: e.transpose(pc[:, 8:16], t1r[:, o:o + 128], identf[0:8, 0:8]), reads=[*rb[4], identf], writes=[pc], signal=not g["sample"])
                    if g["sample"]:
                        P.op("tensor", lambda e, pc=pc: e.transpose(pc[:, 16:24], t2r[:, 0:128], identf[0:8, 0:8]), reads=[*rb[5], identf], writes=[pc], signal=True)
                        P.op("vector", lambda e, pc=pc: e.tensor_copy(gcols[:], pc[:, 16:24]), reads=[pc], writes=[gcols])
                        pd = PA.get()
                        for h in range(8):
                            P.op("tensor", lambda e, h=h, pd=pd: e.matmul(pd[:, h * 16:(h + 1) * 16], lhsT=sel[:, h, :], rhs=zr[:, 0:16], start=True, stop=True),
                                 reads=[sel, *rb[1]], writes=[pd], signal=(h == 7))
                        P.op("vector", lambda e, pd=pd: e.tensor_copy(decb[:].rearrange("p h b -> p (h b)"), pd[:, 0:128]), reads=[pd], writes=[decb])
                    P.op("vector", lambda e, til=til, pc=pc: e.tensor_scalar(acols[:, til, 0:8], pc[:, 0:8], LNSCALE, None, ALU.add), reads=[pc], writes=[acols])
                    P.op("vector", lambda e, til=til, pc=pc: e.tensor_copy(acols[:, til, 8:16], pc[:, 8:16]), reads=[pc], writes=[acols])
                    yield
                if g["sample"]:
                    P.op("vector", lambda e: e.tensor_copy(dMs[:, 0:128], lfr[:, 0:128]), reads=[*rb[2]], writes=[dMsb])

            yield
        gates = gates_gen()
        pend_q = []
        for q in range(4):
            s = ring_load([(0, 4096, d_wB_in[q].rearrange("p a b -> p (a b)"))])
            sv = s[:, 0:4096].rearrange("p (a b) -> p a b", a=8)
            for cc in range(4):
                c = q * 4 + cc
                wc = [bcw[:, i, c:c + 1] for i in range(4)]
                cbb = bcb[:, c:c + 1]
                while pend_q:
                    pend_q.pop(0)()
                dg = []
                for i in range(4):
                    d_ = SA.get()
                    P.op("vector", lambda e: e.tensor_scalar(d_[:, 0:128], identb[:], wc[i], None, ALU.mult), reads=[identb, bcw], writes=[d_])
                    dg.append(d_)
                for g in groups:
                    lo, n = g["lo"], g["n"]
                    ps = PA.get()
                    for k in range(8):
                        P.op("tensor", lambda e, k=k, ps=ps: e.matmul(ps[:, 0:n], lhsT=sv[:, k, cc * 128:(cc + 1) * 128],
                                                                      rhs=hT[:, k, lo:lo + n], start=(k == 0), stop=(k == 7)),
                             reads=[s, hg[g["gi"]]], writes=[ps], signal=(k == 7))
                    if not g["sample"]:
                        xb = S16.get()
                        P.op("scalar", lambda e: e.activation(xb[:, 0:n], ps[:, 0:n], AF.Copy), reads=[ps], writes=[xb])
                        t3 = SC.get()
                        P.op("scalar", lambda e: e.activation(t3[:, 0:3], ps[:, 0:3], AF.Identity, bias=cbb, scale=wc[3]),
                             reads=[ps, bcw, bcb], writes=[t3])
                        P.op("vector", lambda e: e.scalar_tensor_tensor(t3[:, 1:3], ps[:, 0:2], wc[2], t3[:, 1:3], ALU.mult, ALU.add),
                             reads=[ps, t3, bcw], writes=[t3])
                        P.op("vector", lambda e: e.scalar_tensor_tensor(t3[:, 2:3], ps[:, 0:1], wc[1], t3[:, 2:3], ALU.mult, ALU.add),
                             reads=[ps, t3, bcw], writes=[t3])
                        for i in range(3):
                            P.op("vector", lambda e: e.scalar_tensor_tensor(t3[:, 0:3 - i], qtail[:, c, i:3], wc[i], t3[:, 0:3 - i], ALU.mult, ALU.add),
                                 reads=[qtail, t3, bcw], writes=[t3])
                        P.op("scalar", lambda e: e.activation(qtail[:, c, :], ps[:, n - 3:n], AF.Copy), reads=[ps], writes=[qtail])
                        P.op("scalar", lambda e: e.activation(bigt[:, c, lo:lo + 3], t3[:, 0:3], AF.Silu), reads=[t3], writes=SLC(c, lo, 3))

                        def finq(xb=xb, c=c, lo=lo, n=n, dg=dg, cbb=cbb):
                            pc = PA.get()
                            for i in range(4):
                                P.op("tensor", lambda e: e.matmul(pc[:, 3:n], lhsT=dg[i][:, 0:128], rhs=xb[:, i:i + n - 3], start=(i == 0), stop=(i == 3)),
                                     reads=[dg[i], xb], writes=[pc], signal=(i == 3))
                            P.op("scalar", lambda e: e.activation(bigt[:, c, lo + 3:lo + n], pc[:, 3:n], AF.Silu, bias=cbb, scale=1.0),
                                 reads=[pc, bcb], writes=SLC(c, lo, n))
                    else:
                        t0 = S32.get()
                        P.op("scalar", lambda e: e.activation(t0[:, 0:n], ps[:, 0:n], AF.Identity, bias=cbb, scale=wc[3]),
                             reads=[ps, bcw, bcb], writes=[t0])
                        hs = HS.get(); os_ = HS.get()
                        hs3 = hs[:, 0:48].rearrange("p (b r) -> p b r", b=16)
                        os3 = os_[:, 0:48].rearrange("p (b r) -> p b r", b=16)
                        ps3 = ps[:, 0:128].rearrange("p (b r) -> p b r", b=16)
                        t03 = t0[:, 0:128].rearrange("p (b r) -> p b r", b=16)
                        P.dma("sync", hs3, d_cvs[:, c, :, :], writes=[hs])
                        for i in range(3):
                            sh = 3 - i
                            P.op("vector", lambda e: e.scalar_tensor_tensor(t03[:, :, sh:8], ps3[:, :, 0:8 - sh], wc[i], t03[:, :, sh:8], ALU.mult, ALU.add),
                                 reads=[ps, t0, bcw], writes=[t0])
                        for i in range(3):
                            P.op("vector", lambda e: e.scalar_tensor_tensor(t03[:, :, 0:3 - i], hs3[:, :, i:3], wc[i], t03[:, :, 0:3 - i], ALU.mult, ALU.add),
                                 reads=[hs, t0, bcw], writes=[t0])
                        P.op("scalar", lambda e: e.activation(os3, ps3[:, :, 5:8], AF.Copy), reads=[ps], writes=[os_])
                        P.dma("sync", o_sconv[:, c, :, :], os3, reads=[os_], sembuf=os_, is_output=True)

                        def finq(t0=t0, c=c, lo=lo, n=n):
                            P.op("scalar", lambda e: e.activation(bigt[:, c, lo:lo + n], t0[:, 0:n], AF.Silu), reads=[t0], writes=SLC(c, lo, n))
                    pend_q.append(finq)
                    if len(pend_q) > 1:
                        pend_q.pop(0)()
                    for _ in range(3):
                        next(gates, None)
        while pend_q:
            pend_q.pop(0)()
        if last_pass:
            P.dma("sync", o_pconv, qtail[:], reads=[qtail], sembuf=qtail, is_output=True)
        P.mark('B.gates')
        for _ in gates:
            pass
        if last_pass:
            P.dma("sync", o_pm, bcar[:, 1:2], reads=[bcar], sembuf=bcar, is_output=True)
        if has_sample:
            nin = S32.get()
            P.dma("sync", nin[:, 0:128], d_nst, writes=[nin])
            pn_ = PA.get()
            P.op("tensor", lambda e: e.transpose(pn_[:, 0:128], nin[:, 0:128], identf[:]), reads=[nin, identf], writes=[pn_])
            P.op("vector", lambda e: e.tensor_copy(nsT[:], pn_[:, 0:128]), reads=[pn_], writes=[nsT])
        P.mark('B.rec')
        Cs16 = bigt[:, 16:18, :].rearrange("p a b -> p (a b)")[:, 0:2056].rearrange("p (b e) -> p b e", b=8)
        Cs16b = SLW(16) + SLW(17)
        Vbd = bigt[:, 18:20, :].rearrange("p a b -> p (a b)")[:, 0:2056].rearrange("p (b e) -> p b e", b=8)
        Vbdb = SLW(18) + SLW(19)
        Bigq = bigt[:, 20:22, :].rearrange("p a b -> p (a b)")[:, 0:1984].rearrange("p (b e) -> p b e", b=8)
        Bigb = SLW(20) + SLW(21)
        if has_sample:
            P.op("vector", lambda e: e.memset(bigt[:, 20:22, :], 0.0), writes=Bigb)
        NT = len(tiles)
        PN = Pool(PA.bufs[4:6])
        PP = Pool(PA.bufs[0:2])
        PS = Pool(PA.bufs[2:4])

        def proj(pr, ti, sv_, so_):
            lo = ti * 128
            gi = lo // 512
            svv = sv_[:, 0:4096].rearrange("p (a b) -> p a b", a=8)
            sov = so_[:, 0:4096].rearrange("p (a b) -> p a b", a=8)
            psv = PP.get()
            for k in range(8):
                P.op("tensor", lambda e: e.matmul(psv[:, 0:512], lhsT=hT[:, k, lo:lo + 128], rhs=svv[:, k, :], start=(k == 0), stop=(k == 7)),
                     reads=[sv_, hg[gi]], writes=[psv], signal=(k == 7))
            va = VA.get()
            va3 = va[:, 0:514].rearrange("p (h e) -> p h e", h=2)
            P.op("scalar", lambda e: e.activation(va3[:, :, 0:256], psv[:, 0:512].rearrange("p (h e) -> p h e", h=2), AF.Copy), reads=[psv], writes=[va])
            P.op("vector", lambda e: e.memset(va3[:, :, 256:257], 1.0), writes=[va])
            pso = PP.get()
            for k in range(8):
                P.op("tensor", lambda e: e.matmul(pso[:, 0:512], lhsT=hT[:, k, lo:lo + 128], rhs=sov[:, k, :], start=(k == 0), stop=(k == 7)),
                     reads=[so_, hg[gi]], writes=[pso], signal=(k == 7))
            og = OG.get()
            def sig():
                P.op("scalar", lambda e: e.activation(og[:, 0:512], pso[:, 0:512], AF.Exp, scale=-1.0), reads=[pso], writes=[og])
                P.op("scalar", lambda e: e.activation(og[:, 0:512], og[:, 0:512], AF.Ln, bias=1.0, scale=1.0), reads=[og], writes=[og])
                P.op("scalar", lambda e: e.activation(og[:, 0:512], og[:, 0:512], AF.Exp, scale=-1.0), reads=[og], writes=[og])
            return va, va3, og, sig

        def stageA(cx):
            ti, h, hh, is_s = cx["ti"], cx["h"], cx["hh"], cx["is_s"]
            lo = ti * 128
            qs, ks = h, 8 + h
            pst = PS.get()
            pmb = pst
            pbk = PB.bufs[cx["idx"] % 2]
            cx["pbk"] = pbk
            P.op("tensor", lambda e: e.matmul(pst[:, 0:128], lhsT=bigt[:, ks, lo:lo + 128], rhs=bigt[:, qs, lo:lo + 128], start=True, stop=True),
                 reads=[*SLC(ks, lo, 128), *SLC(qs, lo, 128)], writes=[pst])
            mc = 1 + lo
            P.op("tensor", lambda e: e.matmul(pmb[:, 128:256], lhsT=sel[:, h, :], rhs=Mrows[:, mc:mc + 128], start=True, stop=False),
                 reads=[sel, Mrows], writes=[pmb], signal=False)
            P.op("tensor", lambda e: e.matmul(pmb[:, 128:256], lhsT=identb[:], rhs=(bdp if is_s else trip)[:], start=False, stop=True),
                 reads=[identb, bdp, trip], writes=[pmb], signal=False)
            if not is_s:
                P.op("tensor", lambda e: e.matmul(pmb[:, 256:384], lhsT=sel[:, h, :], rhs=Mrows[:, mc:mc + 128], start=True, stop=True),
                     reads=[sel, Mrows], writes=[pmb], signal=True)
            else:
                P.op("tensor", lambda e: e.matmul(pmb[:, 256:384], lhsT=sel[:, h, :], rhs=dMs[:, 0:128], start=True, stop=True),
                     reads=[sel, dMsb], writes=[pmb], signal=True)
            ptk = pbk
            P.op("tensor", lambda e: e.transpose(ptk[:, 0:128], bigt[:, ks, lo:lo + 128], identb[:]), reads=[*SLC(ks, lo, 128), identb], writes=[ptk])
            wTt = WT.get()
            P.op("scalar", lambda e: e.activation(wTt[:, 0:128], pmb[:, 128:256], AF.Exp, bias=acols[:, ti, h:h + 1], scale=-1.0),
                 reads=[pmb, acols], writes=[wTt])
            if not is_s:
                P.op("scalar", lambda e: e.activation(wTt[:, 128:256], pmb[:, 256:384], AF.Exp, bias=Mpcol[:, h:h + 1], scale=-1.0),
                     reads=[pmb, Mpcol], writes=[wTt])
                P.op("vector", lambda e: e.tensor_copy(Mpcol[:, h:h + 1], pmb[:, 383:384]), reads=[pmb], writes=[Mpcol])
            else:
                P.op("scalar", lambda e: e.activation(wTt[:, 128:256], pmb[:, 256:384], AF.Exp), reads=[pmb], writes=[wTt])
            scT = SA.get()
            P.op("vector", lambda e: e.tensor_tensor(scT[:, 0:128], pst[:, 0:128], wTt[:, 0:128], ALU.mult),
                 reads=[pst, wTt], writes=[scT])
            kg = SA.get()
            gsc = gcols[:, h:h + 1] if is_s else wTt[:, 127:128]
            P.op("scalar", lambda e: e.activation(kg[:, 0:128], ptk[:, 0:128], AF.Identity, scale=gsc),
                 reads=[ptk, wTt, gcols], writes=[kg])
            cx.update(wTt=wTt, scT=scT, kg=kg)
            if not is_s:
                cb16 = CB.get()
                P.op("scalar", lambda e: e.activation(cb16[:, 0:257], C32[:, h, :], AF.Copy), reads=[C32h[h]], writes=[cb16])
                cx.update(cb16=cb16)
                qp = SA.get()
                P.op("vector", lambda e: e.tensor_tensor(qp[:, 0:128], bigt[:, qs, lo:lo + 128], wTt[:, 128:256], ALU.mult),
                     reads=[*SLC(qs, lo, 128), wTt], writes=[qp])
                cx.update(qp=qp)

        def stageB(cx):
            ti, h, hh, is_s = cx["ti"], cx["h"], cx["hh"], cx["is_s"]
            lo = ti * 128
            qs, ks = h, 8 + h
            va, va3, og = cx["va"], cx["va3"], cx["og"]
            vah = va3[:, hh, :]
            wTt, scT, kg = cx["wTt"], cx["scT"], cx["kg"]
            pbk = cx["pbk"]
            pdcv = pbk[:, 384:898].bitcast(F32)
            if not is_s:
                P.op("tensor", lambda e: e.matmul(pdcv, lhsT=kg[:, 0:128], rhs=vah, start=True, stop=True),
                     reads=[kg, va], writes=[pbk])
                P.op("vector", lambda e: e.scalar_tensor_tensor(C32[:, h, :], C32[:, h, :], wTt[:, 255:256], pdcv, ALU.mult, ALU.add),
                     reads=[pbk, wTt, C32h[h]], writes=[C32h[h]])
            pnum = PN.get()
            if not is_s:
                qp = cx["qp"]
                cb16 = cx["cb16"]
                for kk in range(NFILL):
                    P.op("tensor", lambda e: e.matmul(pnum[:, 0:257], lhsT=zerob[:], rhs=vah, start=(kk == 0), stop=False),
                         reads=[zerob, va], writes=[pnum], signal=False)
                P.op("tensor", lambda e: e.matmul(pnum[:, 0:257], lhsT=qp[:, 0:128], rhs=cb16[:, 0:257], start=(NFILL == 0), stop=False),
                     reads=[qp, cb16], writes=[pnum], signal=False)
                P.op("tensor", lambda e: e.matmul(pnum[:, 0:257], lhsT=scT[:, 0:128], rhs=vah, start=False, stop=True),
                     reads=[scT, va], writes=[pnum], signal=True)
            else:
                for half in range(2):
                    P.op("vector", lambda e: e.tensor_tensor(
                        Bigq[:, :, 120:128], bigt[:, qs, lo + half * 64:lo + half * 64 + 64].rearrange("p (b r) -> p b r", b=8),
                        wTt[:, 128 + half * 64:128 + half * 64 + 64].rearrange("p (b r) -> p b r", b=8), ALU.mult),
                        reads=[*SLC(qs, lo, 128), wTt], writes=Bigb)
                    cs = U16[half]
                    cs3 = cs[:, 0:2056].rearrange("p (b e) -> p b e", b=8)
                    P.dma("sync", cs3[:, :, 0:256], d_Cst[half * 8:(half + 1) * 8, h, :, :].rearrange("b p e -> p b e"), writes=[cs])
                    P.op("vector", lambda e: e.tensor_copy(
                        cs3[:, :, 256:257], nsT[:].rearrange("p (b h) -> p b h", h=8)[:, half * 8:(half + 1) * 8, h:h + 1]),
                        reads=[nsT], writes=[cs])
                    P.op("scalar", lambda e: e.activation(Cs16.rearrange("p b e -> p (b e)"), cs[:, 0:2056], AF.Copy), reads=[cs], writes=Cs16b)
                    for b8 in range(8):
                        bb = half * 8 + b8
                        P.op("tensor", lambda e: e.matmul(
                            pnum[:, 0:257], lhsT=win_ap(Bigq, b8, bb), rhs=Cs16[:, b8, :],
                            start=(bb == 0), stop=False), reads=Bigb + Cs16b, writes=[pnum], signal=(b8 == 7))
                P.op("tensor", lambda e: e.matmul(pnum[:, 0:257], lhsT=scT[:, 0:128], rhs=vah, start=False, stop=True),
                     reads=[scT, va], writes=[pnum], signal=True)
            sc = SC.get()
            st6 = SC.get()
            P.op("scalar", lambda e: e.activation(sc[:, 0:1], pnum[:, 256:257], AF.Square), reads=[pnum], writes=[sc])
            P.op("vector", lambda e: e.bn_stats(st6[:, 0:6], pnum[:, 0:256]), reads=[pnum], writes=[st6])
            P.op("vector", lambda e: e.bn_aggr(st6[:, 6:8], st6[:, 0:6]), reads=[st6], writes=[st6])
            P.op("vector", lambda e: e.tensor_tensor(sc[:, 1:2], sc[:, 0:1], acols[:, ti, 8 + h:9 + h], ALU.max), reads=[sc, acols], writes=[sc])
            P.op("vector", lambda e: e.scalar_tensor_tensor(sc[:, 1:2], sc[:, 1:2], EPS, st6[:, 7:8], ALU.mult, ALU.add), reads=[sc, st6], writes=[sc])
            if is_s:
                for half in range(2):
                    cs = U16[half]
                    cs3 = cs[:, 0:2056].rearrange("p (b e) -> p b e", b=8)
                    P.op("vector", lambda e: e.tensor_tensor(Vbd, vah.unsqueeze(1).to_broadcast([128, 8, 257]),
                                                             bmask[:, half * 8:(half + 1) * 8].unsqueeze(2).to_broadcast([128, 8, 257]), ALU.mult),
                         reads=[va, bmask], writes=Vbdb)
                    for b8 in range(8):
                        bb = half * 8 + b8
                        pdx = PS.get()
                        P.op("tensor", lambda e: e.matmul(pdx[:, 0:257], lhsT=kg[:, 0:128], rhs=Vbd[:, b8, :], start=True, stop=True),
                             reads=[kg] + Vbdb, writes=[pdx])
                        P.op("vector", lambda e: e.scalar_tensor_tensor(cs3[:, b8, :], cs3[:, b8, :], decb[:, h, bb:bb + 1], pdx[:, 0:257], ALU.mult, ALU.add),
                             reads=[pdx, decb, cs], writes=[cs])
                    P.dma("sync", o_sC[half * 8:(half + 1) * 8, h, :, :].rearrange("b p e -> p b e"), cs3[:, :, 0:256], reads=[cs], sembuf=cs, is_output=True)
                    P.op("vector", lambda e: e.tensor_copy(
                        nsT[:].rearrange("p (b h) -> p b h", h=8)[:, half * 8:(half + 1) * 8, h:h + 1], cs3[:, :, 256:257]),
                        reads=[cs], writes=[nsT])
            cx.update(sc=sc, st6=st6, pnum=pnum)

        def stageB2(cx):
            sc = cx["sc"]
            rsqrt_act(sc[:, 2:3], sc[:, 1:2], 1.0, 0.0, [sc], [sc])

        def stageB3(cx):
            ti, h, hh, is_s = cx["ti"], cx["h"], cx["hh"], cx["is_s"]
            va, va3, og = cx["va"], cx["va3"], cx["og"]
            vah = va3[:, hh, :]
            wTt, scT, kg = cx["wTt"], cx["scT"], cx["kg"]
            sc, st6, pnum = cx["sc"], cx["st6"], cx["pnum"]
            hn = S32.get()
            P.op("vector", lambda e: e.scalar_tensor_tensor(hn[:, 0:256], pnum[:, 0:256], st6[:, 6:7], og[:, hh * 256:(hh + 1) * 256], ALU.subtract, ALU.mult),
                 reads=[pnum, st6, og], writes=[hn])
            gt = GT.get()
            P.op("scalar", lambda e: e.activation(gt[:, 0:256], hn[:, 0:256], AF.Identity, scale=sc[:, 2:3]),
                 reads=[hn, sc], writes=[gt])
            cx.update(gt=gt)

        def stageC(cx):
            ti, h = cx["ti"], cx["h"]
            lo = ti * 128
            qs, ks = h, 8 + h
            gt = cx["gt"]
            ptg = cx["pbk"]
            for e2 in range(2):
                P.op("tensor", lambda e: e.transpose(ptg[:, 128 + e2 * 128:128 + (e2 + 1) * 128], gt[:, e2 * 128:(e2 + 1) * 128], identb[:]),
                     reads=[gt, identb], writes=[ptg], signal=(e2 == 1))
            for e2 in range(2):
                sl = qs if e2 == 0 else ks
                P.op("vector", lambda e: e.tensor_scalar(bigt[:, sl, lo:lo + 128], ptg[:, 128 + e2 * 128:128 + (e2 + 1) * 128],
                                                         gncol[:, 2 * h + e2:2 * h + e2 + 1], None, ALU.mult),
                     reads=[ptg, gncol], writes=SLC(sl, lo, 128))

        for pr in range(4):
            sv_ = ring_load([(0, 4096, d_wB_in[4 + pr].rearrange("p a b -> p (a b)"))])
            so_ = ring_load([(0, 4096, d_wB_in[8 + pr].rearrange("p a b -> p (a b)"))])
            items = []
            for ti in range(NT):
                is_s = has_sample and ti == NT - 1
                for hh in range(2):
                    items.append(dict(ti=ti, h=pr * 2 + hh, hh=hh, is_s=is_s))
            NI = len(items)
            pj = {}
            for i in range(NI + 3):
                if i < NI:
                    cx = items[i]
                    cx["idx"] = i
                    if cx["hh"] == 0:
                        pj[cx["ti"]] = proj(pr, cx["ti"], sv_, so_)
                    cx["va"], cx["va3"], cx["og"], sigf = pj[cx["ti"]]
                    stageA(cx)
                    if cx["hh"] == 1:
                        sigf()
                if 0 <= i - 1 < NI:
                    stageB(items[i - 1])
                if 0 <= i - 2 < NI:
                    stageB2(items[i - 2])
                    stageB3(items[i - 2])
                if 0 <= i - 3 < NI:
                    stageC(items[i - 3])
        if last_pass:
            P.dma("sync", o_pC.rearrange("h p e -> p h e"), C32[:, :, 0:256], reads=C32h, sembuf=C32h[0], is_output=True)
            P.dma("sync", o_pn, C32[:, :, 256], reads=C32h, sembuf=C32h[0], is_output=True)
        if has_sample:
            pn2 = PA.get()
            P.op("tensor", lambda e: e.transpose(pn2[:, 0:128], nsT[:], identf[:]), reads=[nsT, identf], writes=[pn2])
            nout = S32.get()
            P.op("vector", lambda e: e.tensor_copy(nout[:, 0:128], pn2[:, 0:128]), reads=[pn2], writes=[nout])
            P.dma("sync", o_sn, nout[:, 0:128], reads=[nout], sembuf=nout, is_output=True)
        P.mark('B.out')
        kslots = []
        for e_ in range(16):
            kslots.append(e_ // 2 if e_ % 2 == 0 else 8 + e_ // 2)
        out_proj(d_wB_out, groups, kslots)

    lnsc = P.sbuf("lnsc", [128, 1], F32)
    LNSCALE_AP = lnsc
    P.op("vector", lambda e: e.memset(lnsc[:], LNSCALE), writes=[lnsc])
    dMs_t = P.sbuf("dMs", [8, 128], F32)
    dMs = dMs_t
    dMsb = dMs_t
    cur_tiles = [None]

    def til_idx(ti):
        return ti

    def win_ap(Bq, b8, bb):
        off = 120 - 8 * bb
        return Bq[:, b8, off:off + 128]

    passes = [dict(ptiles=list(range(0, 8)), sample=False), dict(ptiles=list(range(8, 16)), sample=True)]
    for pi, ps_ in enumerate(passes):
        ntiles = len(ps_["ptiles"]) + (1 if ps_["sample"] else 0)
        groups = groups_of(ntiles, ps_["sample"])
        tiles = list(range(ntiles))
        first, last = pi == 0, pi == len(passes) - 1
        p0 = ps_["ptiles"][0] * 128
        npc = len(ps_["ptiles"]) * 128
        for g in groups:
            lo, n = g["lo"], g["n"]
            if g["sample"]:
                P.dma("sync", xT[:, :, lo:lo + n], d_xTs, writes=[xg[g["gi"]]])
            else:
                P.dma("sync", xT[:, :, lo:lo + n], d_xTp[:, :, p0 + lo:p0 + lo + n], writes=[xg[g["gi"]]])
        step = 0
        if step < stop_after:
            mixer_A(groups, tiles, ps_["sample"])
        step += 1
        if step < stop_after:
            ffn(0, groups, last)
        step += 1
        if step < stop_after:
            mixer_B(groups, tiles, ps_["sample"], first, last)
        step += 1
        if step < stop_after:
            ffn(1, groups, last)
        P.mark('final')
        def dst(g, c, xin, gc, rs, rstd):
            lo, n = g["lo"], g["n"]
            yt = S32.get()
            P.op("vector", lambda e, yt=yt: e.scalar_tensor_tensor(yt[:, 0:n], xin, gc, rs, ALU.mult, ALU.mult),
                 reads=[xg[g["gi"]], rstd, gcol], writes=[yt])
            if g["sample"]:
                P.dma("sync", o_yTs[:, c, :], yt[:, 0:n], reads=[yt], sembuf=yt, is_output=True)
            else:
                P.dma("sync", o_yTp[:, c, p0 + lo:p0 + lo + n], yt[:, 0:n], reads=[yt], sembuf=yt, is_output=True)
        rmsnorm_to(4, groups, dst)

    P.mark('end')
    P.finish("sync")
    if os.environ.get('MK_MARKS'):
        import json
        json.dump(P.marks, open(os.environ['MK_MARKS'], 'w'))
    P.emit()
    P.close()
    print("instr counts", {e: len(P.rec[e]) for e in ENGS}, "sems", P.nsem)
    return nc


def _pk(w, ncols):
    K, N = w.shape
    kc = K // 128
    return np.ascontiguousarray(w.reshape(kc, 128, N // ncols, ncols).transpose(2, 1, 0, 3))


def _consts():
    bf = ml_dtypes.bfloat16
    s = np.arange(128)[:, None]
    t = np.arange(128)[None, :]
    c = {}
    c["identb"] = np.eye(128, dtype=np.float32).astype(bf)
    c["identf"] = np.eye(128, dtype=np.float32)
    c["onesb"] = np.ones((128, 128), np.float32).astype(bf)
    c["trip"] = np.where(s <= t, 0.0, BIG).astype(np.float32).astype(bf)
    c["bdp"] = np.where((s <= t) & (s // 8 == t // 8), 0.0, BIG).astype(np.float32).astype(bf)
    c["tri01"] = (s >= t).astype(np.float32)
    c["bmask"] = (np.arange(128)[:, None] // 8 == np.arange(16)[None, :]).astype(np.float32).astype(bf)
    sel = np.zeros((8, 8, 128), np.float32)
    for h in range(8):
        sel[h, h, :] = 1.0
    c["sel"] = sel
    rst = np.zeros((8, 2, 128), np.float32)
    start = (np.arange(128) % 8 == 0)
    rst[:, 0, :] = np.where(start, 0.0, 1.0)
    rst[:, 1, :] = np.where(start, -1e30, 0.0)
    c["rst"] = rst
    return c


_NC_CACHE = {}


def kernel(**inputs):
    f32 = np.float32
    inp = {k: np.asarray(v) for k, v in inputs.items()}
    stop_after = int(os.environ.get("MK_STOP", "99"))
    if stop_after not in _NC_CACHE:
        _NC_CACHE[stop_after] = build_program(stop_after)
    nc = _NC_CACHE[stop_after]

    shared = {}
    a_w_in = inp["a_w_in"][0]
    shared["wA_in"] = _pk(a_w_in, 512)
    shared["wA_out"] = _pk(inp["a_w_out"][0], 256)
    wfu = []
    for l in range(2):
        w = inp["f_w_up"][l]
        a = w[:, :D_FF].reshape(8, 128, 11, 256)
        g = w[:, D_FF:].reshape(8, 128, 11, 256)
        wfu.append(np.concatenate([a, g], axis=3).transpose(2, 1, 0, 3))
    shared["wF_up"] = np.ascontiguousarray(np.stack(wfu))
    shared["wF_dn"] = np.ascontiguousarray(np.stack([_pk(inp["f_w_down"][l], 128) for l in range(2)]))
    b_w_in = inp["b_w_in"][0]
    shared["wB_in"] = _pk(b_w_in[:, :6144], 512)
    shared["wB_g"] = np.ascontiguousarray(b_w_in[:, 6144:6160].reshape(8, 128, 16).transpose(1, 0, 2))
    shared["wB_out"] = _pk(inp["b_w_out"][0], 256)
    gall = np.stack([inp["norm_mix_g"][0], inp["norm_mix_g"][1], inp["norm_ffn_g"][0], inp["norm_ffn_g"][1], inp["final_norm_g"]])
    shared["gcol"] = np.ascontiguousarray(gall.reshape(5, 8, 128).transpose(2, 0, 1))
    shared["fcw"] = np.ascontiguousarray(inp["f_conv_w"].reshape(2, 3, NJ, 128).transpose(3, 0, 1, 2))
    shared["fcb"] = np.ascontiguousarray(inp["f_conv_b"].reshape(2, NJ, 128).transpose(2, 0, 1))
    shared["bcw"] = np.ascontiguousarray(inp["b_conv_w"][0].reshape(4, 16, 128).transpose(2, 0, 1))
    shared["bcb"] = np.ascontiguousarray(inp["b_conv_b"][0].reshape(16, 128).transpose(1, 0))
    shared["bif"] = np.ascontiguousarray(np.stack([inp["b_bias_i"][0], inp["b_bias_f"][0]], axis=1))
    shared["gncol"] = np.ascontiguousarray(inp["b_gn_g"][0].reshape(16, 128).transpose(1, 0))
    shared["lng"] = np.ascontiguousarray(inp["a_ln_g"][0:1])
    shared["lnb"] = np.ascontiguousarray(inp["a_ln_b"][0:1])
    ws = inp["a_w_s"][0]
    shared["ws"] = np.ascontiguousarray(ws.transpose(1, 0, 2))
    wsbd = np.zeros((8, 128, 128), f32)
    for b in range(16):
        wsbd[:, 8 * b:8 * b + 8, 8 * b:8 * b + 8] = ws[:, :8, :8]
    shared["wsbd"] = np.ascontiguousarray(wsbd.transpose(1, 0, 2))
    bs = inp["a_b_s"][0]
    shared["bs8"] = np.ascontiguousarray(bs)
    shared["bs8s"] = np.ascontiguousarray(np.tile(bs[:, :8], (1, 16)))
    shared.update(_consts())
    shared = {k: (v if v.dtype != np.float64 else v.astype(f32)) for k, v in shared.items()}

    in_maps = []
    for c in range(8):
        m = dict(shared)
        m["xTp"] = np.ascontiguousarray(inp["x_prompt"][c].T.reshape(8, 128, 2048).transpose(1, 0, 2))
        xs = inp["x_sample"][16 * c:16 * c + 16].reshape(128, 1024)
        m["xTs"] = np.ascontiguousarray(xs.T.reshape(8, 128, 128).transpose(1, 0, 2))
        m["Cst"] = np.ascontiguousarray(inp["state_mlstm_C"][0, 16 * c:16 * c + 16])
        m["nst"] = np.ascontiguousarray(inp["state_mlstm_n"][0, 16 * c:16 * c + 16].reshape(128, 128))
        m["m0T"] = np.ascontiguousarray(inp["state_mlstm_m"][0, 16 * c:16 * c + 16].T)
        cv = inp["state_mlstm_conv"][0, 16 * c:16 * c + 16]
        m["cvs"] = np.ascontiguousarray(cv.reshape(16, 3, 16, 128).transpose(3, 2, 0, 1))
        ff = inp["state_ffn_conv"][:, 16 * c:16 * c + 16]
        m["ffs"] = np.ascontiguousarray(ff.reshape(2, 16, 2, NJ, 128).transpose(4, 0, 3, 1, 2))
        in_maps.append(m)

    res = run_bass_kernel_spmd(nc, in_maps, core_ids=list(range(8)))
    R = res.results

    y_prompt = np.stack([R[c]["yTp"].transpose(1, 0, 2).reshape(1024, 2048).T for c in range(8)]).astype(f32)
    y_sample = np.concatenate([R[c]["yTs"].transpose(1, 0, 2).reshape(1024, 128).T.reshape(16, 8, 1024) for c in range(8)]).astype(f32)
    pC = np.stack([R[c]["pC"] for c in range(8)])[None].astype(f32)
    pn = np.stack([R[c]["pn"].T for c in range(8)])[None].astype(f32)
    pm = np.stack([R[c]["pm"][:, 0] for c in range(8)])[None].astype(f32)
    pconv = np.stack([R[c]["pconv"].transpose(2, 1, 0).reshape(3, 2048) for c in range(8)])[None].astype(f32)
    pffn = np.stack([R[c]["pffn"].transpose(1, 3, 2, 0).reshape(2, 2, D_FF) for c in range(8)], axis=1).astype(f32)
    sv = np.concatenate([R[c]["sv"].reshape(16, 8, 2048) for c in range(8)])[None].astype(f32)
    sC = np.concatenate([R[c]["sC"] for c in range(8)])[None].astype(f32)
    sn = np.concatenate([R[c]["sn"].reshape(16, 8, 128) for c in range(8)])[None].astype(f32)
    sm = np.concatenate([R[c]["sm"].T for c in range(8)])[None].astype(f32)
    sconv = np.concatenate([R[c]["sconv"].transpose(2, 3, 1, 0).reshape(16, 3, 2048) for c in range(8)])[None].astype(f32)
    sffn = np.concatenate([R[c]["sffn"].transpose(1, 3, 4, 2, 0).reshape(2, 16, 2, D_FF) for c in range(8)], axis=1).astype(f32)
    return (y_prompt, y_sample, pC, pn, pm, pconv, pffn, sv, sC, sn, sm, sconv, sffn)
```

```python
import os
import types
from contextlib import ExitStack
import numpy as np
import ml_dtypes
import concourse.bass as bass
import concourse.mybir as mybir
from concourse.bass_utils import run_bass_kernel_spmd

F32 = mybir.dt.float32
BF16 = mybir.dt.bfloat16
ALU = mybir.AluOpType
AF = mybir.ActivationFunctionType

ENGS = ("tensor", "vector", "scalar", "gpsimd", "sync")
EPS = 1e-6
LNSCALE = -0.5 * float(np.log(128.0))
NFILL = 10
BIG = 30000.0
D_FF = 2816
NJ = 22
TW = 1152


def _freeze(fn):
    if fn.__closure__ is None:
        return fn
    cells = tuple(types.CellType(c.cell_contents) for c in fn.__closure__)
    return types.FunctionType(fn.__code__, fn.__globals__, fn.__name__, fn.__defaults__, cells)


class Buf:
    __slots__ = ("name", "t", "last_write", "reads", "dsem", "dcount", "excl")

    def __init__(self, name, t=None, excl=False):
        self.name = name
        self.t = t
        self.excl = excl
        self.last_write = None
        self.reads = []
        self.dsem = None
        self.dcount = 0

    def __getitem__(self, idx):
        return self.t[idx]


class Prog:
    def __init__(self, nc):
        self.nc = nc
        self.es = ExitStack()
        self.rec = {e: [] for e in ENGS}
        self.count = {e: 0 for e in ENGS}
        self.waited = {e: {} for e in ENGS}
        self.esem = {}
        self.sems = {}
        self.nsem = 0
        for e in ENGS:
            self.esem[e] = self.new_sem("done_" + e)
        self.owner = {v: k for k, v in self.esem.items()}
        self.out_events = []
        self.marks = []

    def new_sem(self, name):
        s = self.es.enter_context(self.nc.semaphore(name))
        self.nsem += 1
        self.sems[name] = s
        return name

    def sbuf(self, name, shape, dtype):
        t = self.es.enter_context(self.nc.sbuf_tensor("sb_" + name, list(shape), dtype))
        return Buf(name, t)

    def psum(self, name, shape, dtype=F32):
        t = self.es.enter_context(self.nc.psum_tensor("ps_" + name, list(shape), dtype))
        return Buf(name, t, excl=True)

    def _waits(self, eng, reads, writes, dsem=None):
        w = {}

        def need(ev, same_ok):
            if ev is None:
                return
            sk, val = ev
            if same_ok and sk == self.esem[eng]:
                return
            if val > w.get(sk, 0):
                w[sk] = val
        for b in reads:
            need(b.last_write, False)
            if b.excl:
                for ev in b.reads:
                    need(ev, True)
        for b in writes:
            lw = b.last_write
            if not (lw is not None and dsem is not None and lw[0] == dsem):
                need(lw, True)
            for ev in b.reads:
                need(ev, True)
        out = []
        for sk, val in w.items():
            if self.waited[eng].get(sk, 0) >= val:
                continue
            if sk in self.owner and val > self.count[self.owner[sk]]:
                raise RuntimeError(f"{eng} waits on unsignaled op of {self.owner[sk]}")
            self.waited[eng][sk] = val
            out.append((sk, val))
        return out

    def mark(self, name):
        self.marks.append((name, sum(1 for r in self.rec["tensor"] if r[1] is not None)))

    def op(self, eng, fn, reads=(), writes=(), signal=True):
        fn = _freeze(fn)
        waits = self._waits(eng, reads, writes)
        if signal:
            self.count[eng] += 1
            ev = (self.esem[eng], self.count[eng])
        else:
            ev = (self.esem[eng], self.count[eng] + 1)
        self.rec[eng].append((waits, fn, (self.esem[eng], 1) if signal else None))
        for b in reads:
            b.reads.append(ev)
        for b in writes:
            b.last_write = ev
            b.reads = []
        return ev

    def dma(self, eng, out_ap, in_ap, reads=(), writes=(), sembuf=None, is_output=False):
        sb = sembuf if sembuf is not None else (writes[0] if writes else reads[0])
        if sb.dsem is None:
            sb.dsem = self.new_sem("dma_" + sb.name)
        waits = self._waits(eng, reads, writes, dsem=sb.dsem)
        sb.dcount += 16
        ev = (sb.dsem, sb.dcount)

        def fn(e, out_ap=out_ap, in_ap=in_ap):
            return e.dma_start(out=out_ap, in_=in_ap)
        self.rec[eng].append((waits, fn, (sb.dsem, 16)))
        for b in reads:
            b.reads.append(ev)
        for b in writes:
            b.last_write = ev
            b.reads = []
        if is_output:
            self.out_events.append(ev)
        return ev

    def finish(self, eng="sync"):
        w = {}
        for sk, val in self.out_events:
            w[sk] = max(w.get(sk, 0), val)
        self.rec[eng].append(([(sk, v) for sk, v in w.items()], None, None))

    def emit(self):
        with self.nc.Block() as block:
            def make(engname):
                def body(e):
                    with self.nc.allow_non_contiguous_dma(reason="tiny state rows"):
                        for waits, fn, inc in self.rec[engname]:
                            for sk, val in waits:
                                e.wait_ge(self.sems[sk], val)
                            if fn is not None:
                                ins = fn(e)
                                if inc is not None:
                                    ins.then_inc(self.sems[inc[0]], inc[1])
                return body
            block.tensor(make("tensor"))
            block.vector(make("vector"))
            block.scalar(make("scalar"))
            block.gpsimd(make("gpsimd"))
            block.sync(make("sync"))

    def close(self):
        self.es.close()


class Pool:
    def __init__(self, bufs):
        self.bufs = bufs
        self.i = 0

    def get(self):
        b = self.bufs[self.i % len(self.bufs)]
        self.i += 1
        return b


def build_program(stop_after=99):
    nc = bass.Bass("TRN2", target_bir_lowering=False)
    P = Prog(nc)

    def din(name, shape, dt=F32):
        return nc.dram_tensor(name, list(shape), dt, kind="ExternalInput").ap()

    def dout(name, shape, dt=F32):
        return nc.dram_tensor(name, list(shape), dt, kind="ExternalOutput").ap()

    d_xTp = din("xTp", [128, 8, 2048]); d_xTs = din("xTs", [128, 8, 128])
    d_wA_in = din("wA_in", [8, 128, 8, 512]); d_wA_out = din("wA_out", [4, 128, 16, 256])
    d_wF_up = din("wF_up", [2, 11, 128, 8, 512]); d_wF_dn = din("wF_dn", [2, 8, 128, 22, 128])
    d_wB_in = din("wB_in", [12, 128, 8, 512]); d_wB_g = din("wB_g", [128, 8, 16])
    d_wB_out = din("wB_out", [4, 128, 16, 256])
    d_gcol = din("gcol", [128, 5, 8]); d_fcw = din("fcw", [128, 2, 3, 22]); d_fcb = din("fcb", [128, 2, 22])
    d_bcw = din("bcw", [128, 4, 16]); d_bcb = din("bcb", [128, 16]); d_bif = din("bif", [8, 2])
    d_gncol = din("gncol", [128, 16])
    d_lng = din("lng", [1, 2048]); d_lnb = din("lnb", [1, 2048])
    d_ws = din("ws", [128, 8, 128]); d_wsbd = din("wsbd", [128, 8, 128])
    d_bs8 = din("bs8", [8, 128]); d_bs8s = din("bs8s", [8, 128])
    d_Cst = din("Cst", [16, 8, 128, 256]); d_nst = din("nst", [128, 128]); d_m0T = din("m0T", [8, 16])
    d_cvs = din("cvs", [128, 16, 16, 3]); d_ffs = din("ffs", [128, 2, 22, 16, 2])
    d_identb = din("identb", [128, 128], BF16); d_identf = din("identf", [128, 128])
    d_onesb = din("onesb", [128, 128], BF16)
    d_trip = din("trip", [128, 128], BF16); d_bdp = din("bdp", [128, 128], BF16)
    d_tri01 = din("tri01", [128, 128]); d_bmask = din("bmask", [128, 16], BF16)
    d_sel = din("sel", [8, 8, 128]); d_rst = din("rst", [8, 2, 128])
    o_yTp = dout("yTp", [128, 8, 2048]); o_yTs = dout("yTs", [128, 8, 128])
    o_pC = dout("pC", [8, 128, 256]); o_pn = dout("pn", [128, 8]); o_pm = dout("pm", [8, 1])
    o_pconv = dout("pconv", [128, 16, 3]); o_pffn = dout("pffn", [128, 2, 22, 2])
    o_sv = dout("sv", [128, 2048])
    o_sC = dout("sC", [16, 8, 128, 256]); o_sn = dout("sn", [128, 128]); o_sm = dout("sm", [8, 16])
    o_sconv = dout("sconv", [128, 16, 16, 3]); o_sffn = dout("sffn", [128, 2, 22, 16, 2])

    xT = P.sbuf("xT", [128, 8, TW], F32)
    hT = P.sbuf("hT", [128, 8, TW], BF16)
    bigt = P.sbuf("big", [128, NJ, TW], BF16)
    xg = [Buf(f"xg{i}") for i in range(3)]
    hg = [Buf(f"hg{i}") for i in range(3)]
    slotL = [[Buf(f"slot{i}_{t}") for t in range(9)] for i in range(NJ)]

    def SLC(c, lo, n):
        return slotL[c][lo // 128:(lo + n + 127) // 128]

    def SLW(c):
        return list(slotL[c])
    NRING = 4
    ring = [P.sbuf(f"ring{i}", [128, 4096], BF16) for i in range(NRING)]
    ring_i = [0]
    U16 = [P.sbuf(f"U16_{i}", [128, 2056], F32) for i in range(2)]
    wg = P.sbuf("wg", [128, 8, 16], BF16)
    gcol = P.sbuf("gcol", [128, 5, 8], F32); fcw = P.sbuf("fcw", [128, 2, 3, 22], F32)
    fcb = P.sbuf("fcb", [128, 2, 22], F32); bcw = P.sbuf("bcw", [128, 4, 16], F32)
    bcb = P.sbuf("bcb", [128, 16], F32); bif = P.sbuf("bif", [8, 2], F32); gncol = P.sbuf("gncol", [128, 16], F32)
    wsT = P.sbuf("wsT", [128, 8, 128], BF16); wsTs = P.sbuf("wsTs", [128, 8, 128], BF16)
    bsh = P.sbuf("bsh", [40, 2, 128], BF16)
    selb = P.sbuf("selb", [40, 8, 128], BF16)
    identb = P.sbuf("identb", [128, 128], BF16); identf = P.sbuf("identf", [128, 128], F32)
    onesb = P.sbuf("onesb", [128, 128], BF16); trip = P.sbuf("trip", [128, 128], BF16)
    bdp = P.sbuf("bdp", [128, 128], BF16)
    bmask = P.sbuf("bmask", [128, 16], BF16); sel = P.sbuf("sel", [8, 8, 128], F32)
    rst = P.sbuf("rst", [8, 2, 128], F32)
    Mrows = P.sbuf("Mrows", [8, 1 + TW], F32)
    bcar = P.sbuf("bcar", [8, 2], F32)
    acols = P.sbuf("acols", [128, 9, 16], F32)
    gcols = P.sbuf("gcols", [128, 8], F32)
    C32 = P.sbuf("C32", [128, 8, 257], F32)
    C32h = [Buf(f"C32h{h}") for h in range(8)]
    Mpcol = P.sbuf("Mpcol", [128, 8], F32)
    atail = P.sbuf("atail", [128, 2, NJ, 2], F32)
    qtail = P.sbuf("qtail", [128, 16, 3], F32)
    nsT = P.sbuf("nsT", [128, 128], F32)
    decb = P.sbuf("decb", [128, 8, 16], F32)
    zerob = P.sbuf("zerob", [128, 128], BF16)
    m0row = P.sbuf("m0row", [8, 16, 8], F32)
    onesr = P.sbuf("onesr", [8, 1], F32); zerosr = P.sbuf("zerosr", [8, 1], F32)
    S32 = Pool([P.sbuf(f"S32_{i}", [128, 520], F32) for i in range(3)])
    OG = Pool([P.sbuf(f"OG_{i}", [128, 512], F32) for i in range(2)])
    WT = Pool([P.sbuf(f"WT_{i}", [128, 256], F32) for i in range(3)])
    SA = Pool([P.sbuf(f"SA_{i}", [128, 128], BF16) for i in range(7)])
    CB = Pool([P.sbuf(f"CB_{i}", [128, 258], BF16) for i in range(3)])
    GT = Pool([P.sbuf(f"GT_{i}", [128, 256], BF16) for i in range(3)])
    RS = Pool([P.sbuf(f"RS_{i}", [128, 512], F32) for i in range(1)])
    VA = Pool([P.sbuf(f"VA_{i}", [128, 514], BF16) for i in range(2)])
    S16 = Pool([P.sbuf(f"S16_{i}", [128, 512], BF16) for i in range(2)])
    SC = Pool([P.sbuf(f"SC_{i}", [128, 16], F32) for i in range(8)])
    HS = Pool([P.sbuf(f"HS_{i}", [128, 48], F32) for i in range(3)])
    PA = Pool([P.psum(f"pa{i}", [128, 512], F32) for i in range(6)])
    PB = Pool([P.psum(f"pb{i}", [128, 1024], BF16) for i in range(2)])
    print("sbuf remaining", nc.sbuf_bytes_remaining, "sems", P.nsem)

    def V(e):
        return "vector"

    onetime = []
    for sb, d in ((gcol, d_gcol), (fcw, d_fcw), (fcb, d_fcb), (bcw, d_bcw), (bcb, d_bcb), (bif, d_bif),
                  (gncol, d_gncol), (identb, d_identb), (identf, d_identf),
                  (onesb, d_onesb), (trip, d_trip), (bdp, d_bdp), (bmask, d_bmask),
                  (sel, d_sel), (rst, d_rst)):
        P.dma("sync", sb[:], d, writes=[sb], sembuf=gcol)
        onetime.append(sb)
    for sb in onetime:
        sb.last_write = (gcol.dsem, gcol.dcount)
    P.dma("gpsimd", wg[:], d_wB_g, writes=[wg])
    P.op("vector", lambda e: e.memset(onesr[:], 1.0), writes=[onesr])
    P.op("vector", lambda e: e.memset(zerosr[:], 0.0), writes=[zerosr])
    P.op("vector", lambda e: e.memset(C32[:], 0.0), writes=C32h)
    P.op("vector", lambda e: e.memset(Mrows[:, 0:1], 0.0), writes=[Mrows])
    P.op("vector", lambda e: e.memset(bcar[:], 0.0), writes=[bcar])
    P.op("vector", lambda e: e.memset(atail[:], 0.0), writes=[atail])
    P.op("vector", lambda e: e.memset(qtail[:], 0.0), writes=[qtail])
    P.op("vector", lambda e: e.memset(Mpcol[:], 0.0), writes=[Mpcol])
    P.op("vector", lambda e: e.memset(zerob[:], 0.0), writes=[zerob])

    tri01 = S32.get()
    P.dma("sync", tri01[:, 0:128], d_tri01, writes=[tri01])
    bias_init_done = [False]

    def init_bias():
        if bias_init_done[0]:
            return
        bias_init_done[0] = True
        P.op("vector", lambda e: e.memset(selb[:], 0.0), writes=[selb])
        P.op("vector", lambda e: e.memset(bsh[:], 0.0), writes=[bsh])
        P.dma("gpsimd", selb[0:8, :, :], d_sel, writes=[selb])
        P.dma("gpsimd", selb[32:40, :, :], d_sel, writes=[selb])
        for i_, dsrc_ in enumerate((d_bs8, d_bs8s)):
            src_ = S32.get()
            sap = src_[0:8, 0:128]
            P.dma("sync", sap, dsrc_, writes=[src_])
            P.op("vector", lambda e: e.tensor_copy(bsh[0:8, i_, :], sap), reads=[src_], writes=[bsh])
            P.op("vector", lambda e: e.tensor_tensor(sap, sap, bsh[0:8, i_, :], ALU.subtract), reads=[src_, bsh], writes=[src_])
            P.dma("gpsimd", bsh[32:40, i_, :], sap, reads=[src_], writes=[bsh])
    for (dsrc, dst) in ((d_ws, wsT), (d_wsbd, wsTs)):
        for half in range(2):
            t4 = U16[half]
            P.dma("sync", t4[:, 0:512].rearrange("p (g s) -> p g s", g=4), dsrc[:, half * 4:(half + 1) * 4, :], writes=[t4])
            P.op("vector", lambda e, t4=t4: e.tensor_tensor(t4[:, 0:512].rearrange("p (g s) -> p g s", g=4),
                                                             t4[:, 0:512].rearrange("p (g s) -> p g s", g=4),
                                                             tri01[:, 0:128].unsqueeze(1).to_broadcast([128, 4, 128]), ALU.mult),
                 reads=[t4, tri01], writes=[t4])
            pb = PA.get()
            for g in range(4):
                P.op("tensor", lambda e, g=g, t4=t4, pb=pb: e.transpose(pb[:, g * 128:(g + 1) * 128], t4[:, g * 128:(g + 1) * 128], identf[:]),
                     reads=[t4, identf], writes=[pb], signal=(g == 3))
            P.op("vector", lambda e, pb=pb, dst=dst, half=half: e.tensor_copy(dst[:, half * 4:(half + 1) * 4, :],
                                                                               pb[:, 0:512].rearrange("p (g s) -> p g s", g=4)),
                 reads=[pb], writes=[dst])

    def ring_load(parts):
        s = ring[ring_i[0] % NRING]
        ring_i[0] += 1
        for (lo, n, src) in parts:
            P.dma("gpsimd", s[:, lo:lo + n], src, writes=[s])
        return s

    def groups_of(ntiles, has_sample):
        gs = []
        npt = ntiles - (1 if has_sample else 0)
        t = 0
        gi = 0
        while t < npt:
            n = min(4, npt - t)
            gs.append(dict(lo=t * 128, n=n * 128, gi=gi, sample=False, tiles=list(range(t, t + n))))
            t += n
            gi += 1
        if has_sample:
            gs.append(dict(lo=npt * 128, n=128, gi=gi, sample=True, tiles=[npt]))
        return gs

    def rsqrt_act(out_ap, in_ap, scale, bias, rbufs, wbufs):
        P.op("scalar", lambda e: e.activation(out_ap, in_ap, AF.Ln, bias=bias, scale=scale), reads=rbufs, writes=wbufs)
        P.op("scalar", lambda e: e.activation(out_ap, out_ap, AF.Exp, scale=-0.5), reads=wbufs, writes=wbufs)

    def rmsnorm_to(gi_idx, groups, dst_fn):
        for g in groups:
            lo, n = g["lo"], g["n"]
            pss = PA.get()
            for c in range(8):
                sq = S16.get()
                P.op("scalar", lambda e, sq=sq, c=c: e.activation(sq[:, 0:n], xT[:, c, lo:lo + n], AF.Square),
                     reads=[xg[g["gi"]]], writes=[sq])
                P.op("tensor", lambda e, sq=sq, c=c: e.matmul(pss[:, 0:n], lhsT=onesb[:], rhs=sq[:, 0:n], start=(c == 0), stop=(c == 7)),
                     reads=[sq, onesb], writes=[pss], signal=True)
            rstd = RS.get()
            rsqrt_act(rstd[:, 0:n], pss[:, 0:n], 1.0 / 1024.0, EPS, [pss], [rstd])
            for c in range(8):
                dst_fn(g, c, xT[:, c, lo:lo + n], gcol[:, gi_idx, c:c + 1], rstd[:, 0:n], rstd)

    def norm_to_hT(gi_idx, groups):
        def dst(g, c, xin, gc, rs, rstd):
            lo, n = g["lo"], g["n"]
            P.op("vector", lambda e: e.scalar_tensor_tensor(hT[:, c, lo:lo + n], xin, gc, rs, ALU.mult, ALU.mult),
                 reads=[xg[g["gi"]], rstd, gcol], writes=[hg[g["gi"]]])
        rmsnorm_to(gi_idx, groups, dst)

    def resid_add(g, m, ps):
        lo, n = g["lo"], g["n"]
        P.op("vector", lambda e: e.tensor_tensor(xT[:, m, lo:lo + n], xT[:, m, lo:lo + n], ps[:, 0:n], ALU.add),
             reads=[ps, xg[g["gi"]]], writes=[xg[g["gi"]]])

    def out_proj(d_w, groups, kslots):
        for mm in range(4):
            s = ring_load([(0, 4096, d_w[mm].rearrange("p a b -> p (a b)"))])
            sv = s[:, 0:4096].rearrange("p (a b) -> p a b", a=16)
            for g in groups:
                lo, n = g["lo"], g["n"]
                for m2 in range(2):
                    ps = PA.get()
                    for ei in range(16):
                        sl = kslots[ei]
                        P.op("tensor", lambda e, ei=ei, sl=sl, ps=ps, m2=m2: e.matmul(
                            ps[:, 0:n], lhsT=sv[:, ei, m2 * 128:(m2 + 1) * 128], rhs=bigt[:, sl, lo:lo + n],
                            start=(ei == 0), stop=(ei == 15)),
                            reads=[s, *SLC(sl, lo, n)], writes=[ps], signal=(ei == 15))
                    resid_add(g, mm * 2 + m2, ps)

    def mixer_A(groups, tiles, has_sample):
        P.mark('A.norm')
        norm_to_hT(0, groups)
        P.dma("sync", U16[0][:, 0:2048], d_lng.partition_broadcast(128), writes=[U16[0]])
        P.dma("sync", U16[1][:, 0:2048], d_lnb.partition_broadcast(128), writes=[U16[1]])
        P.mark('A.u')
        for q in range(4):
            s = ring_load([(0, 4096, d_wA_in[q].rearrange("p a b -> p (a b)"))])
            sv = s[:, 0:4096].rearrange("p (a b) -> p a b", a=8)
            for g in groups:
                lo, n = g["lo"], g["n"]
                for cc in range(4):
                    c = q * 4 + cc
                    ps = PA.get()
                    for k in range(8):
                        P.op("tensor", lambda e, k=k, cc=cc, ps=ps: e.matmul(ps[:, 0:n], lhsT=sv[:, k, cc * 128:(cc + 1) * 128],
                                                                               rhs=hT[:, k, lo:lo + n], start=(k == 0), stop=(k == 7)),
                             reads=[s, hg[g["gi"]]], writes=[ps], signal=(k == 7))
                    P.op("scalar", lambda e, c=c, ps=ps: e.activation(bigt[:, c, lo:lo + n], ps[:, 0:n], AF.Gelu_apprx_tanh),
                         reads=[ps], writes=SLC(c, lo, n))
        P.mark('A.v')
        vp = []
        for q in range(4):
            s = ring_load([(0, 4096, d_wA_in[4 + q].rearrange("p a b -> p (a b)"))])
            vp.append(s)
        init_bias()
        v32 = bigt[:, 16:20, :].rearrange("p a b -> p (a b)").bitcast(F32)[:, 0:2048]
        v32b = SLW(16) + SLW(17) + SLW(18) + SLW(19)
        vln = bigt[:, 20:22, :].rearrange("p a b -> p (a b)")[:, 0:2048]
        vlnb = SLW(20) + SLW(21)
        PV = Pool(PA.bufs[0:4])
        PM = Pool(PA.bufs[4:6])

        def a_part1(ti):
            lo = ti * 128
            gi = lo // 512
            pss_ = []
            for q in range(4):
                s = vp[q]
                sv = s[:, 0:4096].rearrange("p (a b) -> p a b", a=8)
                ps = PV.get()
                for k in range(8):
                    P.op("tensor", lambda e: e.matmul(ps[:, 0:512], lhsT=hT[:, k, lo:lo + 128], rhs=sv[:, k, :],
                                                      start=(k == 0), stop=(k == 7)),
                         reads=[s, hg[gi]], writes=[ps], signal=(k == 7))
                pss_.append(ps)
            return pss_

        def a_part2(ti, pss_):
            is_s = has_sample and ti == len(tiles) - 1
            for q in range(4):
                ps = pss_[q]
                P.op("scalar", lambda e: e.activation(v32[:, q * 512:(q + 1) * 512], ps[:, 0:512], AF.Gelu_apprx_tanh),
                     reads=[ps], writes=v32b)
            st = S32.get()
            for q in range(4):
                P.op("vector", lambda e: e.bn_stats(st[:, q * 6:(q + 1) * 6], v32[:, q * 512:(q + 1) * 512]),
                     reads=v32b, writes=[st])
            mv = SC.get()
            P.op("vector", lambda e: e.bn_aggr(mv[:, 0:2], st[:, 0:24].rearrange("p (a b) -> p a b", a=4)),
                 reads=[st], writes=[mv])
            rsqrt_act(mv[:, 2:3], mv[:, 1:2], 1.0, EPS, [mv], [mv])
            P.op("vector", lambda e: e.tensor_scalar(v32, v32, mv[:, 0:1], mv[:, 2:3], ALU.subtract, ALU.mult),
                 reads=v32b + [mv], writes=v32b)
            P.op("vector", lambda e: e.tensor_tensor(v32, v32, U16[0][:, 0:2048], ALU.mult), reads=v32b + [U16[0]], writes=v32b)
            if is_s:
                P.op("vector", lambda e: e.tensor_tensor(v32, v32, U16[1][:, 0:2048], ALU.add), reads=v32b + [U16[1]], writes=v32b)
                P.dma("sync", o_sv, v32, reads=v32b, sembuf=slotL[16][0], is_output=True)
                P.op("scalar", lambda e: e.activation(vln, v32, AF.Copy), reads=v32b, writes=vlnb)
            else:
                P.op("vector", lambda e: e.tensor_tensor(vln, v32, U16[1][:, 0:2048], ALU.add), reads=v32b + [U16[1]], writes=vlnb)

        def a_part3(ti):
            lo = ti * 128
            is_s = has_sample and ti == len(tiles) - 1
            wT_ = wsTs if is_s else wsT
            bsi = 1 if is_s else 0
            for cb in range(4):
                ps = PM.get()
                for cc in range(4):
                    c = cb * 4 + cc
                    gidx = c // 2
                    P.op("tensor", lambda e: e.matmul(ps[:, cc * 128:(cc + 1) * 128], lhsT=vln[:, c * 128:(c + 1) * 128],
                                                      rhs=wT_[:, gidx, :], start=True, stop=False),
                         reads=vlnb + [wT_], writes=[ps], signal=False)
                    P.op("tensor", lambda e: e.matmul(ps[:, cc * 128:(cc + 1) * 128], lhsT=selb[:, gidx, :],
                                                      rhs=bsh[:, bsi, :], start=False, stop=True),
                         reads=[selb, bsh], writes=[ps], signal=(cc == 3))
                P.op("vector", lambda e: e.tensor_tensor(bigt[:, cb * 4:cb * 4 + 4, lo:lo + 128],
                                                         ps[:, 0:512].rearrange("p (a b) -> p a b", a=4),
                                                         bigt[:, cb * 4:cb * 4 + 4, lo:lo + 128], ALU.mult),
                     reads=[ps] + [b_ for i in range(4) for b_ in SLC(cb * 4 + i, lo, 128)], writes=[b_ for i in range(4) for b_ in SLC(cb * 4 + i, lo, 128)])

        nt_ = len(tiles)
        pend = a_part1(0)
        a_part2(0, pend)
        for ti in range(nt_):
            nxt = a_part1(ti + 1) if ti + 1 < nt_ else None
            a_part3(ti)
            if nxt is not None:
                a_part2(ti + 1, nxt)
        P.mark('A.out')
        out_proj(d_wA_out, groups, list(range(16)))

    def ffn(l, groups, last_pass):
        P.mark('F.norm')
        norm_to_hT(2 + l, groups)
        P.mark('F.up')
        pend_f = []
        pend_a = []
        for jj in range(11):
            s = ring_load([(0, 4096, d_wF_up[l, jj].rearrange("p a b -> p (a b)"))])
            sv = s[:, 0:4096].rearrange("p (a b) -> p a b", a=8)
            for j2 in range(2):
                j = jj * 2 + j2
                w0 = fcw[:, l, 0, j:j + 1]; w1 = fcw[:, l, 1, j:j + 1]; w2 = fcw[:, l, 2, j:j + 1]
                cb_ = fcb[:, l, j:j + 1]
                for g in groups:
                    lo, n = g["lo"], g["n"]
                    psa = PA.get()
                    for k in range(8):
                        P.op("tensor", lambda e, k=k, psa=psa: e.matmul(psa[:, 0:n], lhsT=sv[:, k, j2 * 128:(j2 + 1) * 128],
                                                                        rhs=hT[:, k, lo:lo + n], start=(k == 0), stop=(k == 7)),
                             reads=[s, hg[g["gi"]]], writes=[psa], signal=(k == 7))
                    psg = PA.get()
                    for k in range(8):
                        P.op("tensor", lambda e, k=k, psg=psg: e.matmul(psg[:, 0:n], lhsT=sv[:, k, 256 + j2 * 128:256 + (j2 + 1) * 128],
                                                                        rhs=hT[:, k, lo:lo + n], start=(k == 0), stop=(k == 7)),
                             reads=[s, hg[g["gi"]]], writes=[psg], signal=(k == 7))
                    t0 = S32.get()
                    P.op("scalar", lambda e: e.activation(t0[:, 0:n], psa[:, 0:n], AF.Identity, bias=cb_, scale=w2),
                         reads=[psa, fcw, fcb], writes=[t0])
                    if pend_a:
                        pend_a.pop()()
                    if not g["sample"]:
                        P.op("vector", lambda e: e.scalar_tensor_tensor(t0[:, 1:n], psa[:, 0:n - 1], w1, t0[:, 1:n], ALU.mult, ALU.add),
                             reads=[psa, t0, fcw], writes=[t0])
                        P.op("vector", lambda e: e.scalar_tensor_tensor(t0[:, 2:n], psa[:, 0:n - 2], w0, t0[:, 2:n], ALU.mult, ALU.add),
                             reads=[psa, t0, fcw], writes=[t0])
                        P.op("vector", lambda e: e.scalar_tensor_tensor(t0[:, 0:2], atail[:, l, j, 0:2], w0, t0[:, 0:2], ALU.mult, ALU.add),
                             reads=[atail, t0, fcw], writes=[t0])
                        P.op("vector", lambda e: e.scalar_tensor_tensor(t0[:, 0:1], atail[:, l, j, 1:2], w1, t0[:, 0:1], ALU.mult, ALU.add),
                             reads=[atail, t0, fcw], writes=[t0])

                        def act2(t0=t0, psa=psa, j=j, n=n):
                            P.op("scalar", lambda e: e.activation(atail[:, l, j, :], psa[:, n - 2:n], AF.Copy), reads=[psa], writes=[atail])
                            P.op("scalar", lambda e: e.activation(t0[:, 0:n], t0[:, 0:n], AF.Gelu_apprx_tanh), reads=[t0], writes=[t0])
                    else:
                        hs = HS.get(); os_ = HS.get()
                        hs3 = hs[:, 0:32].rearrange("p (b r) -> p b r", b=16)
                        os3 = os_[:, 0:32].rearrange("p (b r) -> p b r", b=16)
                        psa3 = psa[:, 0:128].rearrange("p (b r) -> p b r", b=16)
                        t03 = t0[:, 0:128].rearrange("p (b r) -> p b r", b=16)
                        P.dma("sync", hs3, d_ffs[:, l, j, :, :], writes=[hs])
                        P.op("vector", lambda e: e.scalar_tensor_tensor(t03[:, :, 1:8], psa3[:, :, 0:7], w1, t03[:, :, 1:8], ALU.mult, ALU.add),
                             reads=[psa, t0, fcw], writes=[t0])
                        P.op("vector", lambda e: e.scalar_tensor_tensor(t03[:, :, 2:8], psa3[:, :, 0:6], w0, t03[:, :, 2:8], ALU.mult, ALU.add),
                             reads=[psa, t0, fcw], writes=[t0])
                        P.op("vector", lambda e: e.scalar_tensor_tensor(t03[:, :, 0:2], hs3[:, :, 0:2], w0, t03[:, :, 0:2], ALU.mult, ALU.add),
                             reads=[hs, t0, fcw], writes=[t0])
                        P.op("vector", lambda e: e.scalar_tensor_tensor(t03[:, :, 0:1], hs3[:, :, 1:2], w1, t03[:, :, 0:1], ALU.mult, ALU.add),
                             reads=[hs, t0, fcw], writes=[t0])

                        def act2(t0=t0, psa3=psa3, psa=psa, os3=os3, os_=os_, j=j, n=n):
                            P.op("scalar", lambda e: e.activation(os3, psa3[:, :, 6:8], AF.Copy), reads=[psa], writes=[os_])
                            P.dma("sync", o_sffn[:, l, j, :, :], os3, reads=[os_], sembuf=os_, is_output=True)
                            P.op("scalar", lambda e: e.activation(t0[:, 0:n], t0[:, 0:n], AF.Gelu_apprx_tanh), reads=[t0], writes=[t0])
                    pend_a.append(act2)
                    if pend_f:
                        pend_f.pop()()

                    def fin(t0=t0, psg=psg, j=j, lo=lo, n=n):
                        P.op("vector", lambda e: e.tensor_tensor(bigt[:, j, lo:lo + n], t0[:, 0:n], psg[:, 0:n], ALU.mult),
                             reads=[t0, psg], writes=SLC(j, lo, n))
                    pend_f.append(fin)
        if pend_a:
            pend_a.pop()()
        if pend_f:
            pend_f.pop()()
        if last_pass:
            P.dma("sync", o_pffn[:, l, :, :], atail[:, l, :, :], reads=[atail], sembuf=atail, is_output=True)
        P.mark('F.down')
        for m in range(8):
            s = ring_load([(0, NJ * 128, d_wF_dn[l, m].rearrange("p a b -> p (a b)"))])
            sv = s[:, 0:NJ * 128].rearrange("p (a b) -> p a b", a=NJ)
            for g in groups:
                lo, n = g["lo"], g["n"]
                ps = PA.get()
                for j in range(NJ):
                    P.op("tensor", lambda e, j=j, ps=ps: e.matmul(ps[:, 0:n], lhsT=sv[:, j, :], rhs=bigt[:, j, lo:lo + n],
                                                                 start=(j == 0), stop=(j == NJ - 1)),
                         reads=[s, *SLC(j, lo, n)], writes=[ps], signal=(j == NJ - 1))
                resid_add(g, m, ps)

    def mixer_B(groups, tiles, has_sample, first_pass, last_pass):
        P.mark('B.norm')
        norm_to_hT(1, groups)
        P.mark('B.qk')
        npt = len(tiles) - (1 if has_sample else 0)
        def gates_gen():
            RW = 520
            def rowbuf(i):
                return bigt[0:8, 16 + i, :].bitcast(F32)[:, 0:RW]
            for g in groups:
                lo, n, gi = g["lo"], g["n"], g["gi"]
                igr, zr, lfr, bgr, t1r, t2r = [rowbuf(i) for i in range(6)]
                rb = [SLW(16 + i) for i in range(6)]
                psi = PA.get()
                for k in range(8):
                    P.op("tensor", lambda e, k=k, psi=psi: e.matmul(psi[0:8, 0:n], lhsT=wg[:, k, 0:8], rhs=hT[:, k, lo:lo + n], start=(k == 0), stop=(k == 7)),
                         reads=[wg, hg[gi]], writes=[psi], signal=(k == 7))
                P.op("scalar", lambda e, psi=psi: e.activation(igr[:, 0:n], psi[0:8, 0:n], AF.Identity, bias=bif[:, 0:1], scale=1.0),
                     reads=[psi, bif], writes=[*rb[0]])
                yield
                psf = PA.get()
                for k in range(8):
                    P.op("tensor", lambda e, k=k, psf=psf: e.matmul(psf[0:8, 0:n], lhsT=wg[:, k, 8:16], rhs=hT[:, k, lo:lo + n], start=(k == 0), stop=(k == 7)),
                         reads=[wg, hg[gi]], writes=[psf], signal=(k == 7))
                P.op("scalar", lambda e, psf=psf: e.activation(zr[:, 0:n], psf[0:8, 0:n], AF.Identity, bias=bif[:, 1:2], scale=1.0),
                     reads=[psf, bif], writes=[*rb[1]])
                yield
                P.op("vector", lambda e: e.scalar_tensor_tensor(t1r[:, 0:n], zr[:, 0:n], -1.0, zr[:, 0:n], ALU.mult, ALU.max), reads=[*rb[1]], writes=[*rb[4]])
                yield
                P.op("scalar", lambda e: e.activation(t1r[:, 0:n], t1r[:, 0:n], AF.Exp, scale=-1.0), reads=[*rb[4]], writes=[*rb[4]])
                yield
                P.op("scalar", lambda e: e.activation(t1r[:, 0:n], t1r[:, 0:n], AF.Ln, bias=1.0, scale=1.0), reads=[*rb[4]], writes=[*rb[4]])
                yield
                P.op("vector", lambda e: e.tensor_scalar(lfr[:, 0:n], zr[:, 0:n], 0.0, None, ALU.min), reads=[*rb[1]], writes=[*rb[2]])
                yield
                P.op("vector", lambda e: e.tensor_tensor(lfr[:, 0:n], lfr[:, 0:n], t1r[:, 0:n], ALU.subtract), reads=[*rb[2], *rb[4]], writes=[*rb[2]])
                yield
                mcol0 = 1 + lo
                if not g["sample"]:
                    P.op("vector", lambda e: e.tensor_tensor_scan(bgr[:, 0:n], onesr[:, 0:1].to_broadcast([8, n]), lfr[:, 0:n], bcar[:, 0:1], ALU.mult, ALU.add),
                         reads=[onesr, *rb[2], bcar], writes=[*rb[3]])
                    yield
                    P.op("vector", lambda e: e.tensor_copy(bcar[:, 0:1], bgr[:, n - 1:n]), reads=[*rb[3]], writes=[bcar])
                    yield
                    P.op("vector", lambda e: e.tensor_tensor(igr[:, 0:n], igr[:, 0:n], bgr[:, 0:n], ALU.subtract), reads=[*rb[0], *rb[3]], writes=[*rb[0]])
                    yield
                    P.op("vector", lambda e: e.tensor_tensor_scan(Mrows[:, mcol0:mcol0 + n], zerosr[:, 0:1].to_broadcast([8, n]), igr[:, 0:n], Mrows[:, mcol0 - 1:mcol0], ALU.add, ALU.max),
                         reads=[zerosr, *rb[0], Mrows], writes=[Mrows])
                    yield
                else:
                    P.dma("sync", m0row[:, :, 0:1], d_m0T.unsqueeze(2), writes=[m0row])
                    P.op("vector", lambda e: e.tensor_copy(m0row[:, :, 1:8], m0row[:, :, 0:1].to_broadcast([8, 16, 7])), reads=[m0row], writes=[m0row])
                    yield
                    P.op("vector", lambda e: e.tensor_tensor_scan(bgr[:, 0:n], rst[:, 0, :], lfr[:, 0:n], 0.0, ALU.mult, ALU.add),
                         reads=[rst, *rb[2]], writes=[*rb[3]])
                    yield
                    P.op("vector", lambda e: e.tensor_tensor(igr[:, 0:n], igr[:, 0:n], bgr[:, 0:n], ALU.subtract), reads=[*rb[0], *rb[3]], writes=[*rb[0]])
                    yield
                    P.op("vector", lambda e: e.scalar_tensor_tensor(t2r[:, 0:n], rst[:, 0, :], -1e30, m0row[:].rearrange("p b r -> p (b r)"), ALU.mult, ALU.add),
                         reads=[rst, m0row], writes=[*rb[5]])
                    yield
                    P.op("vector", lambda e: e.tensor_tensor(t2r[:, 0:n], t2r[:, 0:n], igr[:, 0:n], ALU.max), reads=[*rb[5], *rb[0]], writes=[*rb[5]])
                    yield
                    P.op("vector", lambda e: e.tensor_tensor_scan(Mrows[:, mcol0:mcol0 + n], rst[:, 1, :], t2r[:, 0:n], -1e30, ALU.add, ALU.max),
                         reads=[rst, *rb[5]], writes=[Mrows])
                    yield
                P.op("vector", lambda e: e.tensor_tensor(t1r[:, 0:n], bgr[:, 0:n], Mrows[:, mcol0:mcol0 + n], ALU.add), reads=[*rb[3], Mrows], writes=[*rb[4]])
                yield
                if not g["sample"]:
                    P.op("vector", lambda e: e.tensor_copy(bcar[:, 1:2], t1r[:, n - 1:n]), reads=[*rb[4]], writes=[bcar])
                    yield
                else:
                    P.dma("sync", o_sm.unsqueeze(2), t1r[:, 0:128].rearrange("p (b r) -> p b r", b=16)[:, :, 7:8], reads=[*rb[4]], sembuf=m0row, is_output=True)
                P.op("vector", lambda e: e.tensor_scalar(t1r[:, 0:n], t1r[:, 0:n], -2.0, 80.0, ALU.mult, ALU.min), reads=[*rb[4]], writes=[*rb[4]])
                yield
                P.op("scalar", lambda e: e.activation(t1r[:, 0:n], t1r[:, 0:n], AF.Exp), reads=[*rb[4]], writes=[*rb[4]])
                yield
                if g["sample"]:
                    Mend = Mrows[:, mcol0:mcol0 + 128].rearrange("p (b r) -> p b r", b=16)[:, :, 7:8]
                    P.op("vector", lambda e: e.tensor_tensor(t2r[:, 0:128].rearrange("p (b r) -> p b r", b=16), igr[:, 0:128].rearrange("p (b r) -> p b r", b=16),
                                                              Mend.to_broadcast([8, 16, 8]), ALU.subtract), reads=[*rb[0], Mrows], writes=[*rb[5]])
                    yield
                    P.op("scalar", lambda e: e.activation(t2r[:, 0:128], t2r[:, 0:128], AF.Exp, bias=LNSCALE_AP[0:8, :], scale=1.0), reads=[*rb[5], lnsc], writes=[*rb[5]])
                    yield
                    P.op("vector", lambda e: e.tensor_tensor(zr[:, 0:16].unsqueeze(2), m0row[:, :, 0:1], Mend, ALU.subtract), reads=[m0row, Mrows], writes=[*rb[1]])
                    yield
                    P.op("scalar", lambda e: e.activation(zr[:, 0:16], zr[:, 0:16], AF.Exp), reads=[*rb[1]], writes=[*rb[1]])
                    yield
                    P.op("vector", lambda e: e.tensor_tensor(lfr[:, 0:128], m0row[:].rearrange("p b r -> p (b r)"), Mrows[:, mcol0:mcol0 + 128], ALU.subtract),
                         reads=[m0row, Mrows], writes=[*rb[2]])
                    yield
                for til in g["tiles"]:
                    o = (til * 128) - lo
                    pc = PA.get()
                    P.op("tensor", lambda e, o=o, pc=pc: e.transpose(pc[:, 0:8], igr[:, o:o + 128], identf[0:8, 0:8]), reads=[*rb[0], identf], writes=[pc], signal=False)
                    P.op("tensor", lambda e, o=o, pc=pc: e.transpose(pc[:, 8:16], t1r[:, o:o + 128], identf[0:8, 0:8]), reads=[*rb[4], identf], writes=[pc], signal=not g["sample"])
                    if g["sample"]:
                        P.op("tensor", lambda e, pc=pc: e.transpose(pc[:, 16:24], t2r[:, 0:128], identf[0:8, 0:8]), reads=[*rb[5], identf], writes=[pc], signal=True)
                        P.op("vector", lambda e, pc=pc: e.tensor_copy(gcols[:], pc[:, 16:24]), reads=[pc], writes=[gcols])
                        pd = PA.get()
                        for h in range(8):
                            P.op("tensor", lambda e, h=h, pd=pd: e.matmul(pd[:, h * 16:(h + 1) * 16], lhsT=sel[:, h, :], rhs=zr[:, 0:16], start=True, stop=True),
                                 reads=[sel, *rb[1]], writes=[pd], signal=(h == 7))
                        P.op("vector", lambda e, pd=pd: e.tensor_copy(decb[:].rearrange("p h b -> p (h b)"), pd[:, 0:128]), reads=[pd], writes=[decb])
                    P.op("vector", lambda e, til=til, pc=pc: e.tensor_scalar(acols[:, til, 0:8], pc[:, 0:8], LNSCALE, None, ALU.add), reads=[pc], writes=[acols])
                    P.op("vector", lambda e, til=til, pc=pc: e.tensor_copy(acols[:, til, 8:16], pc[:, 8:16]), reads=[pc], writes=[acols])
                    yield
                if g["sample"]:
                    P.op("vector", lambda e: e.tensor_copy(dMs[:, 0:128], lfr[:, 0:128]), reads=[*rb[2]], writes=[dMsb])

            yield
        gates = gates_gen()
        pend_q = []
        for q in range(4):
            s = ring_load([(0, 4096, d_wB_in[q].rearrange("p a b -> p (a b)"))])
            sv = s[:, 0:4096].rearrange("p (a b) -> p a b", a=8)
            for cc in range(4):
                c = q * 4 + cc
                wc = [bcw[:, i, c:c + 1] for i in range(4)]
                cbb = bcb[:, c:c + 1]
                while pend_q:
                    pend_q.pop(0)()
                dg = []
                for i in range(4):
                    d_ = SA.get()
                    P.op("vector", lambda e: e.tensor_scalar(d_[:, 0:128], identb[:], wc[i], None, ALU.mult), reads=[identb, bcw], writes=[d_])
                    dg.append(d_)
                for g in groups:
                    lo, n = g["lo"], g["n"]
                    ps = PA.get()
                    for k in range(8):
                        P.op("tensor", lambda e, k=k, ps=ps: e.matmul(ps[:, 0:n], lhsT=sv[:, k, cc * 128:(cc + 1) * 128],
                                                                      rhs=hT[:, k, lo:lo + n], start=(k == 0), stop=(k == 7)),
                             reads=[s, hg[g["gi"]]], writes=[ps], signal=(k == 7))
                    if not g["sample"]:
                        xb = S16.get()
                        P.op("scalar", lambda e: e.activation(xb[:, 0:n], ps[:, 0:n], AF.Copy), reads=[ps], writes=[xb])
                        t3 = SC.get()
                        P.op("scalar", lambda e: e.activation(t3[:, 0:3], ps[:, 0:3], AF.Identity, bias=cbb, scale=wc[3]),
                             reads=[ps, bcw, bcb], writes=[t3])
                        P.op("vector", lambda e: e.scalar_tensor_tensor(t3[:, 1:3], ps[:, 0:2], wc[2], t3[:, 1:3], ALU.mult, ALU.add),
                             reads=[ps, t3, bcw], writes=[t3])
                        P.op("vector", lambda e: e.scalar_tensor_tensor(t3[:, 2:3], ps[:, 0:1], wc[1], t3[:, 2:3], ALU.mult, ALU.add),
                             reads=[ps, t3, bcw], writes=[t3])
                        for i in range(3):
                            P.op("vector", lambda e: e.scalar_tensor_tensor(t3[:, 0:3 - i], qtail[:, c, i:3], wc[i], t3[:, 0:3 - i], ALU.mult, ALU.add),
                                 reads=[qtail, t3, bcw], writes=[t3])
                        P.op("scalar", lambda e: e.activation(qtail[:, c, :], ps[:, n - 3:n], AF.Copy), reads=[ps], writes=[qtail])
                        P.op("scalar", lambda e: e.activation(bigt[:, c, lo:lo + 3], t3[:, 0:3], AF.Silu), reads=[t3], writes=SLC(c, lo, 3))

                        def finq(xb=xb, c=c, lo=lo, n=n, dg=dg, cbb=cbb):
                            pc = PA.get()
                            for i in range(4):
                                P.op("tensor", lambda e: e.matmul(pc[:, 3:n], lhsT=dg[i][:, 0:128], rhs=xb[:, i:i + n - 3], start=(i == 0), stop=(i == 3)),
                                     reads=[dg[i], xb], writes=[pc], signal=(i == 3))
                            P.op("scalar", lambda e: e.activation(bigt[:, c, lo + 3:lo + n], pc[:, 3:n], AF.Silu, bias=cbb, scale=1.0),
                                 reads=[pc, bcb], writes=SLC(c, lo, n))
                    else:
                        t0 = S32.get()
                        P.op("scalar", lambda e: e.activation(t0[:, 0:n], ps[:, 0:n], AF.Identity, bias=cbb, scale=wc[3]),
                             reads=[ps, bcw, bcb], writes=[t0])
                        hs = HS.get(); os_ = HS.get()
                        hs3 = hs[:, 0:48].rearrange("p (b r) -> p b r", b=16)
                        os3 = os_[:, 0:48].rearrange("p (b r) -> p b r", b=16)
                        ps3 = ps[:, 0:128].rearrange("p (b r) -> p b r", b=16)
                        t03 = t0[:, 0:128].rearrange("p (b r) -> p b r", b=16)
                        P.dma("sync", hs3, d_cvs[:, c, :, :], writes=[hs])
                        for i in range(3):
                            sh = 3 - i
                            P.op("vector", lambda e: e.scalar_tensor_tensor(t03[:, :, sh:8], ps3[:, :, 0:8 - sh], wc[i], t03[:, :, sh:8], ALU.mult, ALU.add),
                                 reads=[ps, t0, bcw], writes=[t0])
                        for i in range(3):
                            P.op("vector", lambda e: e.scalar_tensor_tensor(t03[:, :, 0:3 - i], hs3[:, :, i:3], wc[i], t03[:, :, 0:3 - i], ALU.mult, ALU.add),
                                 reads=[hs, t0, bcw], writes=[t0])
                        P.op("scalar", lambda e: e.activation(os3, ps3[:, :, 5:8], AF.Copy), reads=[ps], writes=[os_])
                        P.dma("sync", o_sconv[:, c, :, :], os3, reads=[os_], sembuf=os_, is_output=True)

                        def finq(t0=t0, c=c, lo=lo, n=n):
                            P.op("scalar", lambda e: e.activation(bigt[:, c, lo:lo + n], t0[:, 0:n], AF.Silu), reads=[t0], writes=SLC(c, lo, n))
                    pend_q.append(finq)
                    if len(pend_q) > 1:
                        pend_q.pop(0)()
                    for _ in range(3):
                        next(gates, None)
        while pend_q:
            pend_q.pop(0)()
        if last_pass:
            P.dma("sync", o_pconv, qtail[:], reads=[qtail], sembuf=qtail, is_output=True)
        P.mark('B.gates')
        for _ in gates:
            pass
        if last_pass:
            P.dma("sync", o_pm, bcar[:, 1:2], reads=[bcar], sembuf=bcar, is_output=True)
        if has_sample:
            nin = S32.get()
            P.dma("sync", nin[:, 0:128], d_nst, writes=[nin])
            pn_ = PA.get()
            P.op("tensor", lambda e: e.transpose(pn_[:, 0:128], nin[:, 0:128], identf[:]), reads=[nin, identf], writes=[pn_])
            P.op("vector", lambda e: e.tensor_copy(nsT[:], pn_[:, 0:128]), reads=[pn_], writes=[nsT])
        P.mark('B.rec')
        Cs16 = bigt[:, 16:18, :].rearrange("p a b -> p (a b)")[:, 0:2056].rearrange("p (b e) -> p b e", b=8)
        Cs16b = SLW(16) + SLW(17)
        Vbd = bigt[:, 18:20, :].rearrange("p a b -> p (a b)")[:, 0:2056].rearrange("p (b e) -> p b e", b=8)
        Vbdb = SLW(18) + SLW(19)
        Bigq = bigt[:, 20:22, :].rearrange("p a b -> p (a b)")[:, 0:1984].rearrange("p (b e) -> p b e", b=8)
        Bigb = SLW(20) + SLW(21)
        if has_sample:
            P.op("vector", lambda e: e.memset(bigt[:, 20:22, :], 0.0), writes=Bigb)
        NT = len(tiles)
        PN = Pool(PA.bufs[4:6])
        PP = Pool(PA.bufs[0:2])
        PS = Pool(PA.bufs[2:4])

        def proj(pr, ti, sv_, so_):
            lo = ti * 128
            gi = lo // 512
            svv = sv_[:, 0:4096].rearrange("p (a b) -> p a b", a=8)
            sov = so_[:, 0:4096].rearrange("p (a b) -> p a b", a=8)
            psv = PP.get()
            for k in range(8):
                P.op("tensor", lambda e: e.matmul(psv[:, 0:512], lhsT=hT[:, k, lo:lo + 128], rhs=svv[:, k, :], start=(k == 0), stop=(k == 7)),
                     reads=[sv_, hg[gi]], writes=[psv], signal=(k == 7))
            va = VA.get()
            va3 = va[:, 0:514].rearrange("p (h e) -> p h e", h=2)
            P.op("scalar", lambda e: e.activation(va3[:, :, 0:256], psv[:, 0:512].rearrange("p (h e) -> p h e", h=2), AF.Copy), reads=[psv], writes=[va])
            P.op("vector", lambda e: e.memset(va3[:, :, 256:257], 1.0), writes=[va])
            pso = PP.get()
            for k in range(8):
                P.op("tensor", lambda e: e.matmul(pso[:, 0:512], lhsT=hT[:, k, lo:lo + 128], rhs=sov[:, k, :], start=(k == 0), stop=(k == 7)),
                     reads=[so_, hg[gi]], writes=[pso], signal=(k == 7))
            og = OG.get()
            def sig():
                P.op("scalar", lambda e: e.activation(og[:, 0:512], pso[:, 0:512], AF.Exp, scale=-1.0), reads=[pso], writes=[og])
                P.op("scalar", lambda e: e.activation(og[:, 0:512], og[:, 0:512], AF.Ln, bias=1.0, scale=1.0), reads=[og], writes=[og])
                P.op("scalar", lambda e: e.activation(og[:, 0:512], og[:, 0:512], AF.Exp, scale=-1.0), reads=[og], writes=[og])
            return va, va3, og, sig

        def stageA(cx):
            ti, h, hh, is_s = cx["ti"], cx["h"], cx["hh"], cx["is_s"]
            lo = ti * 128
            qs, ks = h, 8 + h
            pst = PS.get()
            pmb = pst
            pbk = PB.bufs[cx["idx"] % 2]
            cx["pbk"] = pbk
            P.op("tensor", lambda e: e.matmul(pst[:, 0:128], lhsT=bigt[:, ks, lo:lo + 128], rhs=bigt[:, qs, lo:lo + 128], start=True, stop=True),
                 reads=[*SLC(ks, lo, 128), *SLC(qs, lo, 128)], writes=[pst])
            mc = 1 + lo
            P.op("tensor", lambda e: e.matmul(pmb[:, 128:256], lhsT=sel[:, h, :], rhs=Mrows[:, mc:mc + 128], start=True, stop=False),
                 reads=[sel, Mrows], writes=[pmb], signal=False)
            P.op("tensor", lambda e: e.matmul(pmb[:, 128:256], lhsT=identb[:], rhs=(bdp if is_s else trip)[:], start=False, stop=True),
                 reads=[identb, bdp, trip], writes=[pmb], signal=False)
            if not is_s:
                P.op("tensor", lambda e: e.matmul(pmb[:, 256:384], lhsT=sel[:, h, :], rhs=Mrows[:, mc:mc + 128], start=True, stop=True),
                     reads=[sel, Mrows], writes=[pmb], signal=True)
            else:
                P.op("tensor", lambda e: e.matmul(pmb[:, 256:384], lhsT=sel[:, h, :], rhs=dMs[:, 0:128], start=True, stop=True),
                     reads=[sel, dMsb], writes=[pmb], signal=True)
            ptk = pbk
            P.op("tensor", lambda e: e.transpose(ptk[:, 0:128], bigt[:, ks, lo:lo + 128], identb[:]), reads=[*SLC(ks, lo, 128), identb], writes=[ptk])
            wTt = WT.get()
            P.op("scalar", lambda e: e.activation(wTt[:, 0:128], pmb[:, 128:256], AF.Exp, bias=acols[:, ti, h:h + 1], scale=-1.0),
                 reads=[pmb, acols], writes=[wTt])
            if not is_s:
                P.op("scalar", lambda e: e.activation(wTt[:, 128:256], pmb[:, 256:384], AF.Exp, bias=Mpcol[:, h:h + 1], scale=-1.0),
                     reads=[pmb, Mpcol], writes=[wTt])
                P.op("vector", lambda e: e.tensor_copy(Mpcol[:, h:h + 1], pmb[:, 383:384]), reads=[pmb], writes=[Mpcol])
            else:
                P.op("scalar", lambda e: e.activation(wTt[:, 128:256], pmb[:, 256:384], AF.Exp), reads=[pmb], writes=[wTt])
            scT = SA.get()
            P.op("vector", lambda e: e.tensor_tensor(scT[:, 0:128], pst[:, 0:128], wTt[:, 0:128], ALU.mult),
                 reads=[pst, wTt], writes=[scT])
            kg = SA.get()
            gsc = gcols[:, h:h + 1] if is_s else wTt[:, 127:128]
            P.op("scalar", lambda e: e.activation(kg[:, 0:128], ptk[:, 0:128], AF.Identity, scale=gsc),
                 reads=[ptk, wTt, gcols], writes=[kg])
            cx.update(wTt=wTt, scT=scT, kg=kg)
            if not is_s:
                cb16 = CB.get()
                P.op("scalar", lambda e: e.activation(cb16[:, 0:257], C32[:, h, :], AF.Copy), reads=[C32h[h]], writes=[cb16])
                cx.update(cb16=cb16)
                qp = SA.get()
                P.op("vector", lambda e: e.tensor_tensor(qp[:, 0:128], bigt[:, qs, lo:lo + 128], wTt[:, 128:256], ALU.mult),
                     reads=[*SLC(qs, lo, 128), wTt], writes=[qp])
                cx.update(qp=qp)

        def stageB(cx):
            ti, h, hh, is_s = cx["ti"], cx["h"], cx["hh"], cx["is_s"]
            lo = ti * 128
            qs, ks = h, 8 + h
            va, va3, og = cx["va"], cx["va3"], cx["og"]
            vah = va3[:, hh, :]
            wTt, scT, kg = cx["wTt"], cx["scT"], cx["kg"]
            pbk = cx["pbk"]
            pdcv = pbk[:, 384:898].bitcast(F32)
            if not is_s:
                P.op("tensor", lambda e: e.matmul(pdcv, lhsT=kg[:, 0:128], rhs=vah, start=True, stop=True),
                     reads=[kg, va], writes=[pbk])
                P.op("vector", lambda e: e.scalar_tensor_tensor(C32[:, h, :], C32[:, h, :], wTt[:, 255:256], pdcv, ALU.mult, ALU.add),
                     reads=[pbk, wTt, C32h[h]], writes=[C32h[h]])
            pnum = PN.get()
            if not is_s:
                qp = cx["qp"]
                cb16 = cx["cb16"]
                for kk in range(NFILL):
                    P.op("tensor", lambda e: e.matmul(pnum[:, 0:257], lhsT=zerob[:], rhs=vah, start=(kk == 0), stop=False),
                         reads=[zerob, va], writes=[pnum], signal=False)
                P.op("tensor", lambda e: e.matmul(pnum[:, 0:257], lhsT=qp[:, 0:128], rhs=cb16[:, 0:257], start=(NFILL == 0), stop=False),
                     reads=[qp, cb16], writes=[pnum], signal=False)
                P.op("tensor", lambda e: e.matmul(pnum[:, 0:257], lhsT=scT[:, 0:128], rhs=vah, start=False, stop=True),
                     reads=[scT, va], writes=[pnum], signal=True)
            else:
                for half in range(2):
                    P.op("vector", lambda e: e.tensor_tensor(
                        Bigq[:, :, 120:128], bigt[:, qs, lo + half * 64:lo + half * 64 + 64].rearrange("p (b r) -> p b r", b=8),
                        wTt[:, 128 + half * 64:128 + half * 64 + 64].rearrange("p (b r) -> p b r", b=8), ALU.mult),
                        reads=[*SLC(qs, lo, 128), wTt], writes=Bigb)
                    cs = U16[half]
                    cs3 = cs[:, 0:2056].rearrange("p (b e) -> p b e", b=8)
                    P.dma("sync", cs3[:, :, 0:256], d_Cst[half * 8:(half + 1) * 8, h, :, :].rearrange("b p e -> p b e"), writes=[cs])
                    P.op("vector", lambda e: e.tensor_copy(
                        cs3[:, :, 256:257], nsT[:].rearrange("p (b h) -> p b h", h=8)[:, half * 8:(half + 1) * 8, h:h + 1]),
                        reads=[nsT], writes=[cs])
                    P.op("scalar", lambda e: e.activation(Cs16.rearrange("p b e -> p (b e)"), cs[:, 0:2056], AF.Copy), reads=[cs], writes=Cs16b)
                    for b8 in range(8):
                        bb = half * 8 + b8
                        P.op("tensor", lambda e: e.matmul(
                            pnum[:, 0:257], lhsT=win_ap(Bigq, b8, bb), rhs=Cs16[:, b8, :],
                            start=(bb == 0), stop=False), reads=Bigb + Cs16b, writes=[pnum], signal=(b8 == 7))
                P.op("tensor", lambda e: e.matmul(pnum[:, 0:257], lhsT=scT[:, 0:128], rhs=vah, start=False, stop=True),
                     reads=[scT, va], writes=[pnum], signal=True)
            sc = SC.get()
            st6 = SC.get()
            P.op("scalar", lambda e: e.activation(sc[:, 0:1], pnum[:, 256:257], AF.Square), reads=[pnum], writes=[sc])
            P.op("vector", lambda e: e.bn_stats(st6[:, 0:6], pnum[:, 0:256]), reads=[pnum], writes=[st6])
            P.op("vector", lambda e: e.bn_aggr(st6[:, 6:8], st6[:, 0:6]), reads=[st6], writes=[st6])
            P.op("vector", lambda e: e.tensor_tensor(sc[:, 1:2], sc[:, 0:1], acols[:, ti, 8 + h:9 + h], ALU.max), reads=[sc, acols], writes=[sc])
            P.op("vector", lambda e: e.scalar_tensor_tensor(sc[:, 1:2], sc[:, 1:2], EPS, st6[:, 7:8], ALU.mult, ALU.add), reads=[sc, st6], writes=[sc])
            if is_s:
                for half in range(2):
                    cs = U16[half]
                    cs3 = cs[:, 0:2056].rearrange("p (b e) -> p b e", b=8)
                    P.op("vector", lambda e: e.tensor_tensor(Vbd, vah.unsqueeze(1).to_broadcast([128, 8, 257]),
                                                             bmask[:, half * 8:(half + 1) * 8].unsqueeze(2).to_broadcast([128, 8, 257]), ALU.mult),
                         reads=[va, bmask], writes=Vbdb)
                    for b8 in range(8):
                        bb = half * 8 + b8
                        pdx = PS.get()
                        P.op("tensor", lambda e: e.matmul(pdx[:, 0:257], lhsT=kg[:, 0:128], rhs=Vbd[:, b8, :], start=True, stop=True),
                             reads=[kg] + Vbdb, writes=[pdx])
                        P.op("vector", lambda e: e.scalar_tensor_tensor(cs3[:, b8, :], cs3[:, b8, :], decb[:, h, bb:bb + 1], pdx[:, 0:257], ALU.mult, ALU.add),
                             reads=[pdx, decb, cs], writes=[cs])
                    P.dma("sync", o_sC[half * 8:(half + 1) * 8, h, :, :].rearrange("b p e -> p b e"), cs3[:, :, 0:256], reads=[cs], sembuf=cs, is_output=True)
                    P.op("vector", lambda e: e.tensor_copy(
                        nsT[:].rearrange("p (b h) -> p b h", h=8)[:, half * 8:(half + 1) * 8, h:h + 1], cs3[:, :, 256:257]),
                        reads=[cs], writes=[nsT])
            cx.update(sc=sc, st6=st6, pnum=pnum)

        def stageB2(cx):
            sc = cx["sc"]
            rsqrt_act(sc[:, 2:3], sc[:, 1:2], 1.0, 0.0, [sc], [sc])

        def stageB3(cx):
            ti, h, hh, is_s = cx["ti"], cx["h"], cx["hh"], cx["is_s"]
            va, va3, og = cx["va"], cx["va3"], cx["og"]
            vah = va3[:, hh, :]
            wTt, scT, kg = cx["wTt"], cx["scT"], cx["kg"]
            sc, st6, pnum = cx["sc"], cx["st6"], cx["pnum"]
            hn = S32.get()
            P.op("vector", lambda e: e.scalar_tensor_tensor(hn[:, 0:256], pnum[:, 0:256], st6[:, 6:7], og[:, hh * 256:(hh + 1) * 256], ALU.subtract, ALU.mult),
                 reads=[pnum, st6, og], writes=[hn])
            gt = GT.get()
            P.op("scalar", lambda e: e.activation(gt[:, 0:256], hn[:, 0:256], AF.Identity, scale=sc[:, 2:3]),
                 reads=[hn, sc], writes=[gt])
            cx.update(gt=gt)

        def stageC(cx):
            ti, h = cx["ti"], cx["h"]
            lo = ti * 128
            qs, ks = h, 8 + h
            gt = cx["gt"]
            ptg = cx["pbk"]
            for e2 in range(2):
                P.op("tensor", lambda e: e.transpose(ptg[:, 128 + e2 * 128:128 + (e2 + 1) * 128], gt[:, e2 * 128:(e2 + 1) * 128], identb[:]),
                     reads=[gt, identb], writes=[ptg], signal=(e2 == 1))
            for e2 in range(2):
                sl = qs if e2 == 0 else ks
                P.op("vector", lambda e: e.tensor_scalar(bigt[:, sl, lo:lo + 128], ptg[:, 128 + e2 * 128:128 + (e2 + 1) * 128],
                                                         gncol[:, 2 * h + e2:2 * h + e2 + 1], None, ALU.mult),
                     reads=[ptg, gncol], writes=SLC(sl, lo, 128))

        for pr in range(4):
            sv_ = ring_load([(0, 4096, d_wB_in[4 + pr].rearrange("p a b -> p (a b)"))])
            so_ = ring_load([(0, 4096, d_wB_in[8 + pr].rearrange("p a b -> p (a b)"))])
            items = []
            for ti in range(NT):
                is_s = has_sample and ti == NT - 1
                for hh in range(2):
                    items.append(dict(ti=ti, h=pr * 2 + hh, hh=hh, is_s=is_s))
            NI = len(items)
            pj = {}
            for i in range(NI + 3):
                if i < NI:
                    cx = items[i]
                    cx["idx"] = i
                    if cx["hh"] == 0:
                        pj[cx["ti"]] = proj(pr, cx["ti"], sv_, so_)
                    cx["va"], cx["va3"], cx["og"], sigf = pj[cx["ti"]]
                    stageA(cx)
                    if cx["hh"] == 1:
                        sigf()
                if 0 <= i - 1 < NI:
                    stageB(items[i - 1])
                if 0 <= i - 2 < NI:
                    stageB2(items[i - 2])
                    stageB3(items[i - 2])
                if 0 <= i - 3 < NI:
                    stageC(items[i - 3])
        if last_pass:
            P.dma("sync", o_pC.rearrange("h p e -> p h e"), C32[:, :, 0:256], reads=C32h, sembuf=C32h[0], is_output=True)
            P.dma("sync", o_pn, C32[:, :, 256], reads=C32h, sembuf=C32h[0], is_output=True)
        if has_sample:
            pn2 = PA.get()
            P.op("tensor", lambda e: e.transpose(pn2[:, 0:128], nsT[:], identf[:]), reads=[nsT, identf], writes=[pn2])
            nout = S32.get()
            P.op("vector", lambda e: e.tensor_copy(nout[:, 0:128], pn2[:, 0:128]), reads=[pn2], writes=[nout])
            P.dma("sync", o_sn, nout[:, 0:128], reads=[nout], sembuf=nout, is_output=True)
        P.mark('B.out')
        kslots = []
        for e_ in range(16):
            kslots.append(e_ // 2 if e_ % 2 == 0 else 8 + e_ // 2)
        out_proj(d_wB_out, groups, kslots)

    lnsc = P.sbuf("lnsc", [128, 1], F32)
    LNSCALE_AP = lnsc
    P.op("vector", lambda e: e.memset(lnsc[:], LNSCALE), writes=[lnsc])
    dMs_t = P.sbuf("dMs", [8, 128], F32)
    dMs = dMs_t
    dMsb = dMs_t
    cur_tiles = [None]

    def til_idx(ti):
        return ti

    def win_ap(Bq, b8, bb):
        off = 120 - 8 * bb
        return Bq[:, b8, off:off + 128]

    YTP = Pool(S32.bufs + OG.bufs)
    passes = [dict(ptiles=list(range(0, 8)), sample=False), dict(ptiles=list(range(8, 16)), sample=True)]
    for pi, ps_ in enumerate(passes):
        ntiles = len(ps_["ptiles"]) + (1 if ps_["sample"] else 0)
        groups = groups_of(ntiles, ps_["sample"])
        tiles = list(range(ntiles))
        first, last = pi == 0, pi == len(passes) - 1
        p0 = ps_["ptiles"][0] * 128
        npc = len(ps_["ptiles"]) * 128
        for g in groups:
            lo, n = g["lo"], g["n"]
            if g["sample"]:
                P.dma("sync", xT[:, :, lo:lo + n], d_xTs, writes=[xg[g["gi"]]])
            else:
                P.dma("sync", xT[:, :, lo:lo + n], d_xTp[:, :, p0 + lo:p0 + lo + n], writes=[xg[g["gi"]]])
        step = 0
        if step < stop_after:
            mixer_A(groups, tiles, ps_["sample"])
        step += 1
        if step < stop_after:
            ffn(0, groups, last)
        step += 1
        if step < stop_after:
            mixer_B(groups, tiles, ps_["sample"], first, last)
        step += 1
        if step < stop_after:
            ffn(1, groups, last)
        P.mark('final')
        def dst(g, c, xin, gc, rs, rstd):
            lo, n = g["lo"], g["n"]
            yt = YTP.get()
            P.op("vector", lambda e, yt=yt: e.scalar_tensor_tensor(yt[:, 0:n], xin, gc, rs, ALU.mult, ALU.mult),
                 reads=[xg[g["gi"]], rstd, gcol], writes=[yt])
            if g["sample"]:
                P.dma("sync", o_yTs[:, c, :], yt[:, 0:n], reads=[yt], sembuf=yt, is_output=True)
            else:
                P.dma("sync", o_yTp[:, c, p0 + lo:p0 + lo + n], yt[:, 0:n], reads=[yt], sembuf=yt, is_output=True)
        rmsnorm_to(4, groups, dst)

    P.mark('end')
    P.finish("sync")
    if os.environ.get('MK_MARKS'):
        import json
        json.dump(P.marks, open(os.environ['MK_MARKS'], 'w'))
    P.emit()
    P.close()
    print("instr counts", {e: len(P.rec[e]) for e in ENGS}, "sems", P.nsem)
    return nc


def _pk(w, ncols):
    K, N = w.shape
    kc = K // 128
    return np.ascontiguousarray(w.reshape(kc, 128, N // ncols, ncols).transpose(2, 1, 0, 3))


def _consts():
    bf = ml_dtypes.bfloat16
    s = np.arange(128)[:, None]
    t = np.arange(128)[None, :]
    c = {}
    c["identb"] = np.eye(128, dtype=np.float32).astype(bf)
    c["identf"] = np.eye(128, dtype=np.float32)
    c["onesb"] = np.ones((128, 128), np.float32).astype(bf)
    c["trip"] = np.where(s <= t, 0.0, BIG).astype(np.float32).astype(bf)
    c["bdp"] = np.where((s <= t) & (s // 8 == t // 8), 0.0, BIG).astype(np.float32).astype(bf)
    c["tri01"] = (s >= t).astype(np.float32)
    c["bmask"] = (np.arange(128)[:, None] // 8 == np.arange(16)[None, :]).astype(np.float32).astype(bf)
    sel = np.zeros((8, 8, 128), np.float32)
    for h in range(8):
        sel[h, h, :] = 1.0
    c["sel"] = sel
    rst = np.zeros((8, 2, 128), np.float32)
    start = (np.arange(128) % 8 == 0)
    rst[:, 0, :] = np.where(start, 0.0, 1.0)
    rst[:, 1, :] = np.where(start, -1e30, 0.0)
    c["rst"] = rst
    return c


_NC_CACHE = {}


def kernel(**inputs):
    f32 = np.float32
    inp = {k: np.asarray(v) for k, v in inputs.items()}
    stop_after = int(os.environ.get("MK_STOP", "99"))
    if stop_after not in _NC_CACHE:
        _NC_CACHE[stop_after] = build_program(stop_after)
    nc = _NC_CACHE[stop_after]

    shared = {}
    a_w_in = inp["a_w_in"][0]
    shared["wA_in"] = _pk(a_w_in, 512)
    shared["wA_out"] = _pk(inp["a_w_out"][0], 256)
    wfu = []
    for l in range(2):
        w = inp["f_w_up"][l]
        a = w[:, :D_FF].reshape(8, 128, 11, 256)
        g = w[:, D_FF:].reshape(8, 128, 11, 256)
        wfu.append(np.concatenate([a, g], axis=3).transpose(2, 1, 0, 3))
    shared["wF_up"] = np.ascontiguousarray(np.stack(wfu))
    shared["wF_dn"] = np.ascontiguousarray(np.stack([_pk(inp["f_w_down"][l], 128) for l in range(2)]))
    b_w_in = inp["b_w_in"][0]
    shared["wB_in"] = _pk(b_w_in[:, :6144], 512)
    shared["wB_g"] = np.ascontiguousarray(b_w_in[:, 6144:6160].reshape(8, 128, 16).transpose(1, 0, 2))
    shared["wB_out"] = _pk(inp["b_w_out"][0], 256)
    gall = np.stack([inp["norm_mix_g"][0], inp["norm_mix_g"][1], inp["norm_ffn_g"][0], inp["norm_ffn_g"][1], inp["final_norm_g"]])
    shared["gcol"] = np.ascontiguousarray(gall.reshape(5, 8, 128).transpose(2, 0, 1))
    shared["fcw"] = np.ascontiguousarray(inp["f_conv_w"].reshape(2, 3, NJ, 128).transpose(3, 0, 1, 2))
    shared["fcb"] = np.ascontiguousarray(inp["f_conv_b"].reshape(2, NJ, 128).transpose(2, 0, 1))
    shared["bcw"] = np.ascontiguousarray(inp["b_conv_w"][0].reshape(4, 16, 128).transpose(2, 0, 1))
    shared["bcb"] = np.ascontiguousarray(inp["b_conv_b"][0].reshape(16, 128).transpose(1, 0))
    shared["bif"] = np.ascontiguousarray(np.stack([inp["b_bias_i"][0], inp["b_bias_f"][0]], axis=1))
    shared["gncol"] = np.ascontiguousarray(inp["b_gn_g"][0].reshape(16, 128).transpose(1, 0))
    shared["lng"] = np.ascontiguousarray(inp["a_ln_g"][0:1])
    shared["lnb"] = np.ascontiguousarray(inp["a_ln_b"][0:1])
    ws = inp["a_w_s"][0]
    shared["ws"] = np.ascontiguousarray(ws.transpose(1, 0, 2))
    wsbd = np.zeros((8, 128, 128), f32)
    for b in range(16):
        wsbd[:, 8 * b:8 * b + 8, 8 * b:8 * b + 8] = ws[:, :8, :8]
    shared["wsbd"] = np.ascontiguousarray(wsbd.transpose(1, 0, 2))
    bs = inp["a_b_s"][0]
    shared["bs8"] = np.ascontiguousarray(bs)
    shared["bs8s"] = np.ascontiguousarray(np.tile(bs[:, :8], (1, 16)))
    shared.update(_consts())
    shared = {k: (v if v.dtype != np.float64 else v.astype(f32)) for k, v in shared.items()}

    in_maps = []
    for c in range(8):
        m = dict(shared)
        m["xTp"] = np.ascontiguousarray(inp["x_prompt"][c].T.reshape(8, 128, 2048).transpose(1, 0, 2))
        xs = inp["x_sample"][16 * c:16 * c + 16].reshape(128, 1024)
        m["xTs"] = np.ascontiguousarray(xs.T.reshape(8, 128, 128).transpose(1, 0, 2))
        m["Cst"] = np.ascontiguousarray(inp["state_mlstm_C"][0, 16 * c:16 * c + 16])
        m["nst"] = np.ascontiguousarray(inp["state_mlstm_n"][0, 16 * c:16 * c + 16].reshape(128, 128))
        m["m0T"] = np.ascontiguousarray(inp["state_mlstm_m"][0, 16 * c:16 * c + 16].T)
        cv = inp["state_mlstm_conv"][0, 16 * c:16 * c + 16]
        m["cvs"] = np.ascontiguousarray(cv.reshape(16, 3, 16, 128).transpose(3, 2, 0, 1))
        ff = inp["state_ffn_conv"][:, 16 * c:16 * c + 16]
        m["ffs"] = np.ascontiguousarray(ff.reshape(2, 16, 2, NJ, 128).transpose(4, 0, 3, 1, 2))
        in_maps.append(m)

    res = run_bass_kernel_spmd(nc, in_maps, core_ids=list(range(8)))
    R = res.results

    y_prompt = np.stack([R[c]["yTp"].transpose(1, 0, 2).reshape(1024, 2048).T for c in range(8)]).astype(f32)
    y_sample = np.concatenate([R[c]["yTs"].transpose(1, 0, 2).reshape(1024, 128).T.reshape(16, 8, 1024) for c in range(8)]).astype(f32)
    pC = np.stack([R[c]["pC"] for c in range(8)])[None].astype(f32)
    pn = np.stack([R[c]["pn"].T for c in range(8)])[None].astype(f32)
    pm = np.stack([R[c]["pm"][:, 0] for c in range(8)])[None].astype(f32)
    pconv = np.stack([R[c]["pconv"].transpose(2, 1, 0).reshape(3, 2048) for c in range(8)])[None].astype(f32)
    pffn = np.stack([R[c]["pffn"].transpose(1, 3, 2, 0).reshape(2, 2, D_FF) for c in range(8)], axis=1).astype(f32)
    sv = np.concatenate([R[c]["sv"].reshape(16, 8, 2048) for c in range(8)])[None].astype(f32)
    sC = np.concatenate([R[c]["sC"] for c in range(8)])[None].astype(f32)
    sn = np.concatenate([R[c]["sn"].reshape(16, 8, 128) for c in range(8)])[None].astype(f32)
    sm = np.concatenate([R[c]["sm"].T for c in range(8)])[None].astype(f32)
    sconv = np.concatenate([R[c]["sconv"].transpose(2, 3, 1, 0).reshape(16, 3, 2048) for c in range(8)])[None].astype(f32)
    sffn = np.concatenate([R[c]["sffn"].transpose(1, 3, 4, 2, 0).reshape(2, 16, 2, D_FF) for c in range(8)], axis=1).astype(f32)
    return (y_prompt, y_sample, pC, pn, pm, pconv, pffn, sv, sC, sn, sm, sconv, sffn)
```
